# Optimizing a Trainium2 kernel written in Bass

```python
import math
import jax, jax.numpy as jnp
from jax import lax
import numpy as np

D_MODEL = 2048
BATCH = 1
SEQ = 8192
DEPTH = 1

N_HEADS = 16
QK_NOPE_DIM = 128
ROPE_DIM = 64
V_HEAD_DIM = 128
Q_LORA_RANK = 512
KV_LORA_RANK = 512
MLA_WIDTH = N_HEADS * V_HEAD_DIM
CONV_WIDTH = D_MODEL
CONV_K = 3
ROPE_THETA = 10000.0
RMS_EPS = 1e-6
BLOCK_Q = 128

IN_SPLIT = [
    Q_LORA_RANK,
    KV_LORA_RANK,
    ROPE_DIM,
    MLA_WIDTH,
    CONV_WIDTH,
    CONV_WIDTH,
    CONV_WIDTH,
    CONV_WIDTH,
    D_MODEL,
    D_MODEL,
]
IN_TOTAL = sum(IN_SPLIT)

kernel_name = "hybrid_mla_shortconv_gated_merge"


def _rmsnorm(x, g):
    xf = x.astype(jnp.float32)
    r = lax.rsqrt(jnp.mean(xf * xf, axis=-1, keepdims=True) + RMS_EPS)
    return (xf * r * g.astype(jnp.float32)).astype(x.dtype)


def _rope_tables(positions):
    inv_freq = ROPE_THETA ** (-jnp.arange(0, ROPE_DIM, 2, dtype=jnp.float32) / ROPE_DIM)
    ang = positions.astype(jnp.float32)[..., None] * inv_freq
    return jnp.cos(ang), jnp.sin(ang)


def _apply_rope(x, cos, sin):
    xf = x.astype(jnp.float32)
    x1, x2 = jnp.split(xf, 2, axis=-1)
    out = jnp.concatenate([x1 * cos - x2 * sin, x1 * sin + x2 * cos], axis=-1)
    return out.astype(x.dtype)


def _causal_mla_attention(q_nope, q_rope, k_nope, k_rope, v):
    b, s, h, _ = q_nope.shape
    nb = s // BLOCK_Q
    scale = 1.0 / math.sqrt(QK_NOPE_DIM + ROPE_DIM)
    qn_blocks = q_nope.reshape(b, nb, BLOCK_Q, h, QK_NOPE_DIM).transpose(1, 0, 2, 3, 4)
    qr_blocks = q_rope.reshape(b, nb, BLOCK_Q, h, ROPE_DIM).transpose(1, 0, 2, 3, 4)
    key_pos = jnp.arange(s, dtype=jnp.int32)

    def one_block(args):
        qn, qr, blk = args
        sc = (jnp.einsum('bqhd,bkhd->bhqk', qn, k_nope)
              + jnp.einsum('bqhr,bkr->bhqk', qr, k_rope)).astype(jnp.float32) * scale
        q_pos = blk * BLOCK_Q + jnp.arange(BLOCK_Q, dtype=jnp.int32)
        mask = key_pos[None, :] <= q_pos[:, None]
        sc = jnp.where(mask[None, None], sc, -jnp.inf)
        p = jax.nn.softmax(sc, axis=-1).astype(v.dtype)
        return jnp.einsum('bhqk,bkhd->bqhd', p, v)

    out = lax.map(one_block, (qn_blocks, qr_blocks, jnp.arange(nb, dtype=jnp.int32)))
    return out.transpose(1, 0, 2, 3, 4).reshape(b, s, h, V_HEAD_DIM)


def setup_inputs(seed: int = 0) -> dict:
    key = jax.random.key(seed)
    ks = jax.random.split(key, 16)
    f32 = jnp.float32

    def w(k, shape, fan_in):
        return jax.random.normal(k, shape, f32) * (fan_in ** -0.5)

    def gain(k, n):
        return 1.0 + 0.02 * jax.random.normal(k, (n,), f32)

    x = jax.random.normal(ks[0], (BATCH, SEQ, D_MODEL), f32)
    start = jax.random.randint(ks[1], (BATCH, 1), 0, 1024, dtype=jnp.int32)
    positions = start + jnp.arange(SEQ, dtype=jnp.int32)[None, :]
    return {
        "x": x,
        "positions": positions,
        "pre_norm_g": gain(ks[2], D_MODEL),
        "w_in": w(ks[3], (D_MODEL, IN_TOTAL), D_MODEL),
        "q_a_norm_g": gain(ks[4], Q_LORA_RANK),
        "w_q_b": w(ks[5], (Q_LORA_RANK, N_HEADS * (QK_NOPE_DIM + ROPE_DIM)), Q_LORA_RANK),
        "kv_a_norm_g": gain(ks[6], KV_LORA_RANK),
        "w_kv_b": w(ks[7], (KV_LORA_RANK, N_HEADS * (QK_NOPE_DIM + V_HEAD_DIM)), KV_LORA_RANK),
        "conv_w": w(ks[8], (CONV_K, CONV_WIDTH), CONV_K),
        "w_o_mla": w(ks[9], (MLA_WIDTH, D_MODEL), MLA_WIDTH),
        "w_o_conv": w(ks[10], (CONV_WIDTH, D_MODEL), CONV_WIDTH),
        "w_out": w(ks[11], (D_MODEL, D_MODEL), D_MODEL),
        "post_norm_g": gain(ks[12], D_MODEL),
    }


def reference(x, positions, pre_norm_g, w_in, q_a_norm_g, w_q_b, kv_a_norm_g, w_kv_b,
              conv_w, w_o_mla, w_o_conv, w_out, post_norm_g):
    b, s, _ = x.shape
    cos, sin = _rope_tables(positions)
    for _layer in range(DEPTH):
        h = _rmsnorm(x, pre_norm_g)
        proj = jnp.einsum('bsd,de->bse', h, w_in)
        cuts = [int(c) for c in np.cumsum(IN_SPLIT)[:-1]]
        (q_a, c_kv, k_rope, z_mla, c_in, b_gate, c_gate, z_conv,
         g_mla, g_conv) = jnp.split(proj, cuts, axis=-1)

        q = jnp.einsum('bsr,re->bse', _rmsnorm(q_a, q_a_norm_g), w_q_b)
        q = q.reshape(b, s, N_HEADS, QK_NOPE_DIM + ROPE_DIM)
        q_nope, q_rope = q[..., :QK_NOPE_DIM], q[..., QK_NOPE_DIM:]
        kv = jnp.einsum('bsr,re->bse', _rmsnorm(c_kv, kv_a_norm_g), w_kv_b)
        kv = kv.reshape(b, s, N_HEADS, QK_NOPE_DIM + V_HEAD_DIM)
        k_nope, v = kv[..., :QK_NOPE_DIM], kv[..., QK_NOPE_DIM:]
        q_rope = _apply_rope(q_rope, cos[:, :, None, :], sin[:, :, None, :])
        k_rope = _apply_rope(k_rope, cos, sin)
        attn = _causal_mla_attention(q_nope, q_rope, k_nope, k_rope, v)
        attn = attn.reshape(b, s, MLA_WIDTH) * jax.nn.silu(z_mla)
        y_mla = jnp.einsum('bse,ed->bsd', attn, w_o_mla)

        u = c_gate * c_in
        u_pad = jnp.pad(u, ((0, 0), (CONV_K - 1, 0), (0, 0)))
        conv = sum(conv_w[k] * u_pad[:, k:k + s] for k in range(CONV_K))
        y_conv = jnp.einsum('bse,ed->bsd', b_gate * conv * jax.nn.silu(z_conv), w_o_conv)

        merged = jax.nn.sigmoid(g_mla) * y_mla + jax.nn.sigmoid(g_conv) * y_conv
        out = jnp.einsum('bsd,de->bse', merged, w_out)
        x = x + _rmsnorm(out, post_norm_g)
    return x
```

```python
import contextlib
import math

import numpy as np
import concourse.bass as bass
import concourse.mybir as mybir
from concourse.bass_utils import run_bass_kernel_spmd

F32 = mybir.dt.float32
BF16 = mybir.dt.bfloat16
I32 = mybir.dt.int32
AF = mybir.ActivationFunctionType
ALU = mybir.AluOpType

D = 2048
S = 8192
NH = 16
EPS = 1e-6
NCORES = 8
SCALE = 1.0 / math.sqrt(192.0)
TWO_PI = 2.0 * math.pi
import os
FLAGS = os.environ.get("KFLAGS", "").split(",")


class _Op:
    __slots__ = ("eng", "fn", "deps", "kind", "chan", "sig", "needed")


class Prog:
    ENGS = ("pe", "act", "dve", "pool", "sp")

    def __init__(self):
        self.ops = []
        self.res = {}
        self.chan_count = {}
        self.bulk = set()

    def _add(self, eng, fn, reads, writes, kind, chan=None, extra=()):
        op = _Op()
        op.eng, op.fn, op.kind, op.chan = eng, fn, kind, chan
        op.needed, op.sig = False, None
        deps = {}
        for r in reads:
            st = self.res.setdefault(r, [None, []])
            if st[0] is not None:
                deps.setdefault(st[0], "raw")
            if r.startswith("ps"):
                for rd in st[1]:
                    if rd.eng != eng:
                        deps.setdefault(rd, "raw")
        for w in writes:
            st = self.res.setdefault(w, [None, []])
            if st[0] is not None and st[0] not in deps:
                deps[st[0]] = "waw"
            for rd in st[1]:
                deps.setdefault(rd, "war")
        for r in reads:
            self.res[r][1].append(op)
        for w in writes:
            self.res[w] = [op, []]
        final = []
        for d, k in deps.items():
            if d is op:
                continue
            if d.eng == eng and d.kind == "c" and kind == "c":
                if eng == "pe" or k == "war":
                    continue
            if kind == "d" and d.kind == "d" and d.chan == chan and chan in self.bulk:
                continue
            final.append(d)
        for d in extra:
            if d not in final:
                final.append(d)
        op.deps = final
        for d in final:
            d.needed = True
        if kind == "d":
            n = self.chan_count.get(chan, 0) + 1
            self.chan_count[chan] = n
            op.sig = (("chan", chan), 16 * n)
        self.ops.append(op)
        return op

    def op(self, eng, fn, reads=(), writes=(), extra=()):
        return self._add(eng, fn, list(reads), list(writes), "c", extra=extra)

    def dma(self, eng, fn, reads=(), writes=(), chan=None, bulk=False, extra=()):
        assert chan is not None
        if bulk:
            self.bulk.add(chan)
        else:
            assert chan not in self.bulk
        return self._add(eng, fn, list(reads), list(writes), "d", chan, extra=extra)

    def barrier(self):
        last = {}
        lastd = {}
        for o in self.ops:
            if o.fn is None:
                continue
            if o.kind == "d":
                lastd[o.chan] = o
            else:
                last[o.eng] = o
        for e in self.ENGS:
            deps = list(last.values()) + list(lastd.values())
            self.join(e, deps)

    def join(self, eng, ops):
        op = _Op()
        op.eng, op.fn, op.kind, op.chan, op.needed, op.sig = eng, None, "c", None, False, None
        op.deps = list(ops)
        for d in ops:
            d.needed = True
        self.ops.append(op)
        return op

    def emit(self, nc):
        cnt = {e: 0 for e in self.ENGS}
        for op in self.ops:
            if op.kind == "c" and op.needed:
                cnt[op.eng] += 1
                op.sig = (("eng", op.eng), cnt[op.eng])
            if op.kind == "d" and op.chan in self.bulk:
                op.sig = (("chan", op.chan), 16 * self.chan_count[op.chan])
        with contextlib.ExitStack() as st:
            sems = {}
            for e in self.ENGS:
                sems[("eng", e)] = st.enter_context(nc.semaphore("s_" + e))
            for c in self.chan_count:
                sems[("chan", c)] = st.enter_context(nc.semaphore("c_" + str(c)))
            block = st.enter_context(nc.Block())

            def body(ename):
                def f(eng):
                    waited = {}
                    for op in self.ops:
                        if op.eng != ename:
                            continue
                        for d in op.deps:
                            key, val = d.sig
                            if waited.get(key, 0) < val:
                                eng.wait_ge(sems[key], val)
                                waited[key] = val
                        if op.fn is None:
                            continue
                        inst = op.fn(eng)
                        if op.kind == "d":
                            inst.then_inc(sems[op.sig[0]], 16)
                        elif op.needed:
                            inst.then_inc(sems[op.sig[0]], 1)
                return f

            block.tensor(body("pe"))
            block.scalar(body("act"))
            block.vector(body("dve"))
            block.gpsimd(body("pool"))
            block.sync(body("sp"))


class SB:
    def __init__(self, nc, nbytes):
        self.t = nc.alloc_sbuf_tensor("sb", [128, nbytes // 2], BF16)
        self.off = 0
        self.cap = nbytes

    def alloc(self, shape, dtype):
        assert shape[0] == 128
        n = int(np.prod(shape[1:]))
        size = 4 if dtype in (F32, I32) else 2
        nb = (n * size + 63) // 64 * 64
        assert self.off + nb <= self.cap, ("SBUF overflow", self.off, nb, self.cap)
        ap = self.t[:, self.off // 2:(self.off + n * size) // 2]
        self.off += nb
        if dtype != BF16:
            ap = ap.bitcast(dtype)
        if len(shape) == 3:
            ap = ap.rearrange("p (a b) -> p a b", b=shape[2])
        elif len(shape) == 4:
            ap = ap.rearrange("p (a b c) -> p a b c", b=shape[2], c=shape[3])
        return ap

    def mark(self):
        return self.off

    def release(self, m):
        self.off = m


class Ctx:
    pass


def own_blocks(c):
    out = []
    for g in range(4):
        out += [16 * g + c, 16 * g + 15 - c]
    return out


def build(n_kv_groups=16, phases=("K", "Q", "A", "R"), dbg=False, stop=99):
    nc = bass.Bass("TRN2", target_bir_lowering=False)
    P = Prog()
    C = Ctx()
    C.nc, C.P = nc, P
    C.stop = stop
    S_all = n_kv_groups * 512
    C.S_all = S_all

    def din(name, shape, dt=F32):
        return nc.dram_tensor(name, list(shape), dt, kind="ExternalInput").ap()

    scratch_kind = "ExternalOutput" if dbg else "Internal"

    def dscr(name, shape, dt=BF16):
        return nc.dram_tensor(name, list(shape), dt, kind=scratch_kind).ap()

    C.x_all = din("x_all", [S_all, D])
    C.pos_all = din("pos_all", [1, S_all], I32)
    C.g_pre = din("g_pre", [1, D])
    C.w_lat = din("w_lat", [D, 768])
    C.g_kv = din("g_kv", [512])
    C.w_k = din("w_k", [512, 2048])
    C.w_v = din("w_v", [512, 2048])
    C.ident = din("ident", [128, 128], BF16)
    C.ones = din("ones", [128, 128], BF16)
    C.ropec = din("ropec", [128, 2])
    C.kT_d = dscr("kT_d", [NH, 128, S_all])
    C.v_d = dscr("v_d", [NH, 128, S_all // 128, 128])
    C.krT_d = dscr("krT_d", [128, S_all])
    full = any(p in phases for p in ("Q", "A", "R"))
    if full:
        C.x_own = din("x_own", [1024, D])
        C.pos_own = din("pos_own", [1, 1024], I32)
        C.x_halo = din("x_halo", [128, D])
        C.mask16 = din("mask16", [128, 16])
        C.w_in = din("w_in", [D, 15424])
        C.g_qa = din("g_qa", [512])
        C.w_qn = din("w_qn", [512, 2048])
        C.w_qr = din("w_qr", [512, 1024])
        C.w_qrs = din("w_qrs", [512, 1024])
        C.conv_w = din("conv_w", [3, D])
        C.w_o_mla = din("w_o_mla", [D, D])
        C.w_o_conv = din("w_o_conv", [D, D])
        C.w_out = din("w_out", [D, D])
        C.g_post = din("g_post", [1, D])
        okind = "ExternalOutput"
        C.out = nc.dram_tensor("out", [1024, D], F32, kind=okind).ap()
        if dbg:
            C.dbg_attn = nc.dram_tensor("dbg_attn", [128, 16, 1024], BF16, kind=okind).ap()
            C.dbg_qn = nc.dram_tensor("dbg_qn", [128, 16, 1024], BF16, kind=okind).ap()
            C.dbg_qr = nc.dram_tensor("dbg_qr", [128, 8, 1024], BF16, kind=okind).ap()
            C.dbg_gated = nc.dram_tensor("dbg_gated", [128, 16, 1024], BF16, kind=okind).ap()
            C.dbg_gc = nc.dram_tensor("dbg_gc", [128, 16, 1024], BF16, kind=okind).ap()
            C.dbg_mg = nc.dram_tensor("dbg_mg", [128, 16, 1024], BF16, kind=okind).ap()

    C.sb = SB(nc, 206 * 1024)
    C.ps = [nc.alloc_psum_tensor("ps%d" % i, [128, 512], F32) for i in range(8)]
    C.ps_i = 0

    const_setup(C)
    if "K" in phases and C.stop >= 1:
        phase_K(C, n_kv_groups)
    if full:
        sb = C.sb
        C.attn = sb.alloc([128, 16, 1024], BF16)
        m0 = sb.mark()
        C.Qn = sb.alloc([128, 16, 1024], BF16)
        C.Qr = sb.alloc([128, 8, 1024], BF16)
        P.barrier()
        if "Q" in phases:
            phase_Q(C)
        if dbg and "Q" in phases:
            C.final_ops.append(P.dma("sp", lambda e: e.dma_start(out=C.dbg_qn, in_=C.Qn), reads=["Qn%d" % h for h in range(NH)], chan="dbg"))
            C.final_ops.append(P.dma("sp", lambda e: e.dma_start(out=C.dbg_qr, in_=C.Qr), reads=["Qr%d" % h for h in range(8)], chan="dbg"))
        P.barrier()
        if "A" in phases:
            phase_A(C)
        if dbg and "A" in phases:
            C.final_ops.append(P.dma("sp", lambda e: e.dma_start(out=C.dbg_attn, in_=C.attn),
                                     reads=["attn%d" % h for h in range(NH)], chan="dbg"))
        P.barrier()
        sb.release(m0)
        if "R" in phases:
            phase_R(C)

    P.join("sp", C.final_ops)
    P.emit(nc)
    return nc


def ps_next(C):
    i = C.ps_i
    C.ps_i = (i + 1) % 8
    return i


def const_setup(C):
    nc, P, sb = C.nc, C.P, C.sb
    C.final_ops = []
    C.ident_sb = sb.alloc([128, 128], BF16)
    C.ones_sb = sb.alloc([128, 128], BF16)
    C.ropec_sb = sb.alloc([128, 2], F32)
    C.eps_sb = sb.alloc([128, 1], F32)
    C.gb = sb.alloc([128, D], F32)
    P.dma("sp", lambda e: e.dma_start(out=C.ident_sb, in_=C.ident), writes=["ident"], chan="const", bulk=True)
    P.dma("sp", lambda e: e.dma_start(out=C.ones_sb, in_=C.ones), writes=["ones"], chan="const", bulk=True)
    P.dma("sp", lambda e: e.dma_start(out=C.ropec_sb, in_=C.ropec), writes=["ropec"], chan="const", bulk=True)
    P.dma("sp", lambda e: e.dma_start(out=C.gb, in_=C.g_pre.broadcast_to([128, D])), writes=["gb"], chan="const", bulk=True)
    P.op("dve", lambda e: e.memset(C.eps_sb, EPS), writes=["eps"])


def rope_tables(C, pos_src, n, Ct, St, tmp, tag):
    P = C.P
    pi_t, a, kf, r, m = tmp
    P.dma("sp", lambda e: e.dma_start(out=pi_t, in_=pos_src.broadcast_to([128, n])),
          writes=[tag + "pi"], chan=tag + "pi")
    P.op("dve", lambda e: e.tensor_copy(out=a, in_=pi_t), reads=[tag + "pi"], writes=[tag + "a"])
    P.op("dve", lambda e: e.tensor_scalar(out=a, in0=a, scalar1=C.ropec_sb[:, 0:1], scalar2=None, op0=ALU.mult),
         reads=[tag + "a", "ropec"], writes=[tag + "a"])
    P.op("dve", lambda e: e.tensor_scalar(out=kf, in0=a, scalar1=1.0 / TWO_PI, scalar2=None, op0=ALU.mult),
         reads=[tag + "a"], writes=[tag + "kf"])
    ki = pi_t
    P.op("dve", lambda e: e.tensor_copy(out=ki, in_=kf), reads=[tag + "kf"], writes=[tag + "pi"])
    P.op("dve", lambda e: e.tensor_copy(out=kf, in_=ki), reads=[tag + "pi"], writes=[tag + "kf"])
    C1 = 6.28125
    C2 = TWO_PI - C1
    P.op("dve", lambda e: e.scalar_tensor_tensor(out=r, in0=kf, scalar=-C1, in1=a, op0=ALU.mult, op1=ALU.add),
         reads=[tag + "kf", tag + "a"], writes=[tag + "r"])
    P.op("dve", lambda e: e.scalar_tensor_tensor(out=r, in0=kf, scalar=-C2, in1=r, op0=ALU.mult, op1=ALU.add),
         reads=[tag + "kf", tag + "r"], writes=[tag + "r"])

    def wrap(x):
        P.op("dve", lambda e: e.tensor_scalar(out=m, in0=x, scalar1=math.pi, scalar2=-TWO_PI, op0=ALU.is_gt, op1=ALU.mult),
             reads=[tag + "r"], writes=[tag + "m"])
        P.op("dve", lambda e: e.tensor_tensor(out=x, in0=x, in1=m, op=ALU.add),
             reads=[tag + "r", tag + "m"], writes=[tag + "r"])
        P.op("dve", lambda e: e.tensor_scalar(out=m, in0=x, scalar1=-math.pi, scalar2=TWO_PI, op0=ALU.is_lt, op1=ALU.mult),
             reads=[tag + "r"], writes=[tag + "m"])
        P.op("dve", lambda e: e.tensor_tensor(out=x, in0=x, in1=m, op=ALU.add),
             reads=[tag + "r", tag + "m"], writes=[tag + "r"])

    wrap(r)
    P.op("act", lambda e: e.activation(out=St, in_=r, func=AF.Sin, scale=C.ropec_sb[:, 1:2]),
         reads=[tag + "r", "ropec"], writes=[tag + "S"])
    P.op("dve", lambda e: e.tensor_scalar(out=r, in0=r, scalar1=math.pi / 2, scalar2=None, op0=ALU.add),
         reads=[tag + "r"], writes=[tag + "r"])
    wrap(r)
    P.op("act", lambda e: e.activation(out=Ct, in_=r, func=AF.Sin),
         reads=[tag + "r"], writes=[tag + "C"])


def front_end1(C, x_src, slot, bufs, xslot=None):
    P = C.P
    xb, junk, ss, sd, rstd, xs = bufs
    sl = "fe%d" % slot
    xr = "fe%dxb" % (slot if xslot is None else xslot)
    P.dma("sp", lambda e: e.dma_start(out=xb, in_=x_src), writes=[xr], chan=xr)
    P.op("act", lambda e: e.activation(out=xs, in_=xb, func=AF.Square, accum_out=ss),
         reads=[xr], writes=[sl + "xs", sl + "ss"])
    P.op("act", lambda e: e.activation(out=sd, in_=ss, func=AF.Sqrt, scale=1.0 / D, bias=C.eps_sb),
         reads=[sl + "ss", "eps"], writes=[sl + "sd"])
    P.op("dve", lambda e: e.reciprocal(out=rstd, in_=sd), reads=[sl + "sd"], writes=[sl + "rstd"])
    P.op("dve", lambda e: e.scalar_tensor_tensor(out=xs, in0=xb, scalar=rstd, in1=C.gb, op0=ALU.mult, op1=ALU.mult),
         reads=[xr, sl + "rstd", "gb"], writes=[sl + "xs"])


def front_end2(C, slot, hT_dst, hres, xs):
    P = C.P
    sl = "fe%d" % slot
    for q in range(4):
        b = ps_next(C)
        pb = C.ps[b][:, :].bitcast(BF16)
        for kk in range(4):
            kc = 4 * q + kk
            P.op("pe", lambda e, kc=kc, kk=kk, pb=pb: e.transpose(out=pb[:, kk * 128:(kk + 1) * 128],
                                                               in_=xs[:, kc * 128:(kc + 1) * 128], identity=C.ident_sb),
                 reads=[sl + "xs", "ident"], writes=["ps%d" % b])
        src = pb[:, 0:512].rearrange("p (a b) -> p a b", b=128)
        dst = hT_dst[:, 4 * q:4 * q + 4, :]
        if q % 2 == 0:
            P.op("dve", lambda e, src=src, dst=dst: e.tensor_copy(out=dst, in_=src),
                 reads=["ps%d" % b], writes=[hres + "_%d" % q])
        else:
            P.op("act", lambda e, src=src, dst=dst: e.copy(out=dst, in_=src),
                 reads=["ps%d" % b], writes=[hres + "_%d" % q])


def front_end(C, x_src, slot, hT_dst, hres, bufs):
    front_end1(C, x_src, slot, bufs)
    front_end2(C, slot, hT_dst, hres, bufs[5])


def phase_K(C, n_groups):
    nc, P, sb = C.nc, C.P, C.sb
    mk = sb.mark()
    S_all = C.S_all
    wlat = sb.alloc([128, 16, 768], BF16)
    wk = sb.alloc([128, 4, 2048], BF16)
    wv = sb.alloc([128, 4, 2048], BF16)
    gkv = sb.alloc([128, 4], F32)
    for h2 in range(2):
        P.dma("pool", lambda e, h2=h2: e.dma_start(out=wlat[:, 8 * h2:8 * h2 + 8, :],
                                                 in_=C.w_lat[1024 * h2:1024 * (h2 + 1), :].rearrange("(kc p) n -> p kc n", p=128)),
              writes=["wlat%d" % h2], chan="wlat", bulk=True)
    P.dma("pool", lambda e: e.dma_start(out=wk, in_=C.w_k.rearrange("(kc p) n -> p kc n", p=128)), writes=["wk"], chan="wkv", bulk=True)
    P.dma("pool", lambda e: e.dma_start(out=wv, in_=C.w_v.rearrange("(kc p) n -> p kc n", p=128)), writes=["wv"], chan="wkv", bulk=True)
    P.dma("sp", lambda e: e.dma_start(out=gkv, in_=C.g_kv.rearrange("(c p) -> p c", p=128), allow_slow_non_contiguous=True), writes=["gkv"], chan="const", bulk=True)

    xb = [sb.alloc([128, D], F32) for _ in range(2)]
    ssb = [sb.alloc([128, 1], F32) for _ in range(4)]
    sdb = [sb.alloc([128, 1], F32) for _ in range(4)]
    rsb = [sb.alloc([128, 1], F32) for _ in range(4)]
    xs = [sb.alloc([128, D], BF16) for _ in range(4)]
    hT = [sb.alloc([128, 16, 512], BF16) for _ in range(2)]
    sq = sb.alloc([128, 4, 512], BF16)
    junk = None
    craw = sb.alloc([128, 4, 512], F32)
    ckvn = [sb.alloc([128, 4, 512], BF16) for _ in range(2)]
    sdk = sb.alloc([128, 512], F32)
    rk = sb.alloc([128, 512], F32)
    Ct = sb.alloc([128, 512], F32)
    St = sb.alloc([128, 512], F32)
    rtmp = [sb.alloc([128, 512], I32)] + [sb.alloc([128, 512], F32) for _ in range(4)]
    t1 = sb.alloc([128, 512], F32)
    t2 = sb.alloc([128, 512], F32)
    krt = [sb.alloc([128, 512], BF16) for _ in range(2)]
    kst = [sb.alloc([128, 8, 512], BF16) for _ in range(2)]
    vst = sb.alloc([128, 16, 4, 128], BF16)

    def fe1_blk(g, tbk):
        t0 = g * 512 + tbk * 128
        front_end1(C, C.x_all[t0:t0 + 128, :], 20 + tbk, (xb[tbk % 2], junk, ssb[tbk], sdb[tbk], rsb[tbk], xs[tbk]), xslot=30 + tbk % 2)

    def fe1(g):
        for tbk in range(4):
            fe1_blk(g, tbk)

    def fe2(g):
        hs = g % 2
        for tbk in range(4):
            front_end2(C, 20 + tbk, hT[hs][:, :, tbk * 128:(tbk + 1) * 128], "hT%d_%d" % (hs, tbk), xs[tbk])

    def latents(g):
        hs = g % 2
        t0 = g * 512
        rope_tables(C, C.pos_all[:, t0:t0 + 512], 512, Ct, St, rtmp, "rk")
        hres = lambda kc: ["hT%d_%d_%d" % (hs, tb, kc // 4) for tb in range(4)]
        for cb in range(6):
            b = ps_next(C)
            for kc in range(16):
                P.op("pe", lambda e, b=b, cb=cb, kc=kc: e.matmul(out=C.ps[b][:, :], lhsT=wlat[:, kc, cb * 128:(cb + 1) * 128],
                                                               rhs=hT[hs][:, kc, :], start=(kc == 0), stop=(kc == 15)),
                     reads=["wlat%d" % (kc // 8)] + hres(kc), writes=["ps%d" % b])
            if cb < 4:
                P.op("dve", lambda e, b=b, cb=cb: e.tensor_copy(out=craw[:, cb, :], in_=C.ps[b][:, :]),
                     reads=["ps%d" % b], writes=["craw%d" % cb])
                P.op("act", lambda e, cb=cb: e.activation(out=sq[:, cb, :], in_=craw[:, cb, :], func=AF.Square),
                     reads=["craw%d" % cb], writes=["sq%d" % cb])
            elif cb == 4:
                P.op("dve", lambda e, b=b: e.tensor_tensor(out=t1, in0=C.ps[b][:, :], in1=Ct, op=ALU.mult),
                     reads=["ps%d" % b, "rkC"], writes=["t1"])
            else:
                P.op("dve", lambda e, b=b: e.tensor_tensor(out=t2, in0=C.ps[b][:, :], in1=St, op=ALU.mult),
                     reads=["ps%d" % b, "rkS"], writes=["t2"])
                ks = g % 2
                P.op("pool", lambda e, ks=ks: e.tensor_tensor(out=krt[ks], in0=t1, in1=t2, op=ALU.add),
                     reads=["t1", "t2"], writes=["krt%d" % ks])
                C.final_ops.append(
                    P.dma("pool", lambda e, ks=ks, t0=t0: e.dma_start(out=C.krT_d[:, t0:t0 + 512], in_=krt[ks]),
                          reads=["krt%d" % ks], writes=["krT_d"], chan="krst%d" % ks))

    def ckv_norm(g):
        cs = g % 2
        b = ps_next(C)
        for cb in range(4):
            P.op("pe", lambda e, b=b, cb=cb: e.matmul(out=C.ps[b][:, :], lhsT=C.ones_sb, rhs=sq[:, cb, :],
                                                    start=(cb == 0), stop=(cb == 3)),
                 reads=["ones", "sq%d" % cb], writes=["ps%d" % b])
        P.op("act", lambda e, b=b: e.activation(out=sdk, in_=C.ps[b][:, :], func=AF.Sqrt, scale=1.0 / 512, bias=C.eps_sb),
             reads=["ps%d" % b, "eps"], writes=["sdk"])
        P.op("dve", lambda e: e.reciprocal(out=rk, in_=sdk), reads=["sdk"], writes=["rk"])
        for cb in range(4):
            P.op("dve", lambda e, cb=cb, cs=cs: e.scalar_tensor_tensor(out=ckvn[cs][:, cb, :], in0=craw[:, cb, :],
                                                                     scalar=gkv[:, cb:cb + 1], in1=rk,
                                                                     op0=ALU.mult, op1=ALU.mult),
                 reads=["craw%d" % cb, "gkv", "rk"], writes=["ckvn%d_%d" % (cs, cb)])

    def k_proj(g):
        cs = g % 2
        t0 = g * 512
        ckres = ["ckvn%d_%d" % (cs, cb) for cb in range(4)]
        for h in range(NH):
            b = ps_next(C)
            for c4 in range(4):
                P.op("pe", lambda e, b=b, h=h, c4=c4: e.matmul(out=C.ps[b][:, :], lhsT=wk[:, c4, h * 128:(h + 1) * 128],
                                                             rhs=ckvn[cs][:, c4, :], start=(c4 == 0), stop=(c4 == 3)),
                     reads=["wk", ckres[c4]], writes=["ps%d" % b])
            half = h // 8
            if h % 2 == 0:
                P.op("act", lambda e, b=b, h=h, half=half: e.copy(out=kst[half][:, h % 8, :], in_=C.ps[b][:, :]),
                     reads=["ps%d" % b], writes=["kst%d" % half])
            else:
                P.op("dve", lambda e, b=b, h=h, half=half: e.tensor_copy(out=kst[half][:, h % 8, :], in_=C.ps[b][:, :]),
                     reads=["ps%d" % b], writes=["kst%d" % half])
            if h % 8 == 7:
                C.final_ops.append(
                    P.dma("pool", lambda e, half=half, t0=t0: e.dma_start(
                        out=C.kT_d[8 * half:8 * half + 8, :, t0:t0 + 512].rearrange("h d t -> d h t"), in_=kst[half]),
                        reads=["kst%d" % half], writes=["kT_d"], chan="kst%d" % half))
            yield

    def v_proj(g):
        cs = g % 2
        t0 = g * 512
        ckres = ["ckvn%d_%d" % (cs, cb) for cb in range(4)]
        for tbk in range(4):
            for cg in range(4):
                b = ps_next(C)
                for c4 in range(4):
                    P.op("pe", lambda e, b=b, cg=cg, c4=c4, tbk=tbk: e.matmul(
                        out=C.ps[b][:, :], lhsT=ckvn[cs][:, c4, tbk * 128:(tbk + 1) * 128],
                        rhs=wv[:, c4, cg * 512:(cg + 1) * 512], start=(c4 == 0), stop=(c4 == 3)),
                        reads=["wv", ckres[c4]], writes=["ps%d" % b])
                src = C.ps[b][:, :].rearrange("p (h d) -> p h d", d=128)
                dst = vst[:, 4 * cg:4 * cg + 4, tbk, :]
                if cg % 2 == 0:
                    P.op("act", lambda e, src=src, dst=dst: e.copy(out=dst, in_=src),
                         reads=["ps%d" % b], writes=["vst"])
                else:
                    P.op("dve", lambda e, src=src, dst=dst: e.tensor_copy(out=dst, in_=src),
                         reads=["ps%d" % b], writes=["vst"])
                yield
        kb0 = t0 // 128
        C.final_ops.append(
            P.dma("pool", lambda e, kb0=kb0: e.dma_start(
                out=C.v_d[:, :, kb0:kb0 + 4, :].rearrange("h p k d -> p h k d"), in_=vst),
                reads=["vst"], writes=["v_d"], chan="vst"))

    G = n_groups
    fe1(0)
    fe2(0)
    latents(0)
    ckv_norm(0)
    if G > 1:
        fe1(1)
    for g in range(G):
        if g + 1 < G:
            fe2(g + 1)
            latents(g + 1)
        cnt = 0

        def tick():
            nonlocal cnt
            cnt += 1

        if g + 2 < G:
            fe1(g + 2)
        for _ in k_proj(g):
            tick()
        if g + 1 < G:
            ckv_norm(g + 1)
        for _ in v_proj(g):
            tick()
    sb.release(mk)


def make_consts():
    import ml_dtypes
    inv = np.power(np.float32(10000.0), -np.arange(0, 64, 2, dtype=np.float32) / np.float32(64)).astype(np.float32)
    ropec = np.zeros((128, 2), np.float32)
    for p in range(128):
        ropec[p, 0] = inv[p % 32]
        ropec[p, 1] = -1.0 if (p % 64) < 32 else 1.0
    return dict(ident=np.eye(128, dtype=np.float32).astype(ml_dtypes.bfloat16),
                ones=np.ones((128, 128), np.float32).astype(ml_dtypes.bfloat16),
                ropec=ropec)


def phase_Q(C):
    P, sb = C.P, C.sb
    mk = sb.mark()
    wqa = sb.alloc([128, 16, 512], BF16)
    wqn = sb.alloc([128, 4, 2048], BF16)
    wqr = sb.alloc([128, 4, 1024], BF16)
    wqrs = sb.alloc([128, 4, 1024], BF16)
    gqa = sb.alloc([128, 4], F32)
    P.dma("pool", lambda e: e.dma_start(out=wqa, in_=C.w_in[:, 0:512].rearrange("(kc p) n -> p kc n", p=128)),
          writes=["wqa"], chan="wq", bulk=True)
    P.dma("pool", lambda e: e.dma_start(out=wqn, in_=C.w_qn.rearrange("(kc p) n -> p kc n", p=128)), writes=["wqn"], chan="wq", bulk=True)
    P.dma("pool", lambda e: e.dma_start(out=wqr, in_=C.w_qr.rearrange("(kc p) n -> p kc n", p=128)), writes=["wqr"], chan="wq", bulk=True)
    P.dma("pool", lambda e: e.dma_start(out=wqrs, in_=C.w_qrs.rearrange("(kc p) n -> p kc n", p=128)), writes=["wqrs"], chan="wq", bulk=True)
    P.dma("sp", lambda e: e.dma_start(out=gqa, in_=C.g_qa.rearrange("(c p) -> p c", p=128), allow_slow_non_contiguous=True),
          writes=["gqa"], chan="constQ", bulk=True)
    xb = sb.alloc([128, D], F32)
    ssb = sb.alloc([128, 1], F32)
    sdb = sb.alloc([128, 1], F32)
    rsb = sb.alloc([128, 1], F32)
    xs = sb.alloc([128, D], BF16)
    hTq = sb.alloc([128, 16, 512], BF16)
    qraw = sb.alloc([128, 4, 512], F32)
    sq = sb.alloc([128, 4, 512], BF16)
    junk = None
    qan = sb.alloc([128, 4, 512], BF16)
    sdq = sb.alloc([128, 512], F32)
    rq = sb.alloc([128, 512], F32)
    Ct = sb.alloc([128, 512], F32)
    St = sb.alloc([128, 512], F32)
    rtmp = [sb.alloc([128, 512], I32)] + [sb.alloc([128, 512], F32) for _ in range(4)]
    t1 = sb.alloc([128, 512], F32)
    t2 = sb.alloc([128, 512], F32)

    def q_half(hf):
        t0 = 512 * hf
        for tbk in range(4):
            front_end(C, C.x_own[t0 + tbk * 128:t0 + (tbk + 1) * 128, :], 7,
                      hTq[:, :, tbk * 128:(tbk + 1) * 128], "hq_%d" % tbk, (xb, junk, ssb, sdb, rsb, xs))
        rope_tables(C, C.pos_own[:, t0:t0 + 512], 512, Ct, St, rtmp, "rq")
        hres = lambda kc: ["hq_%d_%d" % (tb, kc // 4) for tb in range(4)]
        for cb in range(4):
            b = ps_next(C)
            for kc in range(16):
                P.op("pe", lambda e, b=b, cb=cb, kc=kc: e.matmul(out=C.ps[b][:, :], lhsT=wqa[:, kc, cb * 128:(cb + 1) * 128],
                                                               rhs=hTq[:, kc, :], start=(kc == 0), stop=(kc == 15)),
                     reads=["wqa"] + hres(kc), writes=["ps%d" % b])
            P.op("dve", lambda e, b=b, cb=cb: e.tensor_copy(out=qraw[:, cb, :], in_=C.ps[b][:, :]),
                 reads=["ps%d" % b], writes=["qraw%d" % cb])
            P.op("act", lambda e, cb=cb: e.activation(out=sq[:, cb, :], in_=qraw[:, cb, :], func=AF.Square),
                 reads=["qraw%d" % cb], writes=["qsq%d" % cb])
        b = ps_next(C)
        for cb in range(4):
            P.op("pe", lambda e, b=b, cb=cb: e.matmul(out=C.ps[b][:, :], lhsT=C.ones_sb, rhs=sq[:, cb, :],
                                                    start=(cb == 0), stop=(cb == 3)),
                 reads=["ones", "qsq%d" % cb], writes=["ps%d" % b])
        P.op("act", lambda e, b=b: e.activation(out=sdq, in_=C.ps[b][:, :], func=AF.Sqrt, scale=1.0 / 512, bias=C.eps_sb),
             reads=["ps%d" % b, "eps"], writes=["sdq"])
        P.op("dve", lambda e: e.reciprocal(out=rq, in_=sdq), reads=["sdq"], writes=["rq"])
        for cb in range(4):
            P.op("dve", lambda e, cb=cb: e.scalar_tensor_tensor(out=qan[:, cb, :], in0=qraw[:, cb, :], scalar=gqa[:, cb:cb + 1],
                                                             in1=rq, op0=ALU.mult, op1=ALU.mult),
                 reads=["qraw%d" % cb, "gqa", "rq"], writes=["qan%d" % cb])
        qres = ["qan%d" % cb for cb in range(4)]
        for h in range(NH):
            b = ps_next(C)
            for c4 in range(4):
                P.op("pe", lambda e, b=b, h=h, c4=c4: e.matmul(out=C.ps[b][:, :], lhsT=wqn[:, c4, h * 128:(h + 1) * 128],
                                                             rhs=qan[:, c4, :], start=(c4 == 0), stop=(c4 == 3)),
                     reads=["wqn", qres[c4]], writes=["ps%d" % b])
            if h % 2 == 0:
                P.op("act", lambda e, b=b, h=h: e.copy(out=C.Qn[:, h, t0:t0 + 512], in_=C.ps[b][:, :]),
                     reads=["ps%d" % b], writes=["Qn%d" % h])
            else:
                P.op("dve", lambda e, b=b, h=h: e.tensor_copy(out=C.Qn[:, h, t0:t0 + 512], in_=C.ps[b][:, :]),
                     reads=["ps%d" % b], writes=["Qn%d" % h])
        for hp in range(8):
            b1 = ps_next(C)
            for c4 in range(4):
                P.op("pe", lambda e, b1=b1, hp=hp, c4=c4: e.matmul(out=C.ps[b1][:, :], lhsT=wqr[:, c4, hp * 128:(hp + 1) * 128],
                                                                 rhs=qan[:, c4, :], start=(c4 == 0), stop=(c4 == 3)),
                     reads=["wqr", qres[c4]], writes=["ps%d" % b1])
            P.op("dve", lambda e, b1=b1: e.tensor_tensor(out=t1, in0=C.ps[b1][:, :], in1=Ct, op=ALU.mult),
                 reads=["ps%d" % b1, "rqC"], writes=["qt1"])
            b2 = ps_next(C)
            for c4 in range(4):
                P.op("pe", lambda e, b2=b2, hp=hp, c4=c4: e.matmul(out=C.ps[b2][:, :], lhsT=wqrs[:, c4, hp * 128:(hp + 1) * 128],
                                                                 rhs=qan[:, c4, :], start=(c4 == 0), stop=(c4 == 3)),
                     reads=["wqrs", qres[c4]], writes=["ps%d" % b2])
            P.op("dve", lambda e, b2=b2: e.tensor_tensor(out=t2, in0=C.ps[b2][:, :], in1=St, op=ALU.mult),
                 reads=["ps%d" % b2, "rqS"], writes=["qt2"])
            P.op("pool", lambda e, hp=hp: e.tensor_tensor(out=C.Qr[:, hp, t0:t0 + 512], in0=t1, in1=t2, op=ALU.add),
                 reads=["qt1", "qt2"], writes=["Qr%d" % hp])

    for hf in range(2):
        q_half(hf)
    sb.release(mk)


def phase_A(C):
    P, sb = C.P, C.sb
    mk = sb.mark()
    S_all = C.S_all
    nkb = S_all // 128
    nq = nkb // 16
    KrT = [sb.alloc([128, S_all], BF16) for _ in range(2)]
    kring = [sb.alloc([128, 2048], BF16) for _ in range(4)]
    vring = [sb.alloc([128, 16, 128], BF16) for _ in range(4)]
    NPT = 6
    Pt = [sb.alloc([128, 512], BF16) for _ in range(NPT)]
    negm = sb.alloc([128, 16], F32)
    Of = sb.alloc([128, 1024], F32)
    rl = sb.alloc([128, 1024], F32)
    accS = sb.alloc([128, 1024], F32)
    onesf = sb.alloc([128, 128], F32)
    stores = list(C.final_ops)
    P.op("pool", lambda e: e.memset(onesf, 1.0), writes=["onesf"])
    P.op("pool", lambda e: e.memset(KrT[0][64:128, :], 0.0), writes=["KrT0z"])
    P.op("pool", lambda e: e.memset(KrT[1][0:64, :], 0.0), writes=["KrT1z"])
    P.dma("sp", lambda e: e.dma_start(out=negm, in_=C.mask16), writes=["negm"], chan="constA", bulk=True)
    P.dma("sp", lambda e: e.dma_start(out=KrT[0][0:64, :], in_=C.krT_d[0:64, :]), writes=["KrT0"], chan="constA", bulk=True, extra=stores)
    P.dma("sp", lambda e: e.dma_start(out=KrT[1][64:128, :], in_=C.krT_d[64:128, :]), writes=["KrT1"], chan="constA", bulk=True, extra=stores)

    def load_q(i):
        h, q = divmod(i, nq)
        s = i % 4
        P.dma("sp", lambda e: e.dma_start(out=kring[s], in_=C.kT_d[h, :, 2048 * q:2048 * (q + 1)]),
              writes=["kq%d" % s], chan="kq%d" % s, extra=stores)
        P.dma("sp", lambda e: e.dma_start(out=vring[s], in_=C.v_d[h, :, 16 * q:16 * (q + 1), :]),
              writes=["vq%d" % s], chan="vq%d" % s, extra=stores)

    nload = NH * nq
    for i in range(min(4, nload)):
        load_q(i)

    units = []
    for h in range(NH):
        for kb in range(nkb):
            for half in range(2):
                lo = max(16 * kb, 512 * half)
                hi = 512 * (half + 1)
                if lo < hi:
                    units.append((h, kb, half, lo, hi))
    last_kb = {0: min(nkb - 1, 31), 1: nkb - 1}
    sbank = [5, 6, 7]
    NSB = 3
    ACCB = [3, 4]
    pool_cnt = {}

    def emit_S(u, ui):
        h, kb, half, lo, hi = u
        n = hi - lo
        b = sbank[ui % NSB]
        s = (h * nq + kb // 16) % 4
        kk = (kb % 16) * 128
        par = h % 2
        P.op("pe", lambda e: e.matmul(out=C.ps[b][:, 0:n], lhsT=kring[s][:, kk:kk + 128], rhs=C.Qn[:, h, lo:hi],
                                      start=True, stop=False),
             reads=["kq%d" % s, "Qn%d" % h], writes=["ps%d" % b])
        P.op("pe", lambda e: e.matmul(out=C.ps[b][:, 0:n], lhsT=KrT[par][:, kb * 128:(kb + 1) * 128],
                                      rhs=C.Qr[:, h // 2, lo:hi], start=False, stop=True),
             reads=["KrT%d" % par, "KrT%dz" % par, "Qr%d" % (h // 2)], writes=["ps%d" % b])
        if lo == 16 * kb:
            P.op("dve", lambda e: e.tensor_tensor(out=C.ps[b][:, 0:16], in0=C.ps[b][:, 0:16], in1=negm, op=ALU.add),
                 reads=["ps%d" % b, "negm"], writes=["ps%d" % b])

    def emit_PV(u, ui):
        h, kb, half, lo, hi = u
        n = hi - lo
        b = sbank[ui % NSB]
        pt = Pt[ui % NPT]
        pres = "Pt%d" % (ui % NPT)
        s = (h * nq + kb // 16) % 4
        hp = h % 2
        P.op("act", lambda e: e.activation(out=pt[:, 0:n], in_=C.ps[b][:, 0:n], func=AF.Exp, scale=SCALE),
             reads=["ps%d" % b], writes=[pres])
        c0 = lo - 512 * half
        first = (kb == 0)
        last = (kb == last_kb[half])
        ob, lb = half, 2
        P.op("pe", lambda e: e.matmul(out=C.ps[ob][:, c0:c0 + n], lhsT=vring[s][:, kb % 16, :], rhs=pt[:, 0:n],
                                      start=first, stop=last),
             reads=["vq%d" % s, pres], writes=["ps%d" % ob])
        ab = ACCB[half]
        if first:
            P.op("dve", lambda e: e.tensor_copy(out=C.ps[ab][:, c0:c0 + n], in_=pt[:, 0:n]),
                 reads=[pres], writes=["ps%d" % ab])
        else:
            P.op("dve", lambda e: e.tensor_tensor(out=C.ps[ab][:, c0:c0 + n], in0=C.ps[ab][:, c0:c0 + n], in1=pt[:, 0:n], op=ALU.add),
                 reads=[pres, "ps%d" % ab], writes=["ps%d" % ab])
        if last:
            hs_ = slice(512 * half, 512 * (half + 1))
            P.op("dve", lambda e: e.tensor_copy(out=accS[:, hs_], in_=C.ps[ab][:, :]), reads=["ps%d" % ab], writes=["accS%d" % half])
            P.op("pe", lambda e: e.matmul(out=C.ps[lb][:, :], lhsT=onesf, rhs=accS[:, hs_], start=True, stop=True),
                 reads=["onesf", "accS%d" % half], writes=["ps%d" % lb])
            P.op("dve", lambda e: e.reciprocal(out=rl[:, hs_], in_=C.ps[lb][:, :]), reads=["ps%d" % lb], writes=["rl%d" % half])
            P.op("dve", lambda e: e.tensor_tensor(out=C.attn[:, h, hs_], in0=C.ps[ob][:, :], in1=rl[:, hs_], op=ALU.mult),
                 reads=["ps%d" % ob, "rl%d" % half], writes=["attn%d" % h])
        if kb % 16 == 15 and half == 1:
            i = h * nq + kb // 16
            if i + 4 < nload:
                load_q(i + 4)

    LOOK = 2
    nu = len(units)
    for ui in range(min(LOOK, nu)):
        emit_S(units[ui], ui)
    for ui in range(nu):
        if ui + LOOK < nu:
            emit_S(units[ui + LOOK], ui + LOOK)
        emit_PV(units[ui], ui)
    sb.release(mk)


OFF_Z, OFF_CIN, OFF_BG, OFF_CG, OFF_ZC, OFF_GM, OFF_GC = 1088, 3136, 5184, 7232, 9280, 11328, 13376


def phase_R(C):
    P, sb = C.P, C.sb
    mg = sb.alloc([128, 16, 1024], BF16)
    mR = sb.mark()
    hT = sb.alloc([128, 16, 1024], BF16)
    hTh = sb.alloc([128, 16, 128], BF16)
    gc = sb.alloc([128, 16, 1024], BF16)
    wr = [sb.alloc([128, 16, 256], BF16) for _ in range(4)]
    convw = sb.alloc([128, 3, 16], F32)
    for k in range(3):
        P.dma("sp", lambda e, k=k: e.dma_start(out=convw[:, k, :], in_=C.conv_w[k].rearrange("(j p) -> p j", p=128),
                                             allow_slow_non_contiguous=True),
              writes=["convw"], chan="constR", bulk=True)
    mk = sb.mark()
    xb = sb.alloc([128, D], F32)
    junk = None
    ssb = sb.alloc([128, 1], F32)
    sdb = sb.alloc([128, 1], F32)
    rsb = sb.alloc([128, 1], F32)
    xs = sb.alloc([128, D], BF16)
    for blk in range(8):
        front_end(C, C.x_own[blk * 128:(blk + 1) * 128, :], 8, hT[:, :, blk * 128:(blk + 1) * 128], "hr_%d" % blk,
                  (xb, junk, ssb, sdb, rsb, xs))
    front_end(C, C.x_halo, 8, hTh[:, :, :], "hr_8", (xb, junk, ssb, sdb, rsb, xs))
    hres = lambda kc, half: ["hr_%d_%d" % (4 * half + tb, kc // 4) for tb in range(4)]
    hhres = lambda kc: ["hr_8_%d" % (kc // 4)]
    sb.release(mk)

    tiles = []
    for jp in range(8):
        tiles.append(C.w_in[:, OFF_Z + 256 * jp:OFF_Z + 256 * (jp + 1)])
    for jp in range(8):
        for off in (OFF_CIN, OFF_CG, OFF_BG, OFF_ZC):
            tiles.append(C.w_in[:, off + 256 * jp:off + 256 * (jp + 1)])
    for jp in range(8):
        tiles.append(C.w_o_mla[:, 256 * jp:256 * (jp + 1)])
        tiles.append(C.w_in[:, OFF_GM + 256 * jp:OFF_GM + 256 * (jp + 1)])
        tiles.append(C.w_in[:, OFF_GC + 256 * jp:OFF_GC + 256 * (jp + 1)])
        tiles.append(C.w_o_conv[:, 256 * jp:256 * (jp + 1)])
    st = {"next": 0}

    def issue(n=1):
        for _ in range(n):
            i = st["next"]
            if i >= len(tiles):
                return
            st["next"] = i + 1
            s_ = i % 4
            src = tiles[i].rearrange("(kc p) n -> p kc n", p=128)
            P.dma("pool", lambda e, s_=s_, src=src: e.dma_start(out=wr[s_], in_=src), writes=["wr%d" % s_], chan="wr%d" % s_)

    ti = {"i": 0}

    def take():
        i = ti["i"]
        ti["i"] = i + 1
        while st["next"] <= min(i + 3, len(tiles) - 1):
            issue(1)
        return wr[i % 4], "wr%d" % (i % 4)

    def mm(w, wres, sub, rhs_fn, rres_fn, n):
        b = ps_next(C)
        for kc in range(16):
            P.op("pe", lambda e, b=b, kc=kc: e.matmul(out=C.ps[b][:, 0:n], lhsT=w[:, kc, sub * 128:(sub + 1) * 128],
                                                    rhs=rhs_fn(kc), start=(kc == 0), stop=(kc == 15)),
                 reads=[wres] + rres_fn(kc), writes=["ps%d" % b])
        return b

    tA = sb.alloc([128, 2, 1024], F32)
    tB = sb.alloc([128, 2, 1024], F32)
    U = sb.alloc([128, 2, 64 * 18], F32)
    cinh = sb.alloc([128, 2, 128], F32)
    halves = [slice(0, 512), slice(512, 1024)]

    def own_mm(w, wres, sub, half):
        hsl = halves[half]
        return mm(w, wres, sub, lambda kc: hT[:, kc, hsl], lambda kc: hres(kc, half), 512)

    def r1(jp):
        w, wres = take()
        for sub in range(2):
            j = 2 * jp + sub
            for half in range(2):
                hsl = halves[half]
                b = own_mm(w, wres, sub, half)
                P.op("act", lambda e, b=b, hsl=hsl, sub=sub: e.activation(out=tA[:, sub, hsl], in_=C.ps[b][:, :], func=AF.Silu),
                     reads=["ps%d" % b], writes=["tA%d_%d" % (sub, half)])
                P.op("dve", lambda e, j=j, hsl=hsl, sub=sub: e.tensor_tensor(out=C.attn[:, j, hsl], in0=C.attn[:, j, hsl],
                                                                         in1=tA[:, sub, hsl], op=ALU.mult),
                     reads=["tA%d_%d" % (sub, half), "attn%d" % j], writes=["attn%d" % j])

    for jp in range(8):
        r1(jp)

    def Uv(sub):
        return U[:, sub, :].rearrange("p (m j) -> p m j", j=18)

    def r2(jp):
        w, wres = take()
        for sub in range(2):
            for half in range(2):
                hsl = halves[half]
                b = own_mm(w, wres, sub, half)
                P.op("act", lambda e, b=b, hsl=hsl, sub=sub: e.copy(out=tA[:, sub, hsl], in_=C.ps[b][:, :]),
                     reads=["ps%d" % b], writes=["tA%d_%d" % (sub, half)])
            b = mm(w, wres, sub, lambda kc: hTh[:, kc, :], hhres, 128)
            P.op("act", lambda e, b=b, sub=sub: e.copy(out=cinh[:, sub, :], in_=C.ps[b][:, 0:128]),
                 reads=["ps%d" % b], writes=["cinh%d" % sub])
        w, wres = take()
        for sub in range(2):
            j = 2 * jp + sub
            for half in range(2):
                hsl = halves[half]
                b = own_mm(w, wres, sub, half)
                P.op("dve", lambda e, b=b, half=half, hsl=hsl, sub=sub: e.tensor_tensor(
                    out=Uv(sub)[:, 32 * half:32 * (half + 1), 2:18], in0=C.ps[b][:, :].rearrange("p (m j) -> p m j", j=16),
                    in1=tA[:, sub, hsl].rearrange("p (m j) -> p m j", j=16), op=ALU.mult),
                    reads=["ps%d" % b, "tA%d_%d" % (sub, half)], writes=["Uo%d_%d" % (sub, half)])
            b = mm(w, wres, sub, lambda kc: hTh[:, kc, :], hhres, 128)
            P.op("dve", lambda e, b=b, sub=sub: e.tensor_tensor(out=Uv(sub)[:, :, 0:2],
                                                             in0=C.ps[b][:, 0:128].rearrange("p (m j) -> p m j", j=2),
                                                             in1=cinh[:, sub, :].rearrange("p (m j) -> p m j", j=2), op=ALU.mult),
                 reads=["ps%d" % b, "cinh%d" % sub], writes=["Uh%d" % sub])
            tB3 = tB[:, sub, :].rearrange("p (m j) -> p m j", j=16)
            ures = ["Uo%d_0" % sub, "Uo%d_1" % sub, "Uh%d" % sub]
            tres = "tB%d" % sub
            P.op("dve", lambda e, j=j, sub=sub, tB3=tB3: e.tensor_scalar(out=tB3, in0=Uv(sub)[:, :, 0:16], scalar1=convw[:, 0, j:j + 1],
                                                                     scalar2=None, op0=ALU.mult),
                 reads=ures + ["convw"], writes=[tres])
            P.op("dve", lambda e, j=j, sub=sub, tB3=tB3: e.scalar_tensor_tensor(out=tB3, in0=Uv(sub)[:, :, 1:17], scalar=convw[:, 1, j:j + 1],
                                                                            in1=tB3, op0=ALU.mult, op1=ALU.add),
                 reads=ures + ["convw", tres], writes=[tres])
            P.op("dve", lambda e, j=j, sub=sub, tB3=tB3: e.scalar_tensor_tensor(out=tB3, in0=Uv(sub)[:, :, 2:18], scalar=convw[:, 2, j:j + 1],
                                                                            in1=tB3, op0=ALU.mult, op1=ALU.add),
                 reads=ures + ["convw", tres], writes=[tres])
        w, wres = take()
        for sub in range(2):
            for half in range(2):
                hsl = halves[half]
                b = own_mm(w, wres, sub, half)
                P.op("dve", lambda e, b=b, hsl=hsl, sub=sub: e.tensor_tensor(out=tB[:, sub, hsl], in0=C.ps[b][:, :], in1=tB[:, sub, hsl], op=ALU.mult),
                     reads=["ps%d" % b, "tB%d" % sub], writes=["tB%d" % sub])
        w, wres = take()
        for sub in range(2):
            j = 2 * jp + sub
            for half in range(2):
                hsl = halves[half]
                b = own_mm(w, wres, sub, half)
                P.op("act", lambda e, b=b, hsl=hsl, sub=sub: e.activation(out=tA[:, sub, hsl], in_=C.ps[b][:, :], func=AF.Silu),
                     reads=["ps%d" % b], writes=["tA%d_%d" % (sub, half)])
                P.op("pool", lambda e, j=j, hsl=hsl, sub=sub: e.tensor_tensor(out=gc[:, j, hsl], in0=tB[:, sub, hsl], in1=tA[:, sub, hsl], op=ALU.mult),
                     reads=["tB%d" % sub, "tA%d_%d" % (sub, half)], writes=["gc%d" % j])

    for jp in range(8):
        r2(jp)

    allattn = ["attn%d" % j for j in range(16)]
    allgc = ["gc%d" % j for j in range(16)]

    def r3(jp):
        w, wres = take()
        for sub in range(2):
            for half in range(2):
                hsl = halves[half]
                b = mm(w, wres, sub, lambda kc, hsl=hsl: C.attn[:, kc, hsl], lambda kc: ["attn%d" % kc], 512)
                P.op("act", lambda e, b=b, hsl=hsl, sub=sub: e.copy(out=tB[:, sub, hsl], in_=C.ps[b][:, :]),
                     reads=["ps%d" % b], writes=["tB%d" % sub])
        w, wres = take()
        for sub in range(2):
            for half in range(2):
                hsl = halves[half]
                b = own_mm(w, wres, sub, half)
                P.op("act", lambda e, b=b, hsl=hsl, sub=sub: e.activation(out=tA[:, sub, hsl], in_=C.ps[b][:, :], func=AF.Sigmoid),
                     reads=["ps%d" % b], writes=["tA%d_%d" % (sub, half)])
                P.op("dve", lambda e, hsl=hsl, sub=sub: e.tensor_tensor(out=tB[:, sub, hsl], in0=tB[:, sub, hsl], in1=tA[:, sub, hsl], op=ALU.mult),
                     reads=["tB%d" % sub, "tA%d_%d" % (sub, half)], writes=["tB%d" % sub])
        w, wres = take()
        for sub in range(2):
            for half in range(2):
                hsl = halves[half]
                b = own_mm(w, wres, sub, half)
                P.op("act", lambda e, b=b, hsl=hsl, sub=sub: e.activation(out=tA[:, sub, hsl], in_=C.ps[b][:, :], func=AF.Sigmoid),
                     reads=["ps%d" % b], writes=["tA%d_%d" % (sub, half)])
        w, wres = take()
        for sub in range(2):
            j = 2 * jp + sub
            for half in range(2):
                hsl = halves[half]
                b = mm(w, wres, sub, lambda kc, hsl=hsl: gc[:, kc, hsl], lambda kc: ["gc%d" % kc], 512)
                P.op("dve", lambda e, b=b, hsl=hsl, sub=sub: e.tensor_tensor(out=tA[:, sub, hsl], in0=C.ps[b][:, :], in1=tA[:, sub, hsl], op=ALU.mult),
                     reads=["ps%d" % b, "tA%d_%d" % (sub, half)], writes=["tA%d_%d" % (sub, half)])
                P.op("pool", lambda e, j=j, hsl=hsl, sub=sub: e.tensor_tensor(out=mg[:, j, hsl], in0=tB[:, sub, hsl], in1=tA[:, sub, hsl], op=ALU.add),
                     reads=["tB%d" % sub, "tA%d_%d" % (sub, half)], writes=["mg%d" % j])

    for jp in range(8):
        r3(jp)

    if hasattr(C, "dbg_mg"):
        C.final_ops.append(P.dma("sp", lambda e: e.dma_start(out=C.dbg_gated, in_=C.attn), reads=allattn, chan="dbg"))
        C.final_ops.append(P.dma("sp", lambda e: e.dma_start(out=C.dbg_gc, in_=gc), reads=allgc, chan="dbg"))
        C.final_ops.append(P.dma("sp", lambda e: e.dma_start(out=C.dbg_mg, in_=mg), reads=["mg%d" % j for j in range(16)], chan="dbg"))
    P.barrier()
    sb.release(mR)
    wout = sb.alloc([128, 16, 2048], BF16)
    gpost = sb.alloc([128, D], F32)
    xr = [sb.alloc([128, D], F32) for _ in range(2)]
    ot = [sb.alloc([128, D], F32) for _ in range(2)]
    ssq = sb.alloc([128, 4], F32)
    sst = sb.alloc([128, 1], F32)
    sdo = sb.alloc([128, 1], F32)
    rso = sb.alloc([128, 1], F32)
    junk4 = sb.alloc([128, 512], BF16)
    for cg in range(4):
        P.dma("pool", lambda e, cg=cg: e.dma_start(out=wout[:, :, 512 * cg:512 * (cg + 1)],
                                                 in_=C.w_out[:, 512 * cg:512 * (cg + 1)].rearrange("(kc p) n -> p kc n", p=128)),
              writes=["wout%d" % cg], chan="wout%d" % cg)
    P.dma("sp", lambda e: e.dma_start(out=gpost, in_=C.g_post.broadcast_to([128, D])), writes=["gpost"], chan="constR4", bulk=True)
    allmg = ["mg%d" % j for j in range(16)]

    def r4(blk):
        s_ = blk % 2
        tsl = slice(128 * blk, 128 * (blk + 1))
        P.dma("sp", lambda e: e.dma_start(out=xr[s_], in_=C.x_own[tsl, :]), writes=["xr%d" % s_], chan="xr%d" % s_)
        banks = []
        for cg in range(4):
            b = ps_next(C)
            banks.append(b)
            for kc in range(16):
                P.op("pe", lambda e, b=b, kc=kc, cg=cg: e.matmul(out=C.ps[b][:, :], lhsT=mg[:, kc, tsl],
                                                               rhs=wout[:, kc, 512 * cg:512 * (cg + 1)],
                                                               start=(kc == 0), stop=(kc == 15)),
                     reads=["mg%d" % kc, "wout%d" % cg], writes=["ps%d" % b])
            P.op("act", lambda e, b=b, cg=cg: e.activation(out=junk4, in_=C.ps[b][:, :], func=AF.Square, accum_out=ssq[:, cg:cg + 1]),
                 reads=["ps%d" % b], writes=["junk4", "ssq%d" % cg])
        P.op("dve", lambda e: e.tensor_reduce(out=sst, in_=ssq, axis=mybir.AxisListType.X, op=ALU.add),
             reads=["ssq%d" % cg for cg in range(4)], writes=["sst"])
        P.op("act", lambda e: e.activation(out=sdo, in_=sst, func=AF.Sqrt, scale=1.0 / D, bias=C.eps_sb),
             reads=["sst", "eps"], writes=["sdo"])
        P.op("dve", lambda e: e.reciprocal(out=rso, in_=sdo), reads=["sdo"], writes=["rso"])
        for cg in range(4):
            b = banks[cg]
            csl = slice(512 * cg, 512 * (cg + 1))
            P.op("dve", lambda e, b=b, csl=csl: e.scalar_tensor_tensor(out=ot[s_][:, csl], in0=C.ps[b][:, :], scalar=rso,
                                                                     in1=gpost[:, csl], op0=ALU.mult, op1=ALU.mult),
                 reads=["ps%d" % b, "rso", "gpost"], writes=["ot%d_%d" % (s_, cg)])
            P.op("pool", lambda e, csl=csl: e.tensor_tensor(out=ot[s_][:, csl], in0=ot[s_][:, csl], in1=xr[s_][:, csl], op=ALU.add),
                 reads=["ot%d_%d" % (s_, cg), "xr%d" % s_], writes=["ot%d_%d" % (s_, cg)])
        C.final_ops.append(
            P.dma("sp", lambda e: e.dma_start(out=C.out[tsl, :], in_=ot[s_]),
                  reads=["ot%d_%d" % (s_, cg) for cg in range(4)], writes=["out"], chan="ot%d" % s_))

    for blk in range(8):
        r4(blk)


def _own_rows(c):
    return np.concatenate([np.arange(128 * m + 16 * c, 128 * m + 16 * c + 16) for m in range(64)])


def prep(x, positions, pre_norm_g, w_in, q_a_norm_g, w_q_b, kv_a_norm_g, w_kv_b,
         conv_w, w_o_mla, w_o_conv, w_out, post_norm_g, cores=range(NCORES)):
    import ml_dtypes
    x2 = np.ascontiguousarray(np.asarray(x, dtype=np.float32).reshape(S, D))
    pos = np.ascontiguousarray(np.asarray(positions, dtype=np.int32).reshape(1, S))
    w_in = np.asarray(w_in, dtype=np.float32)
    kr = w_in[:, 1024:1088]
    krs = np.concatenate([kr[:, 32:], kr[:, :32]], axis=1)
    w_lat = np.ascontiguousarray(np.concatenate([w_in[:, 512:1024], kr, kr, krs, krs], axis=1))
    wkv = np.asarray(w_kv_b, dtype=np.float32).reshape(512, NH, 256)
    w_k = np.ascontiguousarray(wkv[:, :, :128].reshape(512, 2048))
    w_v = np.ascontiguousarray(wkv[:, :, 128:].reshape(512, 2048))
    wq = np.asarray(w_q_b, dtype=np.float32).reshape(512, NH, 192)
    w_qn = np.ascontiguousarray(wq[:, :, :128].reshape(512, 2048))
    qr = wq[:, :, 128:]
    w_qr = np.ascontiguousarray(qr.reshape(512, 1024))
    w_qrs = np.ascontiguousarray(np.concatenate([qr[:, :, 32:], qr[:, :, :32]], axis=2).reshape(512, 1024))
    consts = make_consts()
    shared = dict(x_all=x2, pos_all=pos, g_pre=np.asarray(pre_norm_g, np.float32).reshape(1, D), w_lat=w_lat,
                  g_kv=np.asarray(kv_a_norm_g, np.float32), w_k=w_k, w_v=w_v, w_in=np.ascontiguousarray(w_in),
                  g_qa=np.asarray(q_a_norm_g, np.float32), w_qn=w_qn, w_qr=w_qr, w_qrs=w_qrs,
                  conv_w=np.ascontiguousarray(np.asarray(conv_w, np.float32)),
                  w_o_mla=np.ascontiguousarray(np.asarray(w_o_mla, np.float32)),
                  w_o_conv=np.ascontiguousarray(np.asarray(w_o_conv, np.float32)),
                  w_out=np.ascontiguousarray(np.asarray(w_out, np.float32)),
                  g_post=np.asarray(post_norm_g, np.float32).reshape(1, D), **consts)
    in_maps = []
    rows_all = []
    for c in cores:
        rows = _own_rows(c)
        rows_all.append(rows)
        x_halo = np.zeros((128, D), np.float32)
        for m in range(64):
            for t in range(2):
                g = 128 * m + 16 * c - 2 + t
                if g >= 0:
                    x_halo[2 * m + t] = x2[g]
        kk = np.arange(128)[:, None]
        jj = np.arange(16)[None, :]
        mask16 = np.where(kk <= 16 * c + jj, 0.0, -30000.0).astype(np.float32)
        im = dict(shared)
        im.update(x_own=np.ascontiguousarray(x2[rows]), pos_own=np.ascontiguousarray(pos[:, rows]),
                  x_halo=x_halo, mask16=mask16)
        in_maps.append(im)
    return in_maps, rows_all


def kernel(x, positions, pre_norm_g, w_in, q_a_norm_g, w_q_b, kv_a_norm_g, w_kv_b,
           conv_w, w_o_mla, w_o_conv, w_out, post_norm_g):
    in_maps, rows_all = prep(x, positions, pre_norm_g, w_in, q_a_norm_g, w_q_b, kv_a_norm_g, w_kv_b,
                             conv_w, w_o_mla, w_o_conv, w_out, post_norm_g)
    nc = build()
    res = run_bass_kernel_spmd(nc, in_maps, core_ids=list(range(NCORES)))
    out = np.zeros((S, D), np.float32)
    for c in range(NCORES):
        out[rows_all[c]] = np.asarray(res.results[c]["out"], dtype=np.float32)
    return out.reshape(1, S, D)
```

```python
import contextlib
import math

import numpy as np
import concourse.bass as bass
import concourse.mybir as mybir
from concourse.bass_utils import run_bass_kernel_spmd

F32 = mybir.dt.float32
BF16 = mybir.dt.bfloat16
I32 = mybir.dt.int32
AF = mybir.ActivationFunctionType
ALU = mybir.AluOpType

D = 2048
S = 8192
NH = 16
EPS = 1e-6
NCORES = 8
SCALE = 1.0 / math.sqrt(192.0)
TWO_PI = 2.0 * math.pi
INLINE_WAITS = True
import os
FLAGS = os.environ.get("KFLAGS", "").split(",")


class _Op:
    __slots__ = ("eng", "fn", "deps", "kind", "chan", "sig", "needed", "lhs", "depres")


class Prog:
    ENGS = ("pe", "act", "dve", "pool", "sp")

    def __init__(self):
        self.ops = []
        self.res = {}
        self.chan_count = {}
        self.bulk = set()

    def _add(self, eng, fn, reads, writes, kind, chan=None, extra=(), lhs=None):
        op = _Op()
        op.eng, op.fn, op.kind, op.chan = eng, fn, kind, chan
        op.needed, op.sig = False, None
        op.lhs = None if lhs is None else set(lhs)
        deps = {}
        depres = {}

        def put(d, k, r):
            if d not in deps:
                deps[d] = k
                depres[d] = {r}
            else:
                depres[d].add(r)

        for r in reads:
            st = self.res.setdefault(r, [None, []])
            if st[0] is not None:
                put(st[0], "raw", r)
            if r.startswith("ps"):
                for rd in st[1]:
                    if rd.eng != eng:
                        put(rd, "raw", r)
        for w in writes:
            st = self.res.setdefault(w, [None, []])
            if st[0] is not None:
                put(st[0], "waw", w)
            for rd in st[1]:
                put(rd, "war", w)
        op.depres = depres
        for r in reads:
            self.res[r][1].append(op)
        for w in writes:
            self.res[w] = [op, []]
        final = []
        for d, k in deps.items():
            if d is op:
                continue
            if d.eng == eng and d.kind == "c" and kind == "c":
                if eng == "pe" or k == "war":
                    continue
            if kind == "d" and d.kind == "d" and d.chan == chan and chan in self.bulk:
                continue
            final.append(d)
        for d in extra:
            if d not in final:
                final.append(d)
        op.deps = final
        for d in final:
            d.needed = True
        if kind == "d":
            n = self.chan_count.get(chan, 0) + 1
            self.chan_count[chan] = n
            op.sig = (("chan", chan), 16 * n)
        self.ops.append(op)
        return op

    def op(self, eng, fn, reads=(), writes=(), extra=(), lhs=None):
        return self._add(eng, fn, list(reads), list(writes), "c", extra=extra, lhs=lhs)

    def dma(self, eng, fn, reads=(), writes=(), chan=None, bulk=False, extra=()):
        assert chan is not None
        if bulk:
            self.bulk.add(chan)
        else:
            assert chan not in self.bulk
        return self._add(eng, fn, list(reads), list(writes), "d", chan, extra=extra)

    def barrier(self):
        last = {}
        lastd = {}
        for o in self.ops:
            if o.fn is None:
                continue
            if o.kind == "d":
                lastd[o.chan] = o
            else:
                last[o.eng] = o
        for e in self.ENGS:
            deps = list(last.values()) + list(lastd.values())
            self.join(e, deps)

    def join(self, eng, ops):
        op = _Op()
        op.eng, op.fn, op.kind, op.chan, op.needed, op.sig = eng, None, "c", None, False, None
        op.lhs, op.depres = None, {}
        op.deps = list(ops)
        for d in ops:
            d.needed = True
        self.ops.append(op)
        return op

    def emit(self, nc):
        cnt = {e: 0 for e in self.ENGS}
        for op in self.ops:
            if op.kind == "c" and op.needed:
                cnt[op.eng] += 1
                op.sig = (("eng", op.eng), cnt[op.eng])
            if op.kind == "d" and op.chan in self.bulk:
                op.sig = (("chan", op.chan), 16 * self.chan_count[op.chan])
        with contextlib.ExitStack() as st:
            sems = {}
            for e in self.ENGS:
                sems[("eng", e)] = st.enter_context(nc.semaphore("s_" + e))
            for c in self.chan_count:
                sems[("chan", c)] = st.enter_context(nc.semaphore("c_" + str(c)))
            block = st.enter_context(nc.Block())

            def body(ename):
                def f(eng):
                    waited = {}
                    for op in self.ops:
                        if op.eng != ename:
                            continue
                        inline = []
                        for d in op.deps:
                            key, val = d.sig
                            if waited.get(key, 0) < val:
                                waited[key] = val
                                rs = op.depres.get(d)
                                if (ename == "pe" and INLINE_WAITS and op.lhs is not None and op.fn is not None
                                        and rs is not None and not (rs & op.lhs)):
                                    inline.append((key, val))
                                else:
                                    eng.wait_ge(sems[key], val)
                        for key, val in inline[:-1]:
                            eng.wait_ge(sems[key], val)
                        if op.fn is None:
                            continue
                        inst = op.fn(eng)
                        if inline:
                            inst._wait_ge(sems[inline[-1][0]], inline[-1][1])
                        if op.kind == "d":
                            inst.then_inc(sems[op.sig[0]], 16)
                        elif op.needed:
                            inst.then_inc(sems[op.sig[0]], 1)
                return f

            block.tensor(body("pe"))
            block.scalar(body("act"))
            block.vector(body("dve"))
            block.gpsimd(body("pool"))
            block.sync(body("sp"))


class SB:
    def __init__(self, nc, nbytes):
        self.t = nc.alloc_sbuf_tensor("sb", [128, nbytes // 2], BF16)
        self.off = 0
        self.cap = nbytes

    def alloc(self, shape, dtype):
        assert shape[0] == 128
        n = int(np.prod(shape[1:]))
        size = 4 if dtype in (F32, I32) else 2
        nb = (n * size + 63) // 64 * 64
        assert self.off + nb <= self.cap, ("SBUF overflow", self.off, nb, self.cap)
        ap = self.t[:, self.off // 2:(self.off + n * size) // 2]
        self.off += nb
        if dtype != BF16:
            ap = ap.bitcast(dtype)
        if len(shape) == 3:
            ap = ap.rearrange("p (a b) -> p a b", b=shape[2])
        elif len(shape) == 4:
            ap = ap.rearrange("p (a b c) -> p a b c", b=shape[2], c=shape[3])
        return ap

    def mark(self):
        return self.off

    def release(self, m):
        self.off = m


class Ctx:
    pass


def own_blocks(c):
    out = []
    for g in range(4):
        out += [16 * g + c, 16 * g + 15 - c]
    return out


def build(n_kv_groups=16, phases=("K", "Q", "A", "R"), dbg=False, stop=99):
    nc = bass.Bass("TRN2", target_bir_lowering=False)
    P = Prog()
    C = Ctx()
    C.nc, C.P = nc, P
    C.stop = stop
    S_all = n_kv_groups * 512
    C.S_all = S_all

    def din(name, shape, dt=F32):
        return nc.dram_tensor(name, list(shape), dt, kind="ExternalInput").ap()

    scratch_kind = "ExternalOutput" if dbg else "Internal"

    def dscr(name, shape, dt=BF16):
        return nc.dram_tensor(name, list(shape), dt, kind=scratch_kind).ap()

    C.x_all = din("x_all", [S_all, D])
    C.pos_all = din("pos_all", [1, S_all], I32)
    C.g_pre = din("g_pre", [1, D])
    C.w_lat = din("w_lat", [D, 768])
    C.g_kv = din("g_kv", [512])
    C.w_k = din("w_k", [512, 2048])
    C.w_v = din("w_v", [512, 2048])
    C.ident = din("ident", [128, 128], BF16)
    C.ones = din("ones", [128, 128], BF16)
    C.ropec = din("ropec", [128, 2])
    C.kT_d = dscr("kT_d", [NH, 128, S_all])
    C.v_d = dscr("v_d", [NH, 128, S_all // 128, 128])
    C.krT_d = dscr("krT_d", [128, S_all])
    full = any(p in phases for p in ("Q", "A", "R"))
    if full:
        C.x_own = din("x_own", [1024, D])
        C.pos_own = din("pos_own", [1, 1024], I32)
        C.x_halo = din("x_halo", [128, D])
        C.mask16 = din("mask16", [128, 16], BF16)
        C.w_in = din("w_in", [D, 15424])
        C.g_qa = din("g_qa", [512])
        C.w_qn = din("w_qn", [512, 2048])
        C.w_qr = din("w_qr", [512, 1024])
        C.w_qrs = din("w_qrs", [512, 1024])
        C.conv_w = din("conv_w", [3, D])
        C.w_o_mla = din("w_o_mla", [D, D])
        C.w_o_conv = din("w_o_conv", [D, D])
        C.w_out = din("w_out", [D, D])
        C.g_post = din("g_post", [1, D])
        okind = "ExternalOutput"
        C.out = nc.dram_tensor("out", [1024, D], F32, kind=okind).ap()
        if dbg:
            C.dbg_attn = nc.dram_tensor("dbg_attn", [128, 16, 1024], BF16, kind=okind).ap()
            C.dbg_qn = nc.dram_tensor("dbg_qn", [128, 16, 1024], BF16, kind=okind).ap()
            C.dbg_qr = nc.dram_tensor("dbg_qr", [128, 8, 1024], BF16, kind=okind).ap()
            C.dbg_gated = nc.dram_tensor("dbg_gated", [128, 16, 1024], BF16, kind=okind).ap()
            C.dbg_gc = nc.dram_tensor("dbg_gc", [128, 16, 1024], BF16, kind=okind).ap()
            C.dbg_mg = nc.dram_tensor("dbg_mg", [128, 16, 1024], BF16, kind=okind).ap()

    C.sb = SB(nc, 206 * 1024)
    C.ps = [nc.alloc_psum_tensor("ps%d" % i, [128, 512], F32) for i in range(8)]
    C.ps_i = 0

    const_setup(C)
    if "K" in phases and C.stop >= 1:
        phase_K(C, n_kv_groups)
    if full:
        sb = C.sb
        C.attn = sb.alloc([128, 16, 1024], BF16)
        m0 = sb.mark()
        C.Qn = sb.alloc([128, 16, 1024], BF16)
        C.Qr = sb.alloc([128, 8, 1024], BF16)
        P.barrier()
        if "Q" in phases:
            phase_Q(C)
        if dbg and "Q" in phases:
            C.final_ops.append(P.dma("sp", lambda e: e.dma_start(out=C.dbg_qn, in_=C.Qn), reads=["Qn%d" % h for h in range(NH)], chan="dbg"))
            C.final_ops.append(P.dma("sp", lambda e: e.dma_start(out=C.dbg_qr, in_=C.Qr), reads=["Qr%d" % h for h in range(8)], chan="dbg"))
        P.barrier()
        if "A" in phases:
            phase_A(C)
        if dbg and "A" in phases:
            C.final_ops.append(P.dma("sp", lambda e: e.dma_start(out=C.dbg_attn, in_=C.attn),
                                     reads=["attn%d" % h for h in range(NH)], chan="dbg"))
        P.barrier()
        sb.release(m0)
        if "R" in phases:
            phase_R(C)

    P.join("sp", C.final_ops)
    P.emit(nc)
    return nc


def ps_next(C):
    i = C.ps_i
    C.ps_i = (i + 1) % 8
    return i


def const_setup(C):
    nc, P, sb = C.nc, C.P, C.sb
    C.final_ops = []
    C.ident_sb = sb.alloc([128, 128], BF16)
    C.ones_sb = sb.alloc([128, 128], BF16)
    C.ropec_sb = sb.alloc([128, 2], F32)
    C.eps_sb = sb.alloc([128, 1], F32)
    C.gb = sb.alloc([128, D], F32)
    P.dma("sp", lambda e: e.dma_start(out=C.ident_sb, in_=C.ident), writes=["ident"], chan="const", bulk=True)
    P.dma("sp", lambda e: e.dma_start(out=C.ones_sb, in_=C.ones), writes=["ones"], chan="const", bulk=True)
    P.dma("sp", lambda e: e.dma_start(out=C.ropec_sb, in_=C.ropec), writes=["ropec"], chan="const", bulk=True)
    P.dma("sp", lambda e: e.dma_start(out=C.gb, in_=C.g_pre.broadcast_to([128, D])), writes=["gb"], chan="const", bulk=True)
    P.op("dve", lambda e: e.memset(C.eps_sb, EPS), writes=["eps"])


def rope_tables(C, pos_src, n, Ct, St, tmp, tag):
    P = C.P
    pi_t, a, kf, r, m = tmp
    P.dma("sp", lambda e: e.dma_start(out=pi_t, in_=pos_src.broadcast_to([128, n])),
          writes=[tag + "pi"], chan=tag + "pi")
    P.op("dve", lambda e: e.tensor_copy(out=a, in_=pi_t), reads=[tag + "pi"], writes=[tag + "a"])
    P.op("dve", lambda e: e.tensor_scalar(out=a, in0=a, scalar1=C.ropec_sb[:, 0:1], scalar2=None, op0=ALU.mult),
         reads=[tag + "a", "ropec"], writes=[tag + "a"])
    P.op("dve", lambda e: e.tensor_scalar(out=kf, in0=a, scalar1=1.0 / TWO_PI, scalar2=None, op0=ALU.mult),
         reads=[tag + "a"], writes=[tag + "kf"])
    ki = pi_t
    P.op("dve", lambda e: e.tensor_copy(out=ki, in_=kf), reads=[tag + "kf"], writes=[tag + "pi"])
    P.op("dve", lambda e: e.tensor_copy(out=kf, in_=ki), reads=[tag + "pi"], writes=[tag + "kf"])
    C1 = 6.28125
    C2 = TWO_PI - C1
    P.op("dve", lambda e: e.scalar_tensor_tensor(out=r, in0=kf, scalar=-C1, in1=a, op0=ALU.mult, op1=ALU.add),
         reads=[tag + "kf", tag + "a"], writes=[tag + "r"])
    P.op("dve", lambda e: e.scalar_tensor_tensor(out=r, in0=kf, scalar=-C2, in1=r, op0=ALU.mult, op1=ALU.add),
         reads=[tag + "kf", tag + "r"], writes=[tag + "r"])

    def wrap(x):
        P.op("dve", lambda e: e.tensor_scalar(out=m, in0=x, scalar1=math.pi, scalar2=-TWO_PI, op0=ALU.is_gt, op1=ALU.mult),
             reads=[tag + "r"], writes=[tag + "m"])
        P.op("dve", lambda e: e.tensor_tensor(out=x, in0=x, in1=m, op=ALU.add),
             reads=[tag + "r", tag + "m"], writes=[tag + "r"])
        P.op("dve", lambda e: e.tensor_scalar(out=m, in0=x, scalar1=-math.pi, scalar2=TWO_PI, op0=ALU.is_lt, op1=ALU.mult),
             reads=[tag + "r"], writes=[tag + "m"])
        P.op("dve", lambda e: e.tensor_tensor(out=x, in0=x, in1=m, op=ALU.add),
             reads=[tag + "r", tag + "m"], writes=[tag + "r"])

    wrap(r)
    P.op("act", lambda e: e.activation(out=St, in_=r, func=AF.Sin, scale=C.ropec_sb[:, 1:2]),
         reads=[tag + "r", "ropec"], writes=[tag + "S"])
    P.op("dve", lambda e: e.tensor_scalar(out=r, in0=r, scalar1=math.pi / 2, scalar2=None, op0=ALU.add),
         reads=[tag + "r"], writes=[tag + "r"])
    wrap(r)
    P.op("act", lambda e: e.activation(out=Ct, in_=r, func=AF.Sin),
         reads=[tag + "r"], writes=[tag + "C"])


def front_end1(C, x_src, slot, bufs, xslot=None):
    P = C.P
    xb, junk, ss, sd, rstd, xs = bufs
    sl = "fe%d" % slot
    xr = "fe%dxb" % (slot if xslot is None else xslot)
    P.dma("sp", lambda e: e.dma_start(out=xb, in_=x_src), writes=[xr], chan=xr)
    P.op("act", lambda e: e.activation(out=xs, in_=xb, func=AF.Square, accum_out=ss),
         reads=[xr], writes=[sl + "xs", sl + "ss"])
    P.op("act", lambda e: e.activation(out=sd, in_=ss, func=AF.Sqrt, scale=1.0 / D, bias=C.eps_sb),
         reads=[sl + "ss", "eps"], writes=[sl + "sd"])
    P.op("dve", lambda e: e.reciprocal(out=rstd, in_=sd), reads=[sl + "sd"], writes=[sl + "rstd"])
    P.op("dve", lambda e: e.scalar_tensor_tensor(out=xs, in0=xb, scalar=rstd, in1=C.gb, op0=ALU.mult, op1=ALU.mult),
         reads=[xr, sl + "rstd", "gb"], writes=[sl + "xs"])


def front_end2(C, slot, hT_dst, hres, xs):
    P = C.P
    sl = "fe%d" % slot
    for q in range(4):
        b = ps_next(C)
        pb = C.ps[b][:, :].bitcast(BF16)
        for kk in range(4):
            kc = 4 * q + kk
            P.op("pe", lambda e, kc=kc, kk=kk, pb=pb: e.transpose(out=pb[:, kk * 128:(kk + 1) * 128],
                                                               in_=xs[:, kc * 128:(kc + 1) * 128], identity=C.ident_sb),
                 reads=[sl + "xs", "ident"], writes=["ps%d" % b], lhs=[sl + "xs"])
        src = pb[:, 0:512].rearrange("p (a b) -> p a b", b=128)
        dst = hT_dst[:, 4 * q:4 * q + 4, :]
        if q % 2 == 0:
            P.op("dve", lambda e, src=src, dst=dst: e.tensor_copy(out=dst, in_=src),
                 reads=["ps%d" % b], writes=[hres + "_%d" % q])
        else:
            P.op("act", lambda e, src=src, dst=dst: e.copy(out=dst, in_=src),
                 reads=["ps%d" % b], writes=[hres + "_%d" % q])


def front_end(C, x_src, slot, hT_dst, hres, bufs):
    front_end1(C, x_src, slot, bufs)
    front_end2(C, slot, hT_dst, hres, bufs[5])


def phase_K(C, n_groups):
    nc, P, sb = C.nc, C.P, C.sb
    mk = sb.mark()
    S_all = C.S_all
    wlat = sb.alloc([128, 16, 768], BF16)
    wk = sb.alloc([128, 4, 2048], BF16)
    wv = sb.alloc([128, 4, 2048], BF16)
    gkv = sb.alloc([128, 4], F32)
    for h2 in range(2):
        P.dma("pool", lambda e, h2=h2: e.dma_start(out=wlat[:, 8 * h2:8 * h2 + 8, :],
                                                 in_=C.w_lat[1024 * h2:1024 * (h2 + 1), :].rearrange("(kc p) n -> p kc n", p=128)),
              writes=["wlat%d" % h2], chan="wlat", bulk=True)
    P.dma("pool", lambda e: e.dma_start(out=wk, in_=C.w_k.rearrange("(kc p) n -> p kc n", p=128)), writes=["wk"], chan="wkv", bulk=True)
    P.dma("pool", lambda e: e.dma_start(out=wv, in_=C.w_v.rearrange("(kc p) n -> p kc n", p=128)), writes=["wv"], chan="wkv", bulk=True)
    P.dma("sp", lambda e: e.dma_start(out=gkv, in_=C.g_kv.rearrange("(c p) -> p c", p=128), allow_slow_non_contiguous=True), writes=["gkv"], chan="const", bulk=True)

    xb = [sb.alloc([128, D], F32) for _ in range(2)]
    ssb = [sb.alloc([128, 1], F32) for _ in range(4)]
    sdb = [sb.alloc([128, 1], F32) for _ in range(4)]
    rsb = [sb.alloc([128, 1], F32) for _ in range(4)]
    xs = [sb.alloc([128, D], BF16) for _ in range(4)]
    hT = [sb.alloc([128, 16, 512], BF16) for _ in range(2)]
    sq = sb.alloc([128, 4, 512], BF16)
    junk = None
    craw = sb.alloc([128, 4, 512], F32)
    ckvn = [sb.alloc([128, 4, 512], BF16) for _ in range(2)]
    sdk = sb.alloc([128, 512], F32)
    rk = sb.alloc([128, 512], F32)
    Ct = sb.alloc([128, 512], F32)
    St = sb.alloc([128, 512], F32)
    rtmp = [sb.alloc([128, 512], I32)] + [sb.alloc([128, 512], F32) for _ in range(4)]
    t1 = sb.alloc([128, 512], F32)
    t2 = sb.alloc([128, 512], F32)
    krt = [sb.alloc([128, 512], BF16) for _ in range(2)]
    kst = [sb.alloc([128, 8, 512], BF16) for _ in range(2)]
    vst = sb.alloc([128, 16, 4, 128], BF16)

    def fe1_blk(g, tbk):
        t0 = g * 512 + tbk * 128
        front_end1(C, C.x_all[t0:t0 + 128, :], 20 + tbk, (xb[tbk % 2], junk, ssb[tbk], sdb[tbk], rsb[tbk], xs[tbk]), xslot=30 + tbk % 2)

    def fe1(g):
        for tbk in range(4):
            fe1_blk(g, tbk)

    def fe2(g):
        hs = g % 2
        for tbk in range(4):
            front_end2(C, 20 + tbk, hT[hs][:, :, tbk * 128:(tbk + 1) * 128], "hT%d_%d" % (hs, tbk), xs[tbk])

    def latents(g):
        hs = g % 2
        t0 = g * 512
        rope_tables(C, C.pos_all[:, t0:t0 + 512], 512, Ct, St, rtmp, "rk")
        hres = lambda kc: ["hT%d_%d_%d" % (hs, tb, kc // 4) for tb in range(4)]
        for cb in range(6):
            b = ps_next(C)
            for kc in range(16):
                P.op("pe", lambda e, b=b, cb=cb, kc=kc: e.matmul(out=C.ps[b][:, :], lhsT=wlat[:, kc, cb * 128:(cb + 1) * 128],
                                                               rhs=hT[hs][:, kc, :], start=(kc == 0), stop=(kc == 15)),
                     reads=["wlat%d" % (kc // 8)] + hres(kc), writes=["ps%d" % b], lhs=["wlat%d" % (kc // 8)])
            if cb < 4:
                P.op("dve", lambda e, b=b, cb=cb: e.tensor_copy(out=craw[:, cb, :], in_=C.ps[b][:, :]),
                     reads=["ps%d" % b], writes=["craw%d" % cb])
                P.op("act", lambda e, cb=cb: e.activation(out=sq[:, cb, :], in_=craw[:, cb, :], func=AF.Square),
                     reads=["craw%d" % cb], writes=["sq%d" % cb])
            elif cb == 4:
                P.op("dve", lambda e, b=b: e.tensor_tensor(out=t1, in0=C.ps[b][:, :], in1=Ct, op=ALU.mult),
                     reads=["ps%d" % b, "rkC"], writes=["t1"])
            else:
                P.op("dve", lambda e, b=b: e.tensor_tensor(out=t2, in0=C.ps[b][:, :], in1=St, op=ALU.mult),
                     reads=["ps%d" % b, "rkS"], writes=["t2"])
                ks = g % 2
                P.op("pool", lambda e, ks=ks: e.tensor_tensor(out=krt[ks], in0=t1, in1=t2, op=ALU.add),
                     reads=["t1", "t2"], writes=["krt%d" % ks])
                C.final_ops.append(
                    P.dma("pool", lambda e, ks=ks, t0=t0: e.dma_start(out=C.krT_d[:, t0:t0 + 512], in_=krt[ks]),
                          reads=["krt%d" % ks], writes=["krT_d"], chan="krst%d" % ks))

    def ckv_norm(g):
        cs = g % 2
        b = ps_next(C)
        for cb in range(4):
            P.op("pe", lambda e, b=b, cb=cb: e.matmul(out=C.ps[b][:, :], lhsT=C.ones_sb, rhs=sq[:, cb, :],
                                                    start=(cb == 0), stop=(cb == 3)),
                 reads=["ones", "sq%d" % cb], writes=["ps%d" % b], lhs=["ones"])
        P.op("act", lambda e, b=b: e.activation(out=sdk, in_=C.ps[b][:, :], func=AF.Sqrt, scale=1.0 / 512, bias=C.eps_sb),
             reads=["ps%d" % b, "eps"], writes=["sdk"])
        P.op("dve", lambda e: e.reciprocal(out=rk, in_=sdk), reads=["sdk"], writes=["rk"])
        for cb in range(4):
            P.op("dve", lambda e, cb=cb, cs=cs: e.scalar_tensor_tensor(out=ckvn[cs][:, cb, :], in0=craw[:, cb, :],
                                                                     scalar=gkv[:, cb:cb + 1], in1=rk,
                                                                     op0=ALU.mult, op1=ALU.mult),
                 reads=["craw%d" % cb, "gkv", "rk"], writes=["ckvn%d_%d" % (cs, cb)])

    def k_proj(g):
        cs = g % 2
        t0 = g * 512
        ckres = ["ckvn%d_%d" % (cs, cb) for cb in range(4)]
        for h in range(NH):
            b = ps_next(C)
            for c4 in range(4):
                P.op("pe", lambda e, b=b, h=h, c4=c4: e.matmul(out=C.ps[b][:, :], lhsT=wk[:, c4, h * 128:(h + 1) * 128],
                                                             rhs=ckvn[cs][:, c4, :], start=(c4 == 0), stop=(c4 == 3)),
                     reads=["wk", ckres[c4]], writes=["ps%d" % b], lhs=["wk"])
            half = h // 8
            if h % 2 == 0:
                P.op("act", lambda e, b=b, h=h, half=half: e.copy(out=kst[half][:, h % 8, :], in_=C.ps[b][:, :]),
                     reads=["ps%d" % b], writes=["kst%d" % half])
            else:
                P.op("dve", lambda e, b=b, h=h, half=half: e.tensor_copy(out=kst[half][:, h % 8, :], in_=C.ps[b][:, :]),
                     reads=["ps%d" % b], writes=["kst%d" % half])
            if h % 8 == 7:
                C.final_ops.append(
                    P.dma("pool", lambda e, half=half, t0=t0: e.dma_start(
                        out=C.kT_d[8 * half:8 * half + 8, :, t0:t0 + 512].rearrange("h d t -> d h t"), in_=kst[half]),
                        reads=["kst%d" % half], writes=["kT_d"], chan="kst%d" % half))
            yield

    def v_proj(g):
        cs = g % 2
        t0 = g * 512
        ckres = ["ckvn%d_%d" % (cs, cb) for cb in range(4)]
        for tbk in range(4):
            for cg in range(4):
                b = ps_next(C)
                for c4 in range(4):
                    P.op("pe", lambda e, b=b, cg=cg, c4=c4, tbk=tbk: e.matmul(
                        out=C.ps[b][:, :], lhsT=ckvn[cs][:, c4, tbk * 128:(tbk + 1) * 128],
                        rhs=wv[:, c4, cg * 512:(cg + 1) * 512], start=(c4 == 0), stop=(c4 == 3)),
                        reads=["wv", ckres[c4]], writes=["ps%d" % b], lhs=[ckres[c4]])
                src = C.ps[b][:, :].rearrange("p (h d) -> p h d", d=128)
                dst = vst[:, 4 * cg:4 * cg + 4, tbk, :]
                if cg % 2 == 0:
                    P.op("act", lambda e, src=src, dst=dst: e.copy(out=dst, in_=src),
                         reads=["ps%d" % b], writes=["vst"])
                else:
                    P.op("dve", lambda e, src=src, dst=dst: e.tensor_copy(out=dst, in_=src),
                         reads=["ps%d" % b], writes=["vst"])
                yield
        kb0 = t0 // 128
        C.final_ops.append(
            P.dma("pool", lambda e, kb0=kb0: e.dma_start(
                out=C.v_d[:, :, kb0:kb0 + 4, :].rearrange("h p k d -> p h k d"), in_=vst),
                reads=["vst"], writes=["v_d"], chan="vst"))

    G = n_groups
    fe1(0)
    fe2(0)
    latents(0)
    ckv_norm(0)
    if G > 1:
        fe1(1)
    for g in range(G):
        if g + 1 < G:
            fe2(g + 1)
            latents(g + 1)
        cnt = 0

        def tick():
            nonlocal cnt
            cnt += 1

        if g + 2 < G:
            fe1(g + 2)
        for _ in k_proj(g):
            tick()
        if g + 1 < G:
            ckv_norm(g + 1)
        for _ in v_proj(g):
            tick()
    sb.release(mk)


def make_consts():
    import ml_dtypes
    inv = np.power(np.float32(10000.0), -np.arange(0, 64, 2, dtype=np.float32) / np.float32(64)).astype(np.float32)
    ropec = np.zeros((128, 2), np.float32)
    for p in range(128):
        ropec[p, 0] = inv[p % 32]
        ropec[p, 1] = -1.0 if (p % 64) < 32 else 1.0
    return dict(ident=np.eye(128, dtype=np.float32).astype(ml_dtypes.bfloat16),
                ones=np.ones((128, 128), np.float32).astype(ml_dtypes.bfloat16),
                ropec=ropec)


def phase_Q(C):
    P, sb = C.P, C.sb
    mk = sb.mark()
    wqa = sb.alloc([128, 16, 512], BF16)
    wqn = sb.alloc([128, 4, 2048], BF16)
    wqr = sb.alloc([128, 4, 1024], BF16)
    wqrs = sb.alloc([128, 4, 1024], BF16)
    gqa = sb.alloc([128, 4], F32)
    P.dma("pool", lambda e: e.dma_start(out=wqa, in_=C.w_in[:, 0:512].rearrange("(kc p) n -> p kc n", p=128)),
          writes=["wqa"], chan="wq", bulk=True)
    P.dma("pool", lambda e: e.dma_start(out=wqn, in_=C.w_qn.rearrange("(kc p) n -> p kc n", p=128)), writes=["wqn"], chan="wq", bulk=True)
    P.dma("pool", lambda e: e.dma_start(out=wqr, in_=C.w_qr.rearrange("(kc p) n -> p kc n", p=128)), writes=["wqr"], chan="wq", bulk=True)
    P.dma("pool", lambda e: e.dma_start(out=wqrs, in_=C.w_qrs.rearrange("(kc p) n -> p kc n", p=128)), writes=["wqrs"], chan="wq", bulk=True)
    P.dma("sp", lambda e: e.dma_start(out=gqa, in_=C.g_qa.rearrange("(c p) -> p c", p=128), allow_slow_non_contiguous=True),
          writes=["gqa"], chan="constQ", bulk=True)
    xb = sb.alloc([128, D], F32)
    ssb = sb.alloc([128, 1], F32)
    sdb = sb.alloc([128, 1], F32)
    rsb = sb.alloc([128, 1], F32)
    xs = sb.alloc([128, D], BF16)
    hTq = sb.alloc([128, 16, 512], BF16)
    qraw = sb.alloc([128, 4, 512], F32)
    sq = sb.alloc([128, 4, 512], BF16)
    junk = None
    qan = sb.alloc([128, 4, 512], BF16)
    sdq = sb.alloc([128, 512], F32)
    rq = sb.alloc([128, 512], F32)
    Ct = sb.alloc([128, 512], F32)
    St = sb.alloc([128, 512], F32)
    rtmp = [sb.alloc([128, 512], I32)] + [sb.alloc([128, 512], F32) for _ in range(4)]
    t1 = sb.alloc([128, 512], F32)
    t2 = sb.alloc([128, 512], F32)

    def q_half(hf):
        t0 = 512 * hf
        for tbk in range(4):
            front_end(C, C.x_own[t0 + tbk * 128:t0 + (tbk + 1) * 128, :], 7,
                      hTq[:, :, tbk * 128:(tbk + 1) * 128], "hq_%d" % tbk, (xb, junk, ssb, sdb, rsb, xs))
        rope_tables(C, C.pos_own[:, t0:t0 + 512], 512, Ct, St, rtmp, "rq")
        hres = lambda kc: ["hq_%d_%d" % (tb, kc // 4) for tb in range(4)]
        for cb in range(4):
            b = ps_next(C)
            for kc in range(16):
                P.op("pe", lambda e, b=b, cb=cb, kc=kc: e.matmul(out=C.ps[b][:, :], lhsT=wqa[:, kc, cb * 128:(cb + 1) * 128],
                                                               rhs=hTq[:, kc, :], start=(kc == 0), stop=(kc == 15)),
                     reads=["wqa"] + hres(kc), writes=["ps%d" % b], lhs=["wqa"])
            P.op("dve", lambda e, b=b, cb=cb: e.tensor_copy(out=qraw[:, cb, :], in_=C.ps[b][:, :]),
                 reads=["ps%d" % b], writes=["qraw%d" % cb])
            P.op("act", lambda e, cb=cb: e.activation(out=sq[:, cb, :], in_=qraw[:, cb, :], func=AF.Square),
                 reads=["qraw%d" % cb], writes=["qsq%d" % cb])
        b = ps_next(C)
        for cb in range(4):
            P.op("pe", lambda e, b=b, cb=cb: e.matmul(out=C.ps[b][:, :], lhsT=C.ones_sb, rhs=sq[:, cb, :],
                                                    start=(cb == 0), stop=(cb == 3)),
                 reads=["ones", "qsq%d" % cb], writes=["ps%d" % b], lhs=["ones"])
        P.op("act", lambda e, b=b: e.activation(out=sdq, in_=C.ps[b][:, :], func=AF.Sqrt, scale=1.0 / 512, bias=C.eps_sb),
             reads=["ps%d" % b, "eps"], writes=["sdq"])
        P.op("dve", lambda e: e.reciprocal(out=rq, in_=sdq), reads=["sdq"], writes=["rq"])
        for cb in range(4):
            P.op("dve", lambda e, cb=cb: e.scalar_tensor_tensor(out=qan[:, cb, :], in0=qraw[:, cb, :], scalar=gqa[:, cb:cb + 1],
                                                             in1=rq, op0=ALU.mult, op1=ALU.mult),
                 reads=["qraw%d" % cb, "gqa", "rq"], writes=["qan%d" % cb])
        qres = ["qan%d" % cb for cb in range(4)]
        for h in range(NH):
            b = ps_next(C)
            for c4 in range(4):
                P.op("pe", lambda e, b=b, h=h, c4=c4: e.matmul(out=C.ps[b][:, :], lhsT=wqn[:, c4, h * 128:(h + 1) * 128],
                                                             rhs=qan[:, c4, :], start=(c4 == 0), stop=(c4 == 3)),
                     reads=["wqn", qres[c4]], writes=["ps%d" % b], lhs=["wqn"])
            if h % 2 == 0:
                P.op("act", lambda e, b=b, h=h: e.copy(out=C.Qn[:, h, t0:t0 + 512], in_=C.ps[b][:, :]),
                     reads=["ps%d" % b], writes=["Qn%d" % h])
            else:
                P.op("dve", lambda e, b=b, h=h: e.tensor_copy(out=C.Qn[:, h, t0:t0 + 512], in_=C.ps[b][:, :]),
                     reads=["ps%d" % b], writes=["Qn%d" % h])
        for hp in range(8):
            b1 = ps_next(C)
            for c4 in range(4):
                P.op("pe", lambda e, b1=b1, hp=hp, c4=c4: e.matmul(out=C.ps[b1][:, :], lhsT=wqr[:, c4, hp * 128:(hp + 1) * 128],
                                                                 rhs=qan[:, c4, :], start=(c4 == 0), stop=(c4 == 3)),
                     reads=["wqr", qres[c4]], writes=["ps%d" % b1], lhs=["wqr"])
            P.op("dve", lambda e, b1=b1: e.tensor_tensor(out=t1, in0=C.ps[b1][:, :], in1=Ct, op=ALU.mult),
                 reads=["ps%d" % b1, "rqC"], writes=["qt1"])
            b2 = ps_next(C)
            for c4 in range(4):
                P.op("pe", lambda e, b2=b2, hp=hp, c4=c4: e.matmul(out=C.ps[b2][:, :], lhsT=wqrs[:, c4, hp * 128:(hp + 1) * 128],
                                                                 rhs=qan[:, c4, :], start=(c4 == 0), stop=(c4 == 3)),
                     reads=["wqrs", qres[c4]], writes=["ps%d" % b2], lhs=["wqrs"])
            P.op("dve", lambda e, b2=b2: e.tensor_tensor(out=t2, in0=C.ps[b2][:, :], in1=St, op=ALU.mult),
                 reads=["ps%d" % b2, "rqS"], writes=["qt2"])
            P.op("pool", lambda e, hp=hp: e.tensor_tensor(out=C.Qr[:, hp, t0:t0 + 512], in0=t1, in1=t2, op=ALU.add),
                 reads=["qt1", "qt2"], writes=["Qr%d" % hp])

    for hf in range(2):
        q_half(hf)
    sb.release(mk)


def phase_A(C):
    P, sb = C.P, C.sb
    mk = sb.mark()
    S_all = C.S_all
    nkb = S_all // 128
    nq = nkb // 16
    KrT = [sb.alloc([128, S_all], BF16) for _ in range(2)]
    kring = [sb.alloc([128, 2048], BF16) for _ in range(4)]
    vring = [sb.alloc([128, 16, 128], BF16) for _ in range(4)]
    NPT = 6
    Pt = [sb.alloc([128, 512], BF16) for _ in range(NPT)]
    negm = sb.alloc([128, 16], BF16)
    Of = sb.alloc([128, 1024], F32)
    rl = sb.alloc([128, 1024], F32)
    accS = sb.alloc([128, 1024], F32)
    onesf = sb.alloc([128, 128], F32)
    stores = list(C.final_ops)
    P.op("pool", lambda e: e.memset(onesf, 1.0), writes=["onesf"])
    P.op("pool", lambda e: e.memset(KrT[0][64:128, :], 0.0), writes=["KrT0z"])
    P.op("pool", lambda e: e.memset(KrT[1][0:64, :], 0.0), writes=["KrT1z"])
    P.dma("sp", lambda e: e.dma_start(out=negm, in_=C.mask16), writes=["negm"], chan="constA", bulk=True)
    P.dma("sp", lambda e: e.dma_start(out=KrT[0][0:64, :], in_=C.krT_d[0:64, :]), writes=["KrT0"], chan="constA", bulk=True, extra=stores)
    P.dma("sp", lambda e: e.dma_start(out=KrT[1][64:128, :], in_=C.krT_d[64:128, :]), writes=["KrT1"], chan="constA", bulk=True, extra=stores)

    def load_q(i):
        h, q = divmod(i, nq)
        s = i % 4
        P.dma("sp", lambda e: e.dma_start(out=kring[s], in_=C.kT_d[h, :, 2048 * q:2048 * (q + 1)]),
              writes=["kq%d" % s], chan="kq%d" % s, extra=stores)
        P.dma("sp", lambda e: e.dma_start(out=vring[s], in_=C.v_d[h, :, 16 * q:16 * (q + 1), :]),
              writes=["vq%d" % s], chan="vq%d" % s, extra=stores)

    nload = NH * nq
    for i in range(min(4, nload)):
        load_q(i)

    units = []
    for h in range(NH):
        for kb in range(nkb):
            for half in range(2):
                lo = max(16 * kb, 512 * half)
                hi = 512 * (half + 1)
                if lo < hi:
                    units.append((h, kb, half, lo, hi))
    last_kb = {0: min(nkb - 1, 31), 1: nkb - 1}
    sbank = [5, 6, 7]
    NSB = 3
    ACCB = [3, 4]
    pool_cnt = {}

    def emit_S(u, ui):
        h, kb, half, lo, hi = u
        n = hi - lo
        b = sbank[ui % NSB]
        s = (h * nq + kb // 16) % 4
        kk = (kb % 16) * 128
        par = h % 2
        P.op("pe", lambda e: e.matmul(out=C.ps[b][:, 0:n], lhsT=kring[s][:, kk:kk + 128], rhs=C.Qn[:, h, lo:hi],
                                      start=True, stop=False),
             reads=["kq%d" % s, "Qn%d" % h], writes=["ps%d" % b], lhs=["kq%d" % s])
        P.op("pe", lambda e: e.matmul(out=C.ps[b][:, 0:n], lhsT=KrT[par][:, kb * 128:(kb + 1) * 128],
                                      rhs=C.Qr[:, h // 2, lo:hi], start=False, stop=True),
             reads=["KrT%d" % par, "KrT%dz" % par, "Qr%d" % (h // 2)], writes=["ps%d" % b], lhs=["KrT%d" % par, "KrT%dz" % par])

    def emit_PV(u, ui):
        h, kb, half, lo, hi = u
        n = hi - lo
        b = sbank[ui % NSB]
        pt = Pt[ui % NPT]
        pres = "Pt%d" % (ui % NPT)
        s = (h * nq + kb // 16) % 4
        hp = h % 2
        P.op("act", lambda e: e.activation(out=pt[:, 0:n], in_=C.ps[b][:, 0:n], func=AF.Exp, scale=SCALE),
             reads=["ps%d" % b], writes=[pres])
        if lo == 16 * kb:
            P.op("pool", lambda e: e.tensor_tensor(out=pt[:, 0:16], in0=pt[:, 0:16], in1=negm, op=ALU.mult),
                 reads=[pres, "negm"], writes=[pres])
        c0 = lo - 512 * half
        first = (kb == 0)
        last = (kb == last_kb[half])
        ob, lb = half, 2
        P.op("pe", lambda e: e.matmul(out=C.ps[ob][:, c0:c0 + n], lhsT=vring[s][:, kb % 16, :], rhs=pt[:, 0:n],
                                      start=first, stop=last),
             reads=["vq%d" % s, pres], writes=["ps%d" % ob], lhs=["vq%d" % s])
        ab = ACCB[half]
        if first:
            P.op("dve", lambda e: e.tensor_copy(out=C.ps[ab][:, c0:c0 + n], in_=pt[:, 0:n]),
                 reads=[pres], writes=["ps%d" % ab])
        else:
            P.op("dve", lambda e: e.tensor_tensor(out=C.ps[ab][:, c0:c0 + n], in0=C.ps[ab][:, c0:c0 + n], in1=pt[:, 0:n], op=ALU.add),
                 reads=[pres, "ps%d" % ab], writes=["ps%d" % ab])
        if last:
            hs_ = slice(512 * half, 512 * (half + 1))
            P.op("dve", lambda e: e.tensor_copy(out=accS[:, hs_], in_=C.ps[ab][:, :]), reads=["ps%d" % ab], writes=["accS%d" % half])
            P.op("pe", lambda e: e.matmul(out=C.ps[lb][:, :], lhsT=onesf, rhs=accS[:, hs_], start=True, stop=True),
                 reads=["onesf", "accS%d" % half], writes=["ps%d" % lb])
            P.op("dve", lambda e: e.reciprocal(out=rl[:, hs_], in_=C.ps[lb][:, :]), reads=["ps%d" % lb], writes=["rl%d" % half])
            P.op("dve", lambda e: e.tensor_tensor(out=C.attn[:, h, hs_], in0=C.ps[ob][:, :], in1=rl[:, hs_], op=ALU.mult),
                 reads=["ps%d" % ob, "rl%d" % half], writes=["attn%d" % h])
        if kb % 16 == 15 and half == 1:
            i = h * nq + kb // 16
            if i + 4 < nload:
                load_q(i + 4)

    LOOK = 2
    nu = len(units)
    for ui in range(min(LOOK, nu)):
        emit_S(units[ui], ui)
    for ui in range(nu):
        if ui + LOOK < nu:
            emit_S(units[ui + LOOK], ui + LOOK)
        emit_PV(units[ui], ui)
    sb.release(mk)


OFF_Z, OFF_CIN, OFF_BG, OFF_CG, OFF_ZC, OFF_GM, OFF_GC = 1088, 3136, 5184, 7232, 9280, 11328, 13376


def phase_R(C):
    P, sb = C.P, C.sb
    mg = sb.alloc([128, 16, 1024], BF16)
    mR = sb.mark()
    hT = sb.alloc([128, 16, 1024], BF16)
    hTh = sb.alloc([128, 16, 128], BF16)
    gc = sb.alloc([128, 16, 1024], BF16)
    wr = [sb.alloc([128, 16, 256], BF16) for _ in range(4)]
    convw = sb.alloc([128, 3, 16], F32)
    for k in range(3):
        P.dma("sp", lambda e, k=k: e.dma_start(out=convw[:, k, :], in_=C.conv_w[k].rearrange("(j p) -> p j", p=128),
                                             allow_slow_non_contiguous=True),
              writes=["convw"], chan="constR", bulk=True)
    mk = sb.mark()
    xb = sb.alloc([128, D], F32)
    junk = None
    ssb = sb.alloc([128, 1], F32)
    sdb = sb.alloc([128, 1], F32)
    rsb = sb.alloc([128, 1], F32)
    xs = sb.alloc([128, D], BF16)
    for blk in range(8):
        front_end(C, C.x_own[blk * 128:(blk + 1) * 128, :], 8, hT[:, :, blk * 128:(blk + 1) * 128], "hr_%d" % blk,
                  (xb, junk, ssb, sdb, rsb, xs))
    front_end(C, C.x_halo, 8, hTh[:, :, :], "hr_8", (xb, junk, ssb, sdb, rsb, xs))
    hres = lambda kc, half: ["hr_%d_%d" % (4 * half + tb, kc // 4) for tb in range(4)]
    hhres = lambda kc: ["hr_8_%d" % (kc // 4)]
    sb.release(mk)

    tiles = []
    for jp in range(8):
        tiles.append(C.w_in[:, OFF_Z + 256 * jp:OFF_Z + 256 * (jp + 1)])
    for jp in range(8):
        for off in (OFF_CIN, OFF_CG, OFF_BG, OFF_ZC):
            tiles.append(C.w_in[:, off + 256 * jp:off + 256 * (jp + 1)])
    for jp in range(8):
        tiles.append(C.w_o_mla[:, 256 * jp:256 * (jp + 1)])
        tiles.append(C.w_in[:, OFF_GM + 256 * jp:OFF_GM + 256 * (jp + 1)])
        tiles.append(C.w_in[:, OFF_GC + 256 * jp:OFF_GC + 256 * (jp + 1)])
        tiles.append(C.w_o_conv[:, 256 * jp:256 * (jp + 1)])
    st = {"next": 0}

    def issue(n=1):
        for _ in range(n):
            i = st["next"]
            if i >= len(tiles):
                return
            st["next"] = i + 1
            s_ = i % 4
            src = tiles[i].rearrange("(kc p) n -> p kc n", p=128)
            P.dma("pool", lambda e, s_=s_, src=src: e.dma_start(out=wr[s_], in_=src), writes=["wr%d" % s_], chan="wr%d" % s_)

    ti = {"i": 0}

    def take():
        i = ti["i"]
        ti["i"] = i + 1
        while st["next"] <= min(i + 3, len(tiles) - 1):
            issue(1)
        return wr[i % 4], "wr%d" % (i % 4)

    def mm(w, wres, sub, rhs_fn, rres_fn, n):
        b = ps_next(C)
        for kc in range(16):
            P.op("pe", lambda e, b=b, kc=kc: e.matmul(out=C.ps[b][:, 0:n], lhsT=w[:, kc, sub * 128:(sub + 1) * 128],
                                                    rhs=rhs_fn(kc), start=(kc == 0), stop=(kc == 15)),
                 reads=[wres] + rres_fn(kc), writes=["ps%d" % b], lhs=[wres])
        return b

    tA = sb.alloc([128, 2, 1024], F32)
    tB = sb.alloc([128, 2, 1024], F32)
    U = sb.alloc([128, 2, 64 * 18], F32)
    cinh = sb.alloc([128, 2, 128], F32)
    halves = [slice(0, 512), slice(512, 1024)]

    def own_mm(w, wres, sub, half):
        hsl = halves[half]
        return mm(w, wres, sub, lambda kc: hT[:, kc, hsl], lambda kc: hres(kc, half), 512)

    def r1(jp):
        w, wres = take()
        for sub in range(2):
            j = 2 * jp + sub
            for half in range(2):
                hsl = halves[half]
                b = own_mm(w, wres, sub, half)
                P.op("act", lambda e, b=b, hsl=hsl, sub=sub: e.activation(out=tA[:, sub, hsl], in_=C.ps[b][:, :], func=AF.Silu),
                     reads=["ps%d" % b], writes=["tA%d_%d" % (sub, half)])
                P.op("dve", lambda e, j=j, hsl=hsl, sub=sub: e.tensor_tensor(out=C.attn[:, j, hsl], in0=C.attn[:, j, hsl],
                                                                         in1=tA[:, sub, hsl], op=ALU.mult),
                     reads=["tA%d_%d" % (sub, half), "attn%d" % j], writes=["attn%d" % j])

    for jp in range(8):
        r1(jp)

    def Uv(sub):
        return U[:, sub, :].rearrange("p (m j) -> p m j", j=18)

    def r2(jp):
        w, wres = take()
        for sub in range(2):
            for half in range(2):
                hsl = halves[half]
                b = own_mm(w, wres, sub, half)
                P.op("act", lambda e, b=b, hsl=hsl, sub=sub: e.copy(out=tA[:, sub, hsl], in_=C.ps[b][:, :]),
                     reads=["ps%d" % b], writes=["tA%d_%d" % (sub, half)])
            b = mm(w, wres, sub, lambda kc: hTh[:, kc, :], hhres, 128)
            P.op("act", lambda e, b=b, sub=sub: e.copy(out=cinh[:, sub, :], in_=C.ps[b][:, 0:128]),
                 reads=["ps%d" % b], writes=["cinh%d" % sub])
        w, wres = take()
        for sub in range(2):
            j = 2 * jp + sub
            for half in range(2):
                hsl = halves[half]
                b = own_mm(w, wres, sub, half)
                P.op("dve", lambda e, b=b, half=half, hsl=hsl, sub=sub: e.tensor_tensor(
                    out=Uv(sub)[:, 32 * half:32 * (half + 1), 2:18], in0=C.ps[b][:, :].rearrange("p (m j) -> p m j", j=16),
                    in1=tA[:, sub, hsl].rearrange("p (m j) -> p m j", j=16), op=ALU.mult),
                    reads=["ps%d" % b, "tA%d_%d" % (sub, half)], writes=["Uo%d_%d" % (sub, half)])
            b = mm(w, wres, sub, lambda kc: hTh[:, kc, :], hhres, 128)
            P.op("dve", lambda e, b=b, sub=sub: e.tensor_tensor(out=Uv(sub)[:, :, 0:2],
                                                             in0=C.ps[b][:, 0:128].rearrange("p (m j) -> p m j", j=2),
                                                             in1=cinh[:, sub, :].rearrange("p (m j) -> p m j", j=2), op=ALU.mult),
                 reads=["ps%d" % b, "cinh%d" % sub], writes=["Uh%d" % sub])
            tB3 = tB[:, sub, :].rearrange("p (m j) -> p m j", j=16)
            ures = ["Uo%d_0" % sub, "Uo%d_1" % sub, "Uh%d" % sub]
            tres = "tB%d" % sub
            P.op("dve", lambda e, j=j, sub=sub, tB3=tB3: e.tensor_scalar(out=tB3, in0=Uv(sub)[:, :, 0:16], scalar1=convw[:, 0, j:j + 1],
                                                                     scalar2=None, op0=ALU.mult),
                 reads=ures + ["convw"], writes=[tres])
            P.op("dve", lambda e, j=j, sub=sub, tB3=tB3: e.scalar_tensor_tensor(out=tB3, in0=Uv(sub)[:, :, 1:17], scalar=convw[:, 1, j:j + 1],
                                                                            in1=tB3, op0=ALU.mult, op1=ALU.add),
                 reads=ures + ["convw", tres], writes=[tres])
            P.op("dve", lambda e, j=j, sub=sub, tB3=tB3: e.scalar_tensor_tensor(out=tB3, in0=Uv(sub)[:, :, 2:18], scalar=convw[:, 2, j:j + 1],
                                                                            in1=tB3, op0=ALU.mult, op1=ALU.add),
                 reads=ures + ["convw", tres], writes=[tres])
        w, wres = take()
        for sub in range(2):
            for half in range(2):
                hsl = halves[half]
                b = own_mm(w, wres, sub, half)
                P.op("dve", lambda e, b=b, hsl=hsl, sub=sub: e.tensor_tensor(out=tB[:, sub, hsl], in0=C.ps[b][:, :], in1=tB[:, sub, hsl], op=ALU.mult),
                     reads=["ps%d" % b, "tB%d" % sub], writes=["tB%d" % sub])
        w, wres = take()
        for sub in range(2):
            j = 2 * jp + sub
            for half in range(2):
                hsl = halves[half]
                b = own_mm(w, wres, sub, half)
                P.op("act", lambda e, b=b, hsl=hsl, sub=sub: e.activation(out=tA[:, sub, hsl], in_=C.ps[b][:, :], func=AF.Silu),
                     reads=["ps%d" % b], writes=["tA%d_%d" % (sub, half)])
                P.op("pool", lambda e, j=j, hsl=hsl, sub=sub: e.tensor_tensor(out=gc[:, j, hsl], in0=tB[:, sub, hsl], in1=tA[:, sub, hsl], op=ALU.mult),
                     reads=["tB%d" % sub, "tA%d_%d" % (sub, half)], writes=["gc%d" % j])

    for jp in range(8):
        r2(jp)

    allattn = ["attn%d" % j for j in range(16)]
    allgc = ["gc%d" % j for j in range(16)]

    def r3(jp):
        w, wres = take()
        for sub in range(2):
            for half in range(2):
                hsl = halves[half]
                b = mm(w, wres, sub, lambda kc, hsl=hsl: C.attn[:, kc, hsl], lambda kc: ["attn%d" % kc], 512)
                P.op("act", lambda e, b=b, hsl=hsl, sub=sub: e.copy(out=tB[:, sub, hsl], in_=C.ps[b][:, :]),
                     reads=["ps%d" % b], writes=["tB%d" % sub])
        w, wres = take()
        for sub in range(2):
            for half in range(2):
                hsl = halves[half]
                b = own_mm(w, wres, sub, half)
                P.op("act", lambda e, b=b, hsl=hsl, sub=sub: e.activation(out=tA[:, sub, hsl], in_=C.ps[b][:, :], func=AF.Sigmoid),
                     reads=["ps%d" % b], writes=["tA%d_%d" % (sub, half)])
                P.op("dve", lambda e, hsl=hsl, sub=sub: e.tensor_tensor(out=tB[:, sub, hsl], in0=tB[:, sub, hsl], in1=tA[:, sub, hsl], op=ALU.mult),
                     reads=["tB%d" % sub, "tA%d_%d" % (sub, half)], writes=["tB%d" % sub])
        w, wres = take()
        for sub in range(2):
            for half in range(2):
                hsl = halves[half]
                b = own_mm(w, wres, sub, half)
                P.op("act", lambda e, b=b, hsl=hsl, sub=sub: e.activation(out=tA[:, sub, hsl], in_=C.ps[b][:, :], func=AF.Sigmoid),
                     reads=["ps%d" % b], writes=["tA%d_%d" % (sub, half)])
        w, wres = take()
        for sub in range(2):
            j = 2 * jp + sub
            for half in range(2):
                hsl = halves[half]
                b = mm(w, wres, sub, lambda kc, hsl=hsl: gc[:, kc, hsl], lambda kc: ["gc%d" % kc], 512)
                P.op("dve", lambda e, b=b, hsl=hsl, sub=sub: e.tensor_tensor(out=tA[:, sub, hsl], in0=C.ps[b][:, :], in1=tA[:, sub, hsl], op=ALU.mult),
                     reads=["ps%d" % b, "tA%d_%d" % (sub, half)], writes=["tA%d_%d" % (sub, half)])
                P.op("pool", lambda e, j=j, hsl=hsl, sub=sub: e.tensor_tensor(out=mg[:, j, hsl], in0=tB[:, sub, hsl], in1=tA[:, sub, hsl], op=ALU.add),
                     reads=["tB%d" % sub, "tA%d_%d" % (sub, half)], writes=["mg%d" % j])

    for jp in range(8):
        r3(jp)

    if hasattr(C, "dbg_mg"):
        C.final_ops.append(P.dma("sp", lambda e: e.dma_start(out=C.dbg_gated, in_=C.attn), reads=allattn, chan="dbg"))
        C.final_ops.append(P.dma("sp", lambda e: e.dma_start(out=C.dbg_gc, in_=gc), reads=allgc, chan="dbg"))
        C.final_ops.append(P.dma("sp", lambda e: e.dma_start(out=C.dbg_mg, in_=mg), reads=["mg%d" % j for j in range(16)], chan="dbg"))
    P.barrier()
    sb.release(mR)
    wout = sb.alloc([128, 16, 2048], BF16)
    gpost = sb.alloc([128, D], F32)
    xr = [sb.alloc([128, D], F32) for _ in range(2)]
    ot = [sb.alloc([128, D], F32) for _ in range(2)]
    ssq = sb.alloc([128, 4], F32)
    sst = sb.alloc([128, 1], F32)
    sdo = sb.alloc([128, 1], F32)
    rso = sb.alloc([128, 1], F32)
    junk4 = sb.alloc([128, 512], BF16)
    for cg in range(4):
        P.dma("pool", lambda e, cg=cg: e.dma_start(out=wout[:, :, 512 * cg:512 * (cg + 1)],
                                                 in_=C.w_out[:, 512 * cg:512 * (cg + 1)].rearrange("(kc p) n -> p kc n", p=128)),
              writes=["wout%d" % cg], chan="wout%d" % cg)
    P.dma("sp", lambda e: e.dma_start(out=gpost, in_=C.g_post.broadcast_to([128, D])), writes=["gpost"], chan="constR4", bulk=True)
    allmg = ["mg%d" % j for j in range(16)]

    def r4(blk):
        s_ = blk % 2
        tsl = slice(128 * blk, 128 * (blk + 1))
        P.dma("sp", lambda e: e.dma_start(out=xr[s_], in_=C.x_own[tsl, :]), writes=["xr%d" % s_], chan="xr%d" % s_)
        banks = []
        for cg in range(4):
            b = ps_next(C)
            banks.append(b)
            for kc in range(16):
                P.op("pe", lambda e, b=b, kc=kc, cg=cg: e.matmul(out=C.ps[b][:, :], lhsT=mg[:, kc, tsl],
                                                               rhs=wout[:, kc, 512 * cg:512 * (cg + 1)],
                                                               start=(kc == 0), stop=(kc == 15)),
                     reads=["mg%d" % kc, "wout%d" % cg], writes=["ps%d" % b], lhs=["mg%d" % kc])
            P.op("act", lambda e, b=b, cg=cg: e.activation(out=junk4, in_=C.ps[b][:, :], func=AF.Square, accum_out=ssq[:, cg:cg + 1]),
                 reads=["ps%d" % b], writes=["junk4", "ssq%d" % cg])
        P.op("dve", lambda e: e.tensor_reduce(out=sst, in_=ssq, axis=mybir.AxisListType.X, op=ALU.add),
             reads=["ssq%d" % cg for cg in range(4)], writes=["sst"])
        P.op("act", lambda e: e.activation(out=sdo, in_=sst, func=AF.Sqrt, scale=1.0 / D, bias=C.eps_sb),
             reads=["sst", "eps"], writes=["sdo"])
        P.op("dve", lambda e: e.reciprocal(out=rso, in_=sdo), reads=["sdo"], writes=["rso"])
        for cg in range(4):
            b = banks[cg]
            csl = slice(512 * cg, 512 * (cg + 1))
            P.op("dve", lambda e, b=b, csl=csl: e.scalar_tensor_tensor(out=ot[s_][:, csl], in0=C.ps[b][:, :], scalar=rso,
                                                                     in1=gpost[:, csl], op0=ALU.mult, op1=ALU.mult),
                 reads=["ps%d" % b, "rso", "gpost"], writes=["ot%d_%d" % (s_, cg)])
            P.op("pool", lambda e, csl=csl: e.tensor_tensor(out=ot[s_][:, csl], in0=ot[s_][:, csl], in1=xr[s_][:, csl], op=ALU.add),
                 reads=["ot%d_%d" % (s_, cg), "xr%d" % s_], writes=["ot%d_%d" % (s_, cg)])
        C.final_ops.append(
            P.dma("sp", lambda e: e.dma_start(out=C.out[tsl, :], in_=ot[s_]),
                  reads=["ot%d_%d" % (s_, cg) for cg in range(4)], writes=["out"], chan="ot%d" % s_))

    for blk in range(8):
        r4(blk)


def _own_rows(c):
    return np.concatenate([np.arange(128 * m + 16 * c, 128 * m + 16 * c + 16) for m in range(64)])


def prep(x, positions, pre_norm_g, w_in, q_a_norm_g, w_q_b, kv_a_norm_g, w_kv_b,
         conv_w, w_o_mla, w_o_conv, w_out, post_norm_g, cores=range(NCORES)):
    import ml_dtypes
    x2 = np.ascontiguousarray(np.asarray(x, dtype=np.float32).reshape(S, D))
    pos = np.ascontiguousarray(np.asarray(positions, dtype=np.int32).reshape(1, S))
    w_in = np.asarray(w_in, dtype=np.float32)
    kr = w_in[:, 1024:1088]
    krs = np.concatenate([kr[:, 32:], kr[:, :32]], axis=1)
    w_lat = np.ascontiguousarray(np.concatenate([w_in[:, 512:1024], kr, kr, krs, krs], axis=1))
    wkv = np.asarray(w_kv_b, dtype=np.float32).reshape(512, NH, 256)
    w_k = np.ascontiguousarray(wkv[:, :, :128].reshape(512, 2048))
    w_v = np.ascontiguousarray(wkv[:, :, 128:].reshape(512, 2048))
    wq = np.asarray(w_q_b, dtype=np.float32).reshape(512, NH, 192)
    w_qn = np.ascontiguousarray(wq[:, :, :128].reshape(512, 2048))
    qr = wq[:, :, 128:]
    w_qr = np.ascontiguousarray(qr.reshape(512, 1024))
    w_qrs = np.ascontiguousarray(np.concatenate([qr[:, :, 32:], qr[:, :, :32]], axis=2).reshape(512, 1024))
    consts = make_consts()
    shared = dict(x_all=x2, pos_all=pos, g_pre=np.asarray(pre_norm_g, np.float32).reshape(1, D), w_lat=w_lat,
                  g_kv=np.asarray(kv_a_norm_g, np.float32), w_k=w_k, w_v=w_v, w_in=np.ascontiguousarray(w_in),
                  g_qa=np.asarray(q_a_norm_g, np.float32), w_qn=w_qn, w_qr=w_qr, w_qrs=w_qrs,
                  conv_w=np.ascontiguousarray(np.asarray(conv_w, np.float32)),
                  w_o_mla=np.ascontiguousarray(np.asarray(w_o_mla, np.float32)),
                  w_o_conv=np.ascontiguousarray(np.asarray(w_o_conv, np.float32)),
                  w_out=np.ascontiguousarray(np.asarray(w_out, np.float32)),
                  g_post=np.asarray(post_norm_g, np.float32).reshape(1, D), **consts)
    in_maps = []
    rows_all = []
    for c in cores:
        rows = _own_rows(c)
        rows_all.append(rows)
        x_halo = np.zeros((128, D), np.float32)
        for m in range(64):
            for t in range(2):
                g = 128 * m + 16 * c - 2 + t
                if g >= 0:
                    x_halo[2 * m + t] = x2[g]
        kk = np.arange(128)[:, None]
        jj = np.arange(16)[None, :]
        mask16 = (kk <= 16 * c + jj).astype(np.float32).astype(ml_dtypes.bfloat16)
        im = dict(shared)
        im.update(x_own=np.ascontiguousarray(x2[rows]), pos_own=np.ascontiguousarray(pos[:, rows]),
                  x_halo=x_halo, mask16=mask16)
        in_maps.append(im)
    return in_maps, rows_all


def kernel(x, positions, pre_norm_g, w_in, q_a_norm_g, w_q_b, kv_a_norm_g, w_kv_b,
           conv_w, w_o_mla, w_o_conv, w_out, post_norm_g):
    in_maps, rows_all = prep(x, positions, pre_norm_g, w_in, q_a_norm_g, w_q_b, kv_a_norm_g, w_kv_b,
                             conv_w, w_o_mla, w_o_conv, w_out, post_norm_g)
    nc = build()
    res = run_bass_kernel_spmd(nc, in_maps, core_ids=list(range(NCORES)))
    out = np.zeros((S, D), np.float32)
    for c in range(NCORES):
        out[rows_all[c]] = np.asarray(res.results[c]["out"], dtype=np.float32)
    return out.reshape(1, S, D)
```

```python
import contextlib
import math

import numpy as np
import concourse.bass as bass
import concourse.mybir as mybir
from concourse.bass_utils import run_bass_kernel_spmd

F32 = mybir.dt.float32
BF16 = mybir.dt.bfloat16
I32 = mybir.dt.int32
AF = mybir.ActivationFunctionType
ALU = mybir.AluOpType

D = 2048
S = 8192
NH = 16
EPS = 1e-6
NCORES = 8
SCALE = 1.0 / math.sqrt(192.0)
TWO_PI = 2.0 * math.pi
INLINE_WAITS = True
import os
FLAGS = os.environ.get("KFLAGS", "").split(",")


class _Op:
    __slots__ = ("eng", "fn", "deps", "kind", "chan", "sig", "needed", "lhs", "depres")


class Prog:
    ENGS = ("pe", "act", "dve", "pool", "sp")

    def __init__(self):
        self.ops = []
        self.res = {}
        self.chan_count = {}
        self.bulk = set()

    def _add(self, eng, fn, reads, writes, kind, chan=None, extra=(), lhs=None):
        op = _Op()
        op.eng, op.fn, op.kind, op.chan = eng, fn, kind, chan
        op.needed, op.sig = False, None
        op.lhs = None if lhs is None else set(lhs)
        deps = {}
        depres = {}

        def put(d, k, r):
            if d not in deps:
                deps[d] = k
                depres[d] = {r}
            else:
                depres[d].add(r)

        for r in reads:
            st = self.res.setdefault(r, [None, []])
            if st[0] is not None:
                put(st[0], "raw", r)
            if r.startswith("ps"):
                for rd in st[1]:
                    if rd.eng != eng:
                        put(rd, "raw", r)
        for w in writes:
            st = self.res.setdefault(w, [None, []])
            if st[0] is not None:
                put(st[0], "waw", w)
            for rd in st[1]:
                put(rd, "war", w)
        op.depres = depres
        for r in reads:
            self.res[r][1].append(op)
        for w in writes:
            self.res[w] = [op, []]
        final = []
        for d, k in deps.items():
            if d is op:
                continue
            if d.eng == eng and d.kind == "c" and kind == "c":
                if eng == "pe" or k == "war":
                    continue
            if kind == "d" and d.kind == "d" and d.chan == chan and chan in self.bulk:
                continue
            final.append(d)
        for d in extra:
            if d not in final:
                final.append(d)
        op.deps = final
        for d in final:
            d.needed = True
        if kind == "d":
            n = self.chan_count.get(chan, 0) + 1
            self.chan_count[chan] = n
            op.sig = (("chan", chan), 16 * n)
        self.ops.append(op)
        return op

    def op(self, eng, fn, reads=(), writes=(), extra=(), lhs=None):
        return self._add(eng, fn, list(reads), list(writes), "c", extra=extra, lhs=lhs)

    def dma(self, eng, fn, reads=(), writes=(), chan=None, bulk=False, extra=()):
        assert chan is not None
        if bulk:
            self.bulk.add(chan)
        else:
            assert chan not in self.bulk
        return self._add(eng, fn, list(reads), list(writes), "d", chan, extra=extra)

    def barrier(self):
        last = {}
        lastd = {}
        for o in self.ops:
            if o.fn is None:
                continue
            if o.kind == "d":
                lastd[o.chan] = o
            else:
                last[o.eng] = o
        for e in self.ENGS:
            deps = list(last.values()) + list(lastd.values())
            self.join(e, deps)

    def join(self, eng, ops):
        op = _Op()
        op.eng, op.fn, op.kind, op.chan, op.needed, op.sig = eng, None, "c", None, False, None
        op.lhs, op.depres = None, {}
        op.deps = list(ops)
        for d in ops:
            d.needed = True
        self.ops.append(op)
        return op

    def emit(self, nc):
        cnt = {e: 0 for e in self.ENGS}
        for op in self.ops:
            if op.kind == "c" and op.needed:
                cnt[op.eng] += 1
                op.sig = (("eng", op.eng), cnt[op.eng])
            if op.kind == "d" and op.chan in self.bulk:
                op.sig = (("chan", op.chan), 16 * self.chan_count[op.chan])
        with contextlib.ExitStack() as st:
            sems = {}
            for e in self.ENGS:
                sems[("eng", e)] = st.enter_context(nc.semaphore("s_" + e))
            for c in self.chan_count:
                sems[("chan", c)] = st.enter_context(nc.semaphore("c_" + str(c)))
            block = st.enter_context(nc.Block())

            def body(ename):
                def f(eng):
                    waited = {}
                    for op in self.ops:
                        if op.eng != ename:
                            continue
                        inline = []
                        for d in op.deps:
                            key, val = d.sig
                            if waited.get(key, 0) < val:
                                waited[key] = val
                                rs = op.depres.get(d)
                                if (ename == "pe" and INLINE_WAITS and op.lhs is not None and op.fn is not None
                                        and rs is not None and not (rs & op.lhs)):
                                    inline.append((key, val))
                                else:
                                    eng.wait_ge(sems[key], val)
                        for key, val in inline[:-1]:
                            eng.wait_ge(sems[key], val)
                        if op.fn is None:
                            continue
                        inst = op.fn(eng)
                        if inline:
                            inst._wait_ge(sems[inline[-1][0]], inline[-1][1])
                        if op.kind == "d":
                            inst.then_inc(sems[op.sig[0]], 16)
                        elif op.needed:
                            inst.then_inc(sems[op.sig[0]], 1)
                return f

            block.tensor(body("pe"))
            block.scalar(body("act"))
            block.vector(body("dve"))
            block.gpsimd(body("pool"))
            block.sync(body("sp"))


class SB:
    def __init__(self, nc, nbytes):
        self.t = nc.alloc_sbuf_tensor("sb", [128, nbytes // 2], BF16)
        self.off = 0
        self.cap = nbytes

    def alloc(self, shape, dtype):
        assert shape[0] == 128
        n = int(np.prod(shape[1:]))
        size = 4 if dtype in (F32, I32) else 2
        nb = (n * size + 63) // 64 * 64
        assert self.off + nb <= self.cap, ("SBUF overflow", self.off, nb, self.cap)
        ap = self.t[:, self.off // 2:(self.off + n * size) // 2]
        self.off += nb
        if dtype != BF16:
            ap = ap.bitcast(dtype)
        if len(shape) == 3:
            ap = ap.rearrange("p (a b) -> p a b", b=shape[2])
        elif len(shape) == 4:
            ap = ap.rearrange("p (a b c) -> p a b c", b=shape[2], c=shape[3])
        return ap

    def mark(self):
        return self.off

    def release(self, m):
        self.off = m


class Ctx:
    pass


def own_blocks(c):
    out = []
    for g in range(4):
        out += [16 * g + c, 16 * g + 15 - c]
    return out


def build(n_kv_groups=16, phases=("K", "Q", "A", "R"), dbg=False, stop=99):
    nc = bass.Bass("TRN2", target_bir_lowering=False)
    P = Prog()
    C = Ctx()
    C.nc, C.P = nc, P
    C.stop = stop
    S_all = n_kv_groups * 512
    C.S_all = S_all

    def din(name, shape, dt=F32):
        return nc.dram_tensor(name, list(shape), dt, kind="ExternalInput").ap()

    scratch_kind = "ExternalOutput" if dbg else "Internal"

    def dscr(name, shape, dt=BF16):
        return nc.dram_tensor(name, list(shape), dt, kind=scratch_kind).ap()

    C.x_all = din("x_all", [S_all, D])
    C.pos_all = din("pos_all", [1, S_all], I32)
    C.g_pre = din("g_pre", [1, D])
    C.w_lat = din("w_lat", [D, 768])
    C.g_kv = din("g_kv", [512])
    C.w_k = din("w_k", [512, 2048])
    C.w_v = din("w_v", [512, 2048])
    C.ident = din("ident", [128, 128], BF16)
    C.ones = din("ones", [128, 128], BF16)
    C.ropec = din("ropec", [128, 2])
    C.kT_d = dscr("kT_d", [NH, 128, S_all])
    C.v_d = dscr("v_d", [NH, 128, S_all // 128, 128])
    C.krT_d = dscr("krT_d", [128, S_all])
    full = any(p in phases for p in ("Q", "A", "R"))
    if full:
        C.x_own = din("x_own", [1024, D])
        C.pos_own = din("pos_own", [1, 1024], I32)
        C.x_halo = din("x_halo", [128, D])
        C.mask16 = din("mask16", [128, 16], BF16)
        C.w_in = din("w_in", [D, 15424])
        C.g_qa = din("g_qa", [512])
        C.w_qn = din("w_qn", [512, 2048])
        C.w_qr = din("w_qr", [512, 1024])
        C.w_qrs = din("w_qrs", [512, 1024])
        C.conv_w = din("conv_w", [3, D])
        C.w_o_mla = din("w_o_mla", [D, D])
        C.w_o_conv = din("w_o_conv", [D, D])
        C.w_out = din("w_out", [D, D])
        C.g_post = din("g_post", [1, D])
        okind = "ExternalOutput"
        C.out = nc.dram_tensor("out", [1024, D], F32, kind=okind).ap()
        if dbg:
            C.dbg_attn = nc.dram_tensor("dbg_attn", [128, 16, 1024], BF16, kind=okind).ap()
            C.dbg_qn = nc.dram_tensor("dbg_qn", [128, 16, 1024], BF16, kind=okind).ap()
            C.dbg_qr = nc.dram_tensor("dbg_qr", [128, 8, 1024], BF16, kind=okind).ap()
            C.dbg_gated = nc.dram_tensor("dbg_gated", [128, 16, 1024], BF16, kind=okind).ap()
            C.dbg_gc = nc.dram_tensor("dbg_gc", [128, 16, 1024], BF16, kind=okind).ap()
            C.dbg_mg = nc.dram_tensor("dbg_mg", [128, 16, 1024], BF16, kind=okind).ap()

    C.sb = SB(nc, 206 * 1024)
    C.ps = [nc.alloc_psum_tensor("ps%d" % i, [128, 512], F32) for i in range(8)]
    C.ps_i = 0

    const_setup(C)
    if "K" in phases and C.stop >= 1:
        phase_K(C, n_kv_groups)
    if full:
        sb = C.sb
        C.attn = sb.alloc([128, 16, 1024], BF16)
        m0 = sb.mark()
        C.Qn = sb.alloc([128, 16, 1024], BF16)
        C.Qr = sb.alloc([128, 8, 1024], BF16)
        P.barrier()
        if "Q" in phases:
            phase_Q(C)
        if dbg and "Q" in phases:
            C.final_ops.append(P.dma("sp", lambda e: e.dma_start(out=C.dbg_qn, in_=C.Qn), reads=["Qn%d" % h for h in range(NH)], chan="dbg"))
            C.final_ops.append(P.dma("sp", lambda e: e.dma_start(out=C.dbg_qr, in_=C.Qr), reads=["Qr%d" % h for h in range(8)], chan="dbg"))
        P.barrier()
        if "A" in phases:
            phase_A(C)
        if dbg and "A" in phases:
            C.final_ops.append(P.dma("sp", lambda e: e.dma_start(out=C.dbg_attn, in_=C.attn),
                                     reads=["attn%d" % h for h in range(NH)], chan="dbg"))
        P.barrier()
        sb.release(m0)
        if "R" in phases:
            phase_R(C)

    P.join("sp", C.final_ops)
    P.emit(nc)
    return nc


def ps_next(C):
    i = C.ps_i
    C.ps_i = (i + 1) % 8
    return i


def const_setup(C):
    nc, P, sb = C.nc, C.P, C.sb
    C.final_ops = []
    C.ident_sb = sb.alloc([128, 128], BF16)
    C.ones_sb = sb.alloc([128, 128], BF16)
    C.ropec_sb = sb.alloc([128, 2], F32)
    C.eps_sb = sb.alloc([128, 1], F32)
    C.gb = sb.alloc([128, D], F32)
    P.dma("sp", lambda e: e.dma_start(out=C.ident_sb, in_=C.ident), writes=["ident"], chan="const", bulk=True)
    P.dma("sp", lambda e: e.dma_start(out=C.ones_sb, in_=C.ones), writes=["ones"], chan="const", bulk=True)
    P.dma("sp", lambda e: e.dma_start(out=C.ropec_sb, in_=C.ropec), writes=["ropec"], chan="const", bulk=True)
    P.dma("sp", lambda e: e.dma_start(out=C.gb, in_=C.g_pre.broadcast_to([128, D])), writes=["gb"], chan="const", bulk=True)
    P.op("dve", lambda e: e.memset(C.eps_sb, EPS), writes=["eps"])


def rope_tables(C, pos_src, n, Ct, St, tmp, tag):
    P = C.P
    pi_t, a, kf, r, m = tmp
    P.dma("sp", lambda e: e.dma_start(out=pi_t, in_=pos_src.broadcast_to([128, n])),
          writes=[tag + "pi"], chan=tag + "pi")
    P.op("dve", lambda e: e.tensor_copy(out=a, in_=pi_t), reads=[tag + "pi"], writes=[tag + "a"])
    P.op("dve", lambda e: e.tensor_scalar(out=a, in0=a, scalar1=C.ropec_sb[:, 0:1], scalar2=None, op0=ALU.mult),
         reads=[tag + "a", "ropec"], writes=[tag + "a"])
    P.op("dve", lambda e: e.tensor_scalar(out=kf, in0=a, scalar1=1.0 / TWO_PI, scalar2=None, op0=ALU.mult),
         reads=[tag + "a"], writes=[tag + "kf"])
    ki = pi_t
    P.op("dve", lambda e: e.tensor_copy(out=ki, in_=kf), reads=[tag + "kf"], writes=[tag + "pi"])
    P.op("dve", lambda e: e.tensor_copy(out=kf, in_=ki), reads=[tag + "pi"], writes=[tag + "kf"])
    C1 = 6.28125
    C2 = TWO_PI - C1
    P.op("dve", lambda e: e.scalar_tensor_tensor(out=r, in0=kf, scalar=-C1, in1=a, op0=ALU.mult, op1=ALU.add),
         reads=[tag + "kf", tag + "a"], writes=[tag + "r"])
    P.op("dve", lambda e: e.scalar_tensor_tensor(out=r, in0=kf, scalar=-C2, in1=r, op0=ALU.mult, op1=ALU.add),
         reads=[tag + "kf", tag + "r"], writes=[tag + "r"])

    def wrap(x):
        P.op("dve", lambda e: e.tensor_scalar(out=m, in0=x, scalar1=math.pi, scalar2=-TWO_PI, op0=ALU.is_gt, op1=ALU.mult),
             reads=[tag + "r"], writes=[tag + "m"])
        P.op("dve", lambda e: e.tensor_tensor(out=x, in0=x, in1=m, op=ALU.add),
             reads=[tag + "r", tag + "m"], writes=[tag + "r"])
        P.op("dve", lambda e: e.tensor_scalar(out=m, in0=x, scalar1=-math.pi, scalar2=TWO_PI, op0=ALU.is_lt, op1=ALU.mult),
             reads=[tag + "r"], writes=[tag + "m"])
        P.op("dve", lambda e: e.tensor_tensor(out=x, in0=x, in1=m, op=ALU.add),
             reads=[tag + "r", tag + "m"], writes=[tag + "r"])

    wrap(r)
    P.op("act", lambda e: e.activation(out=St, in_=r, func=AF.Sin, scale=C.ropec_sb[:, 1:2]),
         reads=[tag + "r", "ropec"], writes=[tag + "S"])
    P.op("dve", lambda e: e.tensor_scalar(out=r, in0=r, scalar1=math.pi / 2, scalar2=None, op0=ALU.add),
         reads=[tag + "r"], writes=[tag + "r"])
    wrap(r)
    P.op("act", lambda e: e.activation(out=Ct, in_=r, func=AF.Sin),
         reads=[tag + "r"], writes=[tag + "C"])


def front_end1(C, x_src, slot, bufs, xslot=None):
    P = C.P
    xb, junk, ss, sd, rstd, xs = bufs
    sl = "fe%d" % slot
    xr = "fe%dxb" % (slot if xslot is None else xslot)
    P.dma("sp", lambda e: e.dma_start(out=xb, in_=x_src), writes=[xr], chan=xr)
    P.op("act", lambda e: e.activation(out=xs, in_=xb, func=AF.Square, accum_out=ss),
         reads=[xr], writes=[sl + "xs", sl + "ss"])
    P.op("act", lambda e: e.activation(out=sd, in_=ss, func=AF.Sqrt, scale=1.0 / D, bias=C.eps_sb),
         reads=[sl + "ss", "eps"], writes=[sl + "sd"])
    P.op("dve", lambda e: e.reciprocal(out=rstd, in_=sd), reads=[sl + "sd"], writes=[sl + "rstd"])
    P.op("dve", lambda e: e.scalar_tensor_tensor(out=xs, in0=xb, scalar=rstd, in1=C.gb, op0=ALU.mult, op1=ALU.mult),
         reads=[xr, sl + "rstd", "gb"], writes=[sl + "xs"])


def front_end2(C, slot, hT_dst, hres, xs):
    P = C.P
    sl = "fe%d" % slot
    for q in range(4):
        b = ps_next(C)
        pb = C.ps[b][:, :].bitcast(BF16)
        for kk in range(4):
            kc = 4 * q + kk
            P.op("pe", lambda e, kc=kc, kk=kk, pb=pb: e.transpose(out=pb[:, kk * 128:(kk + 1) * 128],
                                                               in_=xs[:, kc * 128:(kc + 1) * 128], identity=C.ident_sb),
                 reads=[sl + "xs", "ident"], writes=["ps%d" % b], lhs=[sl + "xs"])
        src = pb[:, 0:512].rearrange("p (a b) -> p a b", b=128)
        dst = hT_dst[:, 4 * q:4 * q + 4, :]
        if q % 2 == 0:
            P.op("dve", lambda e, src=src, dst=dst: e.tensor_copy(out=dst, in_=src),
                 reads=["ps%d" % b], writes=[hres + "_%d" % q])
        else:
            P.op("act", lambda e, src=src, dst=dst: e.copy(out=dst, in_=src),
                 reads=["ps%d" % b], writes=[hres + "_%d" % q])


def front_end(C, x_src, slot, hT_dst, hres, bufs):
    front_end1(C, x_src, slot, bufs)
    front_end2(C, slot, hT_dst, hres, bufs[5])


def phase_K(C, n_groups):
    nc, P, sb = C.nc, C.P, C.sb
    mk = sb.mark()
    S_all = C.S_all
    wlat = sb.alloc([128, 16, 768], BF16)
    wk = sb.alloc([128, 4, 2048], BF16)
    wv = sb.alloc([128, 4, 2048], BF16)
    gkv = sb.alloc([128, 4], F32)
    for h2 in range(2):
        P.dma("pool", lambda e, h2=h2: e.dma_start(out=wlat[:, 8 * h2:8 * h2 + 8, :],
                                                 in_=C.w_lat[1024 * h2:1024 * (h2 + 1), :].rearrange("(kc p) n -> p kc n", p=128)),
              writes=["wlat%d" % h2], chan="wlat", bulk=True)
    P.dma("pool", lambda e: e.dma_start(out=wk, in_=C.w_k.rearrange("(kc p) n -> p kc n", p=128)), writes=["wk"], chan="wkv", bulk=True)
    P.dma("pool", lambda e: e.dma_start(out=wv, in_=C.w_v.rearrange("(kc p) n -> p kc n", p=128)), writes=["wv"], chan="wkv", bulk=True)
    P.dma("sp", lambda e: e.dma_start(out=gkv, in_=C.g_kv.rearrange("(c p) -> p c", p=128), allow_slow_non_contiguous=True), writes=["gkv"], chan="const", bulk=True)

    xb = [sb.alloc([128, D], F32) for _ in range(2)]
    ssb = [sb.alloc([128, 1], F32) for _ in range(4)]
    sdb = [sb.alloc([128, 1], F32) for _ in range(4)]
    rsb = [sb.alloc([128, 1], F32) for _ in range(4)]
    xs = [sb.alloc([128, D], BF16) for _ in range(4)]
    hT = [sb.alloc([128, 16, 512], BF16) for _ in range(2)]
    sq = sb.alloc([128, 4, 512], BF16)
    junk = None
    craw = sb.alloc([128, 4, 512], F32)
    ckvn = [sb.alloc([128, 4, 512], BF16) for _ in range(2)]
    sdk = sb.alloc([128, 512], F32)
    rk = sb.alloc([128, 512], F32)
    Ct = sb.alloc([128, 512], F32)
    St = sb.alloc([128, 512], F32)
    rtmp = [sb.alloc([128, 512], I32)] + [sb.alloc([128, 512], F32) for _ in range(4)]
    t1 = sb.alloc([128, 512], F32)
    t2 = sb.alloc([128, 512], F32)
    krt = [sb.alloc([128, 512], BF16) for _ in range(2)]
    kst = [sb.alloc([128, 8, 512], BF16) for _ in range(2)]
    vst = sb.alloc([128, 16, 4, 128], BF16)

    def fe1_blk(g, tbk):
        t0 = g * 512 + tbk * 128
        front_end1(C, C.x_all[t0:t0 + 128, :], 20 + tbk, (xb[tbk % 2], junk, ssb[tbk], sdb[tbk], rsb[tbk], xs[tbk]), xslot=30 + tbk % 2)

    def fe1(g):
        for tbk in range(4):
            fe1_blk(g, tbk)

    def fe2(g):
        hs = g % 2
        for tbk in range(4):
            front_end2(C, 20 + tbk, hT[hs][:, :, tbk * 128:(tbk + 1) * 128], "hT%d_%d" % (hs, tbk), xs[tbk])

    def latents(g):
        hs = g % 2
        t0 = g * 512
        rope_tables(C, C.pos_all[:, t0:t0 + 512], 512, Ct, St, rtmp, "rk")
        hres = lambda kc: ["hT%d_%d_%d" % (hs, tb, kc // 4) for tb in range(4)]
        for cb in range(6):
            b = ps_next(C)
            for kc in range(16):
                P.op("pe", lambda e, b=b, cb=cb, kc=kc: e.matmul(out=C.ps[b][:, :], lhsT=wlat[:, kc, cb * 128:(cb + 1) * 128],
                                                               rhs=hT[hs][:, kc, :], start=(kc == 0), stop=(kc == 15)),
                     reads=["wlat%d" % (kc // 8)] + hres(kc), writes=["ps%d" % b], lhs=["wlat%d" % (kc // 8)])
            if cb < 4:
                P.op("dve", lambda e, b=b, cb=cb: e.tensor_copy(out=craw[:, cb, :], in_=C.ps[b][:, :]),
                     reads=["ps%d" % b], writes=["craw%d" % cb])
                P.op("act", lambda e, cb=cb: e.activation(out=sq[:, cb, :], in_=craw[:, cb, :], func=AF.Square),
                     reads=["craw%d" % cb], writes=["sq%d" % cb])
            elif cb == 4:
                P.op("dve", lambda e, b=b: e.tensor_tensor(out=t1, in0=C.ps[b][:, :], in1=Ct, op=ALU.mult),
                     reads=["ps%d" % b, "rkC"], writes=["t1"])
            else:
                P.op("dve", lambda e, b=b: e.tensor_tensor(out=t2, in0=C.ps[b][:, :], in1=St, op=ALU.mult),
                     reads=["ps%d" % b, "rkS"], writes=["t2"])
                ks = g % 2
                P.op("pool", lambda e, ks=ks: e.tensor_tensor(out=krt[ks], in0=t1, in1=t2, op=ALU.add),
                     reads=["t1", "t2"], writes=["krt%d" % ks])
                C.final_ops.append(
                    P.dma("pool", lambda e, ks=ks, t0=t0: e.dma_start(out=C.krT_d[:, t0:t0 + 512], in_=krt[ks]),
                          reads=["krt%d" % ks], writes=["krT_d"], chan="krst%d" % ks))

    def ckv_norm(g):
        cs = g % 2
        b = ps_next(C)
        for cb in range(4):
            P.op("pe", lambda e, b=b, cb=cb: e.matmul(out=C.ps[b][:, :], lhsT=C.ones_sb, rhs=sq[:, cb, :],
                                                    start=(cb == 0), stop=(cb == 3)),
                 reads=["ones", "sq%d" % cb], writes=["ps%d" % b], lhs=["ones"])
        P.op("act", lambda e, b=b: e.activation(out=sdk, in_=C.ps[b][:, :], func=AF.Sqrt, scale=1.0 / 512, bias=C.eps_sb),
             reads=["ps%d" % b, "eps"], writes=["sdk"])
        P.op("dve", lambda e: e.reciprocal(out=rk, in_=sdk), reads=["sdk"], writes=["rk"])
        for cb in range(4):
            P.op("dve", lambda e, cb=cb, cs=cs: e.scalar_tensor_tensor(out=ckvn[cs][:, cb, :], in0=craw[:, cb, :],
                                                                     scalar=gkv[:, cb:cb + 1], in1=rk,
                                                                     op0=ALU.mult, op1=ALU.mult),
                 reads=["craw%d" % cb, "gkv", "rk"], writes=["ckvn%d_%d" % (cs, cb)])

    def k_proj(g):
        cs = g % 2
        t0 = g * 512
        ckres = ["ckvn%d_%d" % (cs, cb) for cb in range(4)]
        for h in range(NH):
            b = ps_next(C)
            for c4 in range(4):
                P.op("pe", lambda e, b=b, h=h, c4=c4: e.matmul(out=C.ps[b][:, :], lhsT=wk[:, c4, h * 128:(h + 1) * 128],
                                                             rhs=ckvn[cs][:, c4, :], start=(c4 == 0), stop=(c4 == 3)),
                     reads=["wk", ckres[c4]], writes=["ps%d" % b], lhs=["wk"])
            half = h // 8
            if h % 2 == 0:
                P.op("act", lambda e, b=b, h=h, half=half: e.copy(out=kst[half][:, h % 8, :], in_=C.ps[b][:, :]),
                     reads=["ps%d" % b], writes=["kst%d" % half])
            else:
                P.op("dve", lambda e, b=b, h=h, half=half: e.tensor_copy(out=kst[half][:, h % 8, :], in_=C.ps[b][:, :]),
                     reads=["ps%d" % b], writes=["kst%d" % half])
            if h % 8 == 7:
                C.final_ops.append(
                    P.dma("pool", lambda e, half=half, t0=t0: e.dma_start(
                        out=C.kT_d[8 * half:8 * half + 8, :, t0:t0 + 512].rearrange("h d t -> d h t"), in_=kst[half]),
                        reads=["kst%d" % half], writes=["kT_d"], chan="kst%d" % half))
            yield

    def v_proj(g):
        cs = g % 2
        t0 = g * 512
        ckres = ["ckvn%d_%d" % (cs, cb) for cb in range(4)]
        for tbk in range(4):
            for cg in range(4):
                b = ps_next(C)
                for c4 in range(4):
                    P.op("pe", lambda e, b=b, cg=cg, c4=c4, tbk=tbk: e.matmul(
                        out=C.ps[b][:, :], lhsT=ckvn[cs][:, c4, tbk * 128:(tbk + 1) * 128],
                        rhs=wv[:, c4, cg * 512:(cg + 1) * 512], start=(c4 == 0), stop=(c4 == 3)),
                        reads=["wv", ckres[c4]], writes=["ps%d" % b], lhs=[ckres[c4]])
                src = C.ps[b][:, :].rearrange("p (h d) -> p h d", d=128)
                dst = vst[:, 4 * cg:4 * cg + 4, tbk, :]
                if cg % 2 == 0:
                    P.op("act", lambda e, src=src, dst=dst: e.copy(out=dst, in_=src),
                         reads=["ps%d" % b], writes=["vst"])
                else:
                    P.op("dve", lambda e, src=src, dst=dst: e.tensor_copy(out=dst, in_=src),
                         reads=["ps%d" % b], writes=["vst"])
                yield
        kb0 = t0 // 128
        C.final_ops.append(
            P.dma("pool", lambda e, kb0=kb0: e.dma_start(
                out=C.v_d[:, :, kb0:kb0 + 4, :].rearrange("h p k d -> p h k d"), in_=vst),
                reads=["vst"], writes=["v_d"], chan="vst"))

    G = n_groups
    fe1(0)
    fe2(0)
    latents(0)
    ckv_norm(0)
    if G > 1:
        fe1(1)
    for g in range(G):
        if g + 1 < G:
            fe2(g + 1)
            latents(g + 1)
        cnt = 0

        def tick():
            nonlocal cnt
            cnt += 1

        if g + 2 < G:
            fe1(g + 2)
        for _ in k_proj(g):
            tick()
        if g + 1 < G:
            ckv_norm(g + 1)
        for _ in v_proj(g):
            tick()
    sb.release(mk)


def make_consts():
    import ml_dtypes
    inv = np.power(np.float32(10000.0), -np.arange(0, 64, 2, dtype=np.float32) / np.float32(64)).astype(np.float32)
    ropec = np.zeros((128, 2), np.float32)
    for p in range(128):
        ropec[p, 0] = inv[p % 32]
        ropec[p, 1] = -1.0 if (p % 64) < 32 else 1.0
    return dict(ident=np.eye(128, dtype=np.float32).astype(ml_dtypes.bfloat16),
                ones=np.ones((128, 128), np.float32).astype(ml_dtypes.bfloat16),
                ropec=ropec)


def phase_Q(C):
    P, sb = C.P, C.sb
    mk = sb.mark()
    wqa = sb.alloc([128, 16, 512], BF16)
    wqn = sb.alloc([128, 4, 2048], BF16)
    wqr = sb.alloc([128, 4, 1024], BF16)
    wqrs = sb.alloc([128, 4, 1024], BF16)
    gqa = sb.alloc([128, 4], F32)
    P.dma("pool", lambda e: e.dma_start(out=wqa, in_=C.w_in[:, 0:512].rearrange("(kc p) n -> p kc n", p=128)),
          writes=["wqa"], chan="wq", bulk=True)
    P.dma("pool", lambda e: e.dma_start(out=wqn, in_=C.w_qn.rearrange("(kc p) n -> p kc n", p=128)), writes=["wqn"], chan="wq", bulk=True)
    P.dma("pool", lambda e: e.dma_start(out=wqr, in_=C.w_qr.rearrange("(kc p) n -> p kc n", p=128)), writes=["wqr"], chan="wq", bulk=True)
    P.dma("pool", lambda e: e.dma_start(out=wqrs, in_=C.w_qrs.rearrange("(kc p) n -> p kc n", p=128)), writes=["wqrs"], chan="wq", bulk=True)
    P.dma("sp", lambda e: e.dma_start(out=gqa, in_=C.g_qa.rearrange("(c p) -> p c", p=128), allow_slow_non_contiguous=True),
          writes=["gqa"], chan="constQ", bulk=True)
    xb = sb.alloc([128, D], F32)
    ssb = sb.alloc([128, 1], F32)
    sdb = sb.alloc([128, 1], F32)
    rsb = sb.alloc([128, 1], F32)
    xs = sb.alloc([128, D], BF16)
    hTq = sb.alloc([128, 16, 512], BF16)
    qraw = sb.alloc([128, 4, 512], F32)
    sq = sb.alloc([128, 4, 512], BF16)
    junk = None
    qan = sb.alloc([128, 4, 512], BF16)
    sdq = sb.alloc([128, 512], F32)
    rq = sb.alloc([128, 512], F32)
    Ct = sb.alloc([128, 512], F32)
    St = sb.alloc([128, 512], F32)
    rtmp = [sb.alloc([128, 512], I32)] + [sb.alloc([128, 512], F32) for _ in range(4)]
    t1 = sb.alloc([128, 512], F32)
    t2 = sb.alloc([128, 512], F32)

    def q_half(hf):
        t0 = 512 * hf
        for tbk in range(4):
            front_end(C, C.x_own[t0 + tbk * 128:t0 + (tbk + 1) * 128, :], 7,
                      hTq[:, :, tbk * 128:(tbk + 1) * 128], "hq_%d" % tbk, (xb, junk, ssb, sdb, rsb, xs))
        rope_tables(C, C.pos_own[:, t0:t0 + 512], 512, Ct, St, rtmp, "rq")
        hres = lambda kc: ["hq_%d_%d" % (tb, kc // 4) for tb in range(4)]
        for cb in range(4):
            b = ps_next(C)
            for kc in range(16):
                P.op("pe", lambda e, b=b, cb=cb, kc=kc: e.matmul(out=C.ps[b][:, :], lhsT=wqa[:, kc, cb * 128:(cb + 1) * 128],
                                                               rhs=hTq[:, kc, :], start=(kc == 0), stop=(kc == 15)),
                     reads=["wqa"] + hres(kc), writes=["ps%d" % b], lhs=["wqa"])
            P.op("dve", lambda e, b=b, cb=cb: e.tensor_copy(out=qraw[:, cb, :], in_=C.ps[b][:, :]),
                 reads=["ps%d" % b], writes=["qraw%d" % cb])
            P.op("act", lambda e, cb=cb: e.activation(out=sq[:, cb, :], in_=qraw[:, cb, :], func=AF.Square),
                 reads=["qraw%d" % cb], writes=["qsq%d" % cb])
        b = ps_next(C)
        for cb in range(4):
            P.op("pe", lambda e, b=b, cb=cb: e.matmul(out=C.ps[b][:, :], lhsT=C.ones_sb, rhs=sq[:, cb, :],
                                                    start=(cb == 0), stop=(cb == 3)),
                 reads=["ones", "qsq%d" % cb], writes=["ps%d" % b], lhs=["ones"])
        P.op("act", lambda e, b=b: e.activation(out=sdq, in_=C.ps[b][:, :], func=AF.Sqrt, scale=1.0 / 512, bias=C.eps_sb),
             reads=["ps%d" % b, "eps"], writes=["sdq"])
        P.op("dve", lambda e: e.reciprocal(out=rq, in_=sdq), reads=["sdq"], writes=["rq"])
        for cb in range(4):
            P.op("dve", lambda e, cb=cb: e.scalar_tensor_tensor(out=qan[:, cb, :], in0=qraw[:, cb, :], scalar=gqa[:, cb:cb + 1],
                                                             in1=rq, op0=ALU.mult, op1=ALU.mult),
                 reads=["qraw%d" % cb, "gqa", "rq"], writes=["qan%d" % cb])
        qres = ["qan%d" % cb for cb in range(4)]
        for h in range(NH):
            b = ps_next(C)
            for c4 in range(4):
                P.op("pe", lambda e, b=b, h=h, c4=c4: e.matmul(out=C.ps[b][:, :], lhsT=wqn[:, c4, h * 128:(h + 1) * 128],
                                                             rhs=qan[:, c4, :], start=(c4 == 0), stop=(c4 == 3)),
                     reads=["wqn", qres[c4]], writes=["ps%d" % b], lhs=["wqn"])
            if h % 2 == 0:
                P.op("act", lambda e, b=b, h=h: e.copy(out=C.Qn[:, h, t0:t0 + 512], in_=C.ps[b][:, :]),
                     reads=["ps%d" % b], writes=["Qn%d" % h])
            else:
                P.op("dve", lambda e, b=b, h=h: e.tensor_copy(out=C.Qn[:, h, t0:t0 + 512], in_=C.ps[b][:, :]),
                     reads=["ps%d" % b], writes=["Qn%d" % h])
        for hp in range(8):
            b1 = ps_next(C)
            for c4 in range(4):
                P.op("pe", lambda e, b1=b1, hp=hp, c4=c4: e.matmul(out=C.ps[b1][:, :], lhsT=wqr[:, c4, hp * 128:(hp + 1) * 128],
                                                                 rhs=qan[:, c4, :], start=(c4 == 0), stop=(c4 == 3)),
                     reads=["wqr", qres[c4]], writes=["ps%d" % b1], lhs=["wqr"])
            P.op("dve", lambda e, b1=b1: e.tensor_tensor(out=t1, in0=C.ps[b1][:, :], in1=Ct, op=ALU.mult),
                 reads=["ps%d" % b1, "rqC"], writes=["qt1"])
            b2 = ps_next(C)
            for c4 in range(4):
                P.op("pe", lambda e, b2=b2, hp=hp, c4=c4: e.matmul(out=C.ps[b2][:, :], lhsT=wqrs[:, c4, hp * 128:(hp + 1) * 128],
                                                                 rhs=qan[:, c4, :], start=(c4 == 0), stop=(c4 == 3)),
                     reads=["wqrs", qres[c4]], writes=["ps%d" % b2], lhs=["wqrs"])
            P.op("dve", lambda e, b2=b2: e.tensor_tensor(out=t2, in0=C.ps[b2][:, :], in1=St, op=ALU.mult),
                 reads=["ps%d" % b2, "rqS"], writes=["qt2"])
            P.op("pool", lambda e, hp=hp: e.tensor_tensor(out=C.Qr[:, hp, t0:t0 + 512], in0=t1, in1=t2, op=ALU.add),
                 reads=["qt1", "qt2"], writes=["Qr%d" % hp])

    for hf in range(2):
        q_half(hf)
    sb.release(mk)


def phase_A(C):
    P, sb = C.P, C.sb
    mk = sb.mark()
    S_all = C.S_all
    nkb = S_all // 128
    nq = nkb // 16
    KrT = [sb.alloc([128, S_all], BF16) for _ in range(2)]
    kring = [sb.alloc([128, 2048], BF16) for _ in range(4)]
    vring = [sb.alloc([128, 16, 128], BF16) for _ in range(4)]
    NPT = 6
    Pt = [sb.alloc([128, 512], BF16) for _ in range(NPT)]
    negm = sb.alloc([128, 16], BF16)
    Of = sb.alloc([128, 1024], F32)
    rl = sb.alloc([128, 1024], F32)
    accS = sb.alloc([128, 1024], F32)
    onesf = sb.alloc([128, 128], F32)
    stores = list(C.final_ops)
    P.op("pool", lambda e: e.memset(onesf, 1.0), writes=["onesf"])
    P.op("pool", lambda e: e.memset(KrT[0][64:128, :], 0.0), writes=["KrT0z"])
    P.op("pool", lambda e: e.memset(KrT[1][0:64, :], 0.0), writes=["KrT1z"])
    P.dma("sp", lambda e: e.dma_start(out=negm, in_=C.mask16), writes=["negm"], chan="constA", bulk=True)
    P.dma("sp", lambda e: e.dma_start(out=KrT[0][0:64, :], in_=C.krT_d[0:64, :]), writes=["KrT0"], chan="constA", bulk=True, extra=stores)
    P.dma("sp", lambda e: e.dma_start(out=KrT[1][64:128, :], in_=C.krT_d[64:128, :]), writes=["KrT1"], chan="constA", bulk=True, extra=stores)

    def load_q(i):
        h, q = divmod(i, nq)
        s = i % 4
        P.dma("sp", lambda e: e.dma_start(out=kring[s], in_=C.kT_d[h, :, 2048 * q:2048 * (q + 1)]),
              writes=["kq%d" % s], chan="kq%d" % s, extra=stores)
        P.dma("sp", lambda e: e.dma_start(out=vring[s], in_=C.v_d[h, :, 16 * q:16 * (q + 1), :]),
              writes=["vq%d" % s], chan="vq%d" % s, extra=stores)

    nload = NH * nq
    for i in range(min(4, nload)):
        load_q(i)

    units = []
    for h in range(NH):
        for kb in range(nkb):
            for half in range(2):
                lo = max(16 * kb, 512 * half)
                hi = 512 * (half + 1)
                if lo < hi:
                    units.append((h, kb, half, lo, hi))
    last_kb = {0: min(nkb - 1, 31), 1: nkb - 1}
    sbank = [5, 6, 7]
    NSB = 3
    ACCB = [3, 4]
    pool_cnt = {}

    def emit_S(u, ui):
        h, kb, half, lo, hi = u
        n = hi - lo
        b = sbank[ui % NSB]
        s = (h * nq + kb // 16) % 4
        kk = (kb % 16) * 128
        par = h % 2
        P.op("pe", lambda e: e.matmul(out=C.ps[b][:, 0:n], lhsT=kring[s][:, kk:kk + 128], rhs=C.Qn[:, h, lo:hi],
                                      start=True, stop=False),
             reads=["kq%d" % s, "Qn%d" % h], writes=["ps%d" % b], lhs=["kq%d" % s])
        P.op("pe", lambda e: e.matmul(out=C.ps[b][:, 0:n], lhsT=KrT[par][:, kb * 128:(kb + 1) * 128],
                                      rhs=C.Qr[:, h // 2, lo:hi], start=False, stop=True),
             reads=["KrT%d" % par, "KrT%dz" % par, "Qr%d" % (h // 2)], writes=["ps%d" % b], lhs=["KrT%d" % par, "KrT%dz" % par])

    def emit_PV(u, ui):
        h, kb, half, lo, hi = u
        n = hi - lo
        b = sbank[ui % NSB]
        pt = Pt[ui % NPT]
        pres = "Pt%d" % (ui % NPT)
        s = (h * nq + kb // 16) % 4
        hp = h % 2
        P.op("act", lambda e: e.activation(out=pt[:, 0:n], in_=C.ps[b][:, 0:n], func=AF.Exp, scale=SCALE),
             reads=["ps%d" % b], writes=[pres])
        if lo == 16 * kb:
            P.op("pool", lambda e: e.tensor_tensor(out=pt[:, 0:16], in0=pt[:, 0:16], in1=negm, op=ALU.mult),
                 reads=[pres, "negm"], writes=[pres])
        c0 = lo - 512 * half
        first = (kb == 0)
        last = (kb == last_kb[half])
        ob, lb = half, 2
        P.op("pe", lambda e: e.matmul(out=C.ps[ob][:, c0:c0 + n], lhsT=vring[s][:, kb % 16, :], rhs=pt[:, 0:n],
                                      start=first, stop=last),
             reads=["vq%d" % s, pres], writes=["ps%d" % ob], lhs=["vq%d" % s])
        ab = ACCB[half]
        if first:
            P.op("dve", lambda e: e.tensor_copy(out=C.ps[ab][:, c0:c0 + n], in_=pt[:, 0:n]),
                 reads=[pres], writes=["ps%d" % ab])
        else:
            P.op("dve", lambda e: e.tensor_tensor(out=C.ps[ab][:, c0:c0 + n], in0=C.ps[ab][:, c0:c0 + n], in1=pt[:, 0:n], op=ALU.add),
                 reads=[pres, "ps%d" % ab], writes=["ps%d" % ab])
        if last:
            hs_ = slice(512 * half, 512 * (half + 1))
            P.op("dve", lambda e: e.tensor_copy(out=accS[:, hs_], in_=C.ps[ab][:, :]), reads=["ps%d" % ab], writes=["accS%d" % half])
            P.op("pe", lambda e: e.matmul(out=C.ps[lb][:, :], lhsT=onesf, rhs=accS[:, hs_], start=True, stop=True),
                 reads=["onesf", "accS%d" % half], writes=["ps%d" % lb])
            P.op("dve", lambda e: e.reciprocal(out=rl[:, hs_], in_=C.ps[lb][:, :]), reads=["ps%d" % lb], writes=["rl%d" % half])
            P.op("dve", lambda e: e.tensor_tensor(out=C.attn[:, h, hs_], in0=C.ps[ob][:, :], in1=rl[:, hs_], op=ALU.mult),
                 reads=["ps%d" % ob, "rl%d" % half], writes=["attn%d" % h])
        if kb % 16 == 15 and half == 1:
            i = h * nq + kb // 16
            if i + 4 < nload:
                load_q(i + 4)

    LOOK = 2
    nu = len(units)
    for ui in range(min(LOOK, nu)):
        emit_S(units[ui], ui)
    for ui in range(nu):
        if ui + LOOK < nu:
            emit_S(units[ui + LOOK], ui + LOOK)
        emit_PV(units[ui], ui)
    sb.release(mk)


OFF_Z, OFF_CIN, OFF_BG, OFF_CG, OFF_ZC, OFF_GM, OFF_GC = 1088, 3136, 5184, 7232, 9280, 11328, 13376


def phase_R(C):
    P, sb = C.P, C.sb
    mg = sb.alloc([128, 16, 1024], BF16)
    mR = sb.mark()
    hT = sb.alloc([128, 16, 1024], BF16)
    hTh = sb.alloc([128, 16, 128], BF16)
    gc = sb.alloc([128, 16, 1024], BF16)
    wr = [sb.alloc([128, 16, 256], BF16) for _ in range(4)]
    convw = sb.alloc([128, 3, 16], F32)
    for k in range(3):
        P.dma("sp", lambda e, k=k: e.dma_start(out=convw[:, k, :], in_=C.conv_w[k].rearrange("(j p) -> p j", p=128),
                                             allow_slow_non_contiguous=True),
              writes=["convw"], chan="constR", bulk=True)
    mk = sb.mark()
    fbuf = []
    for _ in range(2):
        fbuf.append((sb.alloc([128, D], F32), None, sb.alloc([128, 1], F32), sb.alloc([128, 1], F32),
                     sb.alloc([128, 1], F32), sb.alloc([128, D], BF16)))
    srcs = [C.x_own[blk * 128:(blk + 1) * 128, :] for blk in range(8)] + [C.x_halo]
    dsts = [hT[:, :, blk * 128:(blk + 1) * 128] for blk in range(8)] + [hTh[:, :, :]]
    front_end1(C, srcs[0], 40, fbuf[0])
    for blk in range(9):
        if blk + 1 < 9:
            front_end1(C, srcs[blk + 1], 40 + (blk + 1) % 2, fbuf[(blk + 1) % 2])
        front_end2(C, 40 + blk % 2, dsts[blk], "hr_%d" % blk, fbuf[blk % 2][5])
    hres = lambda kc, half: ["hr_%d_%d" % (4 * half + tb, kc // 4) for tb in range(4)]
    hhres = lambda kc: ["hr_8_%d" % (kc // 4)]
    sb.release(mk)

    tiles = []
    for jp in range(8):
        tiles.append(C.w_in[:, OFF_Z + 256 * jp:OFF_Z + 256 * (jp + 1)])
    for jp in range(8):
        for off in (OFF_CIN, OFF_CG, OFF_BG, OFF_ZC):
            tiles.append(C.w_in[:, off + 256 * jp:off + 256 * (jp + 1)])
    for jp in range(8):
        tiles.append(C.w_o_mla[:, 256 * jp:256 * (jp + 1)])
        tiles.append(C.w_in[:, OFF_GM + 256 * jp:OFF_GM + 256 * (jp + 1)])
        tiles.append(C.w_in[:, OFF_GC + 256 * jp:OFF_GC + 256 * (jp + 1)])
        tiles.append(C.w_o_conv[:, 256 * jp:256 * (jp + 1)])
    st = {"next": 0}

    def issue(n=1):
        for _ in range(n):
            i = st["next"]
            if i >= len(tiles):
                return
            st["next"] = i + 1
            s_ = i % 4
            src = tiles[i].rearrange("(kc p) n -> p kc n", p=128)
            P.dma("pool", lambda e, s_=s_, src=src: e.dma_start(out=wr[s_], in_=src), writes=["wr%d" % s_], chan="wr%d" % s_)

    ti = {"i": 0}

    def take():
        i = ti["i"]
        ti["i"] = i + 1
        while st["next"] <= min(i + 3, len(tiles) - 1):
            issue(1)
        return wr[i % 4], "wr%d" % (i % 4)

    def mm(w, wres, sub, rhs_fn, rres_fn, n):
        b = ps_next(C)
        for kc in range(16):
            P.op("pe", lambda e, b=b, kc=kc: e.matmul(out=C.ps[b][:, 0:n], lhsT=w[:, kc, sub * 128:(sub + 1) * 128],
                                                    rhs=rhs_fn(kc), start=(kc == 0), stop=(kc == 15)),
                 reads=[wres] + rres_fn(kc), writes=["ps%d" % b], lhs=[wres])
        return b

    tA = sb.alloc([128, 2, 1024], F32)
    tB = sb.alloc([128, 2, 1024], F32)
    U = sb.alloc([128, 2, 64 * 18], F32)
    cinh = sb.alloc([128, 2, 128], F32)
    halves = [slice(0, 512), slice(512, 1024)]

    def own_mm(w, wres, sub, half):
        hsl = halves[half]
        return mm(w, wres, sub, lambda kc: hT[:, kc, hsl], lambda kc: hres(kc, half), 512)

    def r1(jp):
        w, wres = take()
        for sub in range(2):
            j = 2 * jp + sub
            for half in range(2):
                hsl = halves[half]
                b = own_mm(w, wres, sub, half)
                P.op("act", lambda e, b=b, hsl=hsl, sub=sub: e.activation(out=tA[:, sub, hsl], in_=C.ps[b][:, :], func=AF.Silu),
                     reads=["ps%d" % b], writes=["tA%d_%d" % (sub, half)])
                P.op("dve", lambda e, j=j, hsl=hsl, sub=sub: e.tensor_tensor(out=C.attn[:, j, hsl], in0=C.attn[:, j, hsl],
                                                                         in1=tA[:, sub, hsl], op=ALU.mult),
                     reads=["tA%d_%d" % (sub, half), "attn%d" % j], writes=["attn%d" % j])

    for jp in range(8):
        r1(jp)

    def Uv(sub):
        return U[:, sub, :].rearrange("p (m j) -> p m j", j=18)

    def r2(jp):
        w, wres = take()
        for sub in range(2):
            for half in range(2):
                hsl = halves[half]
                b = own_mm(w, wres, sub, half)
                P.op("act", lambda e, b=b, hsl=hsl, sub=sub: e.copy(out=tA[:, sub, hsl], in_=C.ps[b][:, :]),
                     reads=["ps%d" % b], writes=["tA%d_%d" % (sub, half)])
            b = mm(w, wres, sub, lambda kc: hTh[:, kc, :], hhres, 128)
            P.op("act", lambda e, b=b, sub=sub: e.copy(out=cinh[:, sub, :], in_=C.ps[b][:, 0:128]),
                 reads=["ps%d" % b], writes=["cinh%d" % sub])
        w, wres = take()
        for sub in range(2):
            j = 2 * jp + sub
            for half in range(2):
                hsl = halves[half]
                b = own_mm(w, wres, sub, half)
                P.op("dve", lambda e, b=b, half=half, hsl=hsl, sub=sub: e.tensor_tensor(
                    out=Uv(sub)[:, 32 * half:32 * (half + 1), 2:18], in0=C.ps[b][:, :].rearrange("p (m j) -> p m j", j=16),
                    in1=tA[:, sub, hsl].rearrange("p (m j) -> p m j", j=16), op=ALU.mult),
                    reads=["ps%d" % b, "tA%d_%d" % (sub, half)], writes=["Uo%d_%d" % (sub, half)])
            b = mm(w, wres, sub, lambda kc: hTh[:, kc, :], hhres, 128)
            P.op("dve", lambda e, b=b, sub=sub: e.tensor_tensor(out=Uv(sub)[:, :, 0:2],
                                                             in0=C.ps[b][:, 0:128].rearrange("p (m j) -> p m j", j=2),
                                                             in1=cinh[:, sub, :].rearrange("p (m j) -> p m j", j=2), op=ALU.mult),
                 reads=["ps%d" % b, "cinh%d" % sub], writes=["Uh%d" % sub])
            tB3 = tB[:, sub, :].rearrange("p (m j) -> p m j", j=16)
            ures = ["Uo%d_0" % sub, "Uo%d_1" % sub, "Uh%d" % sub]
            tres = "tB%d" % sub
            P.op("dve", lambda e, j=j, sub=sub, tB3=tB3: e.tensor_scalar(out=tB3, in0=Uv(sub)[:, :, 0:16], scalar1=convw[:, 0, j:j + 1],
                                                                     scalar2=None, op0=ALU.mult),
                 reads=ures + ["convw"], writes=[tres])
            P.op("dve", lambda e, j=j, sub=sub, tB3=tB3: e.scalar_tensor_tensor(out=tB3, in0=Uv(sub)[:, :, 1:17], scalar=convw[:, 1, j:j + 1],
                                                                            in1=tB3, op0=ALU.mult, op1=ALU.add),
                 reads=ures + ["convw", tres], writes=[tres])
            P.op("dve", lambda e, j=j, sub=sub, tB3=tB3: e.scalar_tensor_tensor(out=tB3, in0=Uv(sub)[:, :, 2:18], scalar=convw[:, 2, j:j + 1],
                                                                            in1=tB3, op0=ALU.mult, op1=ALU.add),
                 reads=ures + ["convw", tres], writes=[tres])
        w, wres = take()
        for sub in range(2):
            for half in range(2):
                hsl = halves[half]
                b = own_mm(w, wres, sub, half)
                P.op("dve", lambda e, b=b, hsl=hsl, sub=sub: e.tensor_tensor(out=tB[:, sub, hsl], in0=C.ps[b][:, :], in1=tB[:, sub, hsl], op=ALU.mult),
                     reads=["ps%d" % b, "tB%d" % sub], writes=["tB%d" % sub])
        w, wres = take()
        for sub in range(2):
            j = 2 * jp + sub
            for half in range(2):
                hsl = halves[half]
                b = own_mm(w, wres, sub, half)
                P.op("act", lambda e, b=b, hsl=hsl, sub=sub: e.activation(out=tA[:, sub, hsl], in_=C.ps[b][:, :], func=AF.Silu),
                     reads=["ps%d" % b], writes=["tA%d_%d" % (sub, half)])
                P.op("pool", lambda e, j=j, hsl=hsl, sub=sub: e.tensor_tensor(out=gc[:, j, hsl], in0=tB[:, sub, hsl], in1=tA[:, sub, hsl], op=ALU.mult),
                     reads=["tB%d" % sub, "tA%d_%d" % (sub, half)], writes=["gc%d" % j])

    for jp in range(8):
        r2(jp)

    allattn = ["attn%d" % j for j in range(16)]
    allgc = ["gc%d" % j for j in range(16)]

    def r3(jp):
        w, wres = take()
        for sub in range(2):
            for half in range(2):
                hsl = halves[half]
                b = mm(w, wres, sub, lambda kc, hsl=hsl: C.attn[:, kc, hsl], lambda kc: ["attn%d" % kc], 512)
                P.op("act", lambda e, b=b, hsl=hsl, sub=sub: e.copy(out=tB[:, sub, hsl], in_=C.ps[b][:, :]),
                     reads=["ps%d" % b], writes=["tB%d" % sub])
        w, wres = take()
        for sub in range(2):
            for half in range(2):
                hsl = halves[half]
                b = own_mm(w, wres, sub, half)
                P.op("act", lambda e, b=b, hsl=hsl, sub=sub: e.activation(out=tA[:, sub, hsl], in_=C.ps[b][:, :], func=AF.Sigmoid),
                     reads=["ps%d" % b], writes=["tA%d_%d" % (sub, half)])
                P.op("dve", lambda e, hsl=hsl, sub=sub: e.tensor_tensor(out=tB[:, sub, hsl], in0=tB[:, sub, hsl], in1=tA[:, sub, hsl], op=ALU.mult),
                     reads=["tB%d" % sub, "tA%d_%d" % (sub, half)], writes=["tB%d" % sub])
        w, wres = take()
        for sub in range(2):
            for half in range(2):
                hsl = halves[half]
                b = own_mm(w, wres, sub, half)
                P.op("act", lambda e, b=b, hsl=hsl, sub=sub: e.activation(out=tA[:, sub, hsl], in_=C.ps[b][:, :], func=AF.Sigmoid),
                     reads=["ps%d" % b], writes=["tA%d_%d" % (sub, half)])
        w, wres = take()
        for sub in range(2):
            j = 2 * jp + sub
            for half in range(2):
                hsl = halves[half]
                b = mm(w, wres, sub, lambda kc, hsl=hsl: gc[:, kc, hsl], lambda kc: ["gc%d" % kc], 512)
                P.op("dve", lambda e, b=b, hsl=hsl, sub=sub: e.tensor_tensor(out=tA[:, sub, hsl], in0=C.ps[b][:, :], in1=tA[:, sub, hsl], op=ALU.mult),
                     reads=["ps%d" % b, "tA%d_%d" % (sub, half)], writes=["tA%d_%d" % (sub, half)])
                P.op("pool", lambda e, j=j, hsl=hsl, sub=sub: e.tensor_tensor(out=mg[:, j, hsl], in0=tB[:, sub, hsl], in1=tA[:, sub, hsl], op=ALU.add),
                     reads=["tB%d" % sub, "tA%d_%d" % (sub, half)], writes=["mg%d" % j])

    for jp in range(8):
        r3(jp)

    if hasattr(C, "dbg_mg"):
        C.final_ops.append(P.dma("sp", lambda e: e.dma_start(out=C.dbg_gated, in_=C.attn), reads=allattn, chan="dbg"))
        C.final_ops.append(P.dma("sp", lambda e: e.dma_start(out=C.dbg_gc, in_=gc), reads=allgc, chan="dbg"))
        C.final_ops.append(P.dma("sp", lambda e: e.dma_start(out=C.dbg_mg, in_=mg), reads=["mg%d" % j for j in range(16)], chan="dbg"))
    P.barrier()
    sb.release(mR)
    wout = sb.alloc([128, 16, 2048], BF16)
    gpost = sb.alloc([128, D], F32)
    xr = [sb.alloc([128, D], F32) for _ in range(2)]
    ot = [sb.alloc([128, D], F32) for _ in range(2)]
    ssq = sb.alloc([128, 4], F32)
    sst = sb.alloc([128, 1], F32)
    sdo = sb.alloc([128, 1], F32)
    rso = sb.alloc([128, 1], F32)
    junk4 = sb.alloc([128, 512], BF16)
    for cg in range(4):
        P.dma("pool", lambda e, cg=cg: e.dma_start(out=wout[:, :, 512 * cg:512 * (cg + 1)],
                                                 in_=C.w_out[:, 512 * cg:512 * (cg + 1)].rearrange("(kc p) n -> p kc n", p=128)),
              writes=["wout%d" % cg], chan="wout%d" % cg)
    P.dma("sp", lambda e: e.dma_start(out=gpost, in_=C.g_post.broadcast_to([128, D])), writes=["gpost"], chan="constR4", bulk=True)
    allmg = ["mg%d" % j for j in range(16)]

    def r4(blk):
        s_ = blk % 2
        tsl = slice(128 * blk, 128 * (blk + 1))
        P.dma("sp", lambda e: e.dma_start(out=xr[s_], in_=C.x_own[tsl, :]), writes=["xr%d" % s_], chan="xr%d" % s_)
        banks = []
        for cg in range(4):
            b = ps_next(C)
            banks.append(b)
            for kc in range(16):
                P.op("pe", lambda e, b=b, kc=kc, cg=cg: e.matmul(out=C.ps[b][:, :], lhsT=mg[:, kc, tsl],
                                                               rhs=wout[:, kc, 512 * cg:512 * (cg + 1)],
                                                               start=(kc == 0), stop=(kc == 15)),
                     reads=["mg%d" % kc, "wout%d" % cg], writes=["ps%d" % b], lhs=["mg%d" % kc])
            P.op("act", lambda e, b=b, cg=cg: e.activation(out=junk4, in_=C.ps[b][:, :], func=AF.Square, accum_out=ssq[:, cg:cg + 1]),
                 reads=["ps%d" % b], writes=["junk4", "ssq%d" % cg])
        P.op("dve", lambda e: e.tensor_reduce(out=sst, in_=ssq, axis=mybir.AxisListType.X, op=ALU.add),
             reads=["ssq%d" % cg for cg in range(4)], writes=["sst"])
        P.op("act", lambda e: e.activation(out=sdo, in_=sst, func=AF.Sqrt, scale=1.0 / D, bias=C.eps_sb),
             reads=["sst", "eps"], writes=["sdo"])
        P.op("dve", lambda e: e.reciprocal(out=rso, in_=sdo), reads=["sdo"], writes=["rso"])
        for cg in range(4):
            b = banks[cg]
            csl = slice(512 * cg, 512 * (cg + 1))
            P.op("dve", lambda e, b=b, csl=csl: e.scalar_tensor_tensor(out=ot[s_][:, csl], in0=C.ps[b][:, :], scalar=rso,
                                                                     in1=gpost[:, csl], op0=ALU.mult, op1=ALU.mult),
                 reads=["ps%d" % b, "rso", "gpost"], writes=["ot%d_%d" % (s_, cg)])
            P.op("pool", lambda e, csl=csl: e.tensor_tensor(out=ot[s_][:, csl], in0=ot[s_][:, csl], in1=xr[s_][:, csl], op=ALU.add),
                 reads=["ot%d_%d" % (s_, cg), "xr%d" % s_], writes=["ot%d_%d" % (s_, cg)])
        C.final_ops.append(
            P.dma("sp", lambda e: e.dma_start(out=C.out[tsl, :], in_=ot[s_]),
                  reads=["ot%d_%d" % (s_, cg) for cg in range(4)], writes=["out"], chan="ot%d" % s_))

    for blk in range(8):
        r4(blk)


def _own_rows(c):
    return np.concatenate([np.arange(128 * m + 16 * c, 128 * m + 16 * c + 16) for m in range(64)])


def prep(x, positions, pre_norm_g, w_in, q_a_norm_g, w_q_b, kv_a_norm_g, w_kv_b,
         conv_w, w_o_mla, w_o_conv, w_out, post_norm_g, cores=range(NCORES)):
    import ml_dtypes
    x2 = np.ascontiguousarray(np.asarray(x, dtype=np.float32).reshape(S, D))
    pos = np.ascontiguousarray(np.asarray(positions, dtype=np.int32).reshape(1, S))
    w_in = np.asarray(w_in, dtype=np.float32)
    kr = w_in[:, 1024:1088]
    krs = np.concatenate([kr[:, 32:], kr[:, :32]], axis=1)
    w_lat = np.ascontiguousarray(np.concatenate([w_in[:, 512:1024], kr, kr, krs, krs], axis=1))
    wkv = np.asarray(w_kv_b, dtype=np.float32).reshape(512, NH, 256)
    w_k = np.ascontiguousarray(wkv[:, :, :128].reshape(512, 2048))
    w_v = np.ascontiguousarray(wkv[:, :, 128:].reshape(512, 2048))
    wq = np.asarray(w_q_b, dtype=np.float32).reshape(512, NH, 192)
    w_qn = np.ascontiguousarray(wq[:, :, :128].reshape(512, 2048))
    qr = wq[:, :, 128:]
    w_qr = np.ascontiguousarray(qr.reshape(512, 1024))
    w_qrs = np.ascontiguousarray(np.concatenate([qr[:, :, 32:], qr[:, :, :32]], axis=2).reshape(512, 1024))
    consts = make_consts()
    shared = dict(x_all=x2, pos_all=pos, g_pre=np.asarray(pre_norm_g, np.float32).reshape(1, D), w_lat=w_lat,
                  g_kv=np.asarray(kv_a_norm_g, np.float32), w_k=w_k, w_v=w_v, w_in=np.ascontiguousarray(w_in),
                  g_qa=np.asarray(q_a_norm_g, np.float32), w_qn=w_qn, w_qr=w_qr, w_qrs=w_qrs,
                  conv_w=np.ascontiguousarray(np.asarray(conv_w, np.float32)),
                  w_o_mla=np.ascontiguousarray(np.asarray(w_o_mla, np.float32)),
                  w_o_conv=np.ascontiguousarray(np.asarray(w_o_conv, np.float32)),
                  w_out=np.ascontiguousarray(np.asarray(w_out, np.float32)),
                  g_post=np.asarray(post_norm_g, np.float32).reshape(1, D), **consts)
    in_maps = []
    rows_all = []
    for c in cores:
        rows = _own_rows(c)
        rows_all.append(rows)
        x_halo = np.zeros((128, D), np.float32)
        for m in range(64):
            for t in range(2):
                g = 128 * m + 16 * c - 2 + t
                if g >= 0:
                    x_halo[2 * m + t] = x2[g]
        kk = np.arange(128)[:, None]
        jj = np.arange(16)[None, :]
        mask16 = (kk <= 16 * c + jj).astype(np.float32).astype(ml_dtypes.bfloat16)
        im = dict(shared)
        im.update(x_own=np.ascontiguousarray(x2[rows]), pos_own=np.ascontiguousarray(pos[:, rows]),
                  x_halo=x_halo, mask16=mask16)
        in_maps.append(im)
    return in_maps, rows_all


def kernel(x, positions, pre_norm_g, w_in, q_a_norm_g, w_q_b, kv_a_norm_g, w_kv_b,
           conv_w, w_o_mla, w_o_conv, w_out, post_norm_g):
    in_maps, rows_all = prep(x, positions, pre_norm_g, w_in, q_a_norm_g, w_q_b, kv_a_norm_g, w_kv_b,
                             conv_w, w_o_mla, w_o_conv, w_out, post_norm_g)
    nc = build()
    res = run_bass_kernel_spmd(nc, in_maps, core_ids=list(range(NCORES)))
    out = np.zeros((S, D), np.float32)
    for c in range(NCORES):
        out[rows_all[c]] = np.asarray(res.results[c]["out"], dtype=np.float32)
    return out.reshape(1, S, D)
```

```python
import contextlib
import math

import numpy as np
import concourse.bass as bass
import concourse.mybir as mybir
from concourse.bass_utils import run_bass_kernel_spmd

F32 = mybir.dt.float32
BF16 = mybir.dt.bfloat16
I32 = mybir.dt.int32
AF = mybir.ActivationFunctionType
ALU = mybir.AluOpType

D = 2048
S = 8192
NH = 16
EPS = 1e-6
NCORES = 8
SCALE = 1.0 / math.sqrt(192.0)
TWO_PI = 2.0 * math.pi
INLINE_WAITS = True
import os
FLAGS = os.environ.get("KFLAGS", "").split(",")


class _Op:
    __slots__ = ("eng", "fn", "deps", "kind", "chan", "sig", "needed", "lhs", "depres")


class Prog:
    ENGS = ("pe", "act", "dve", "pool", "sp")

    def __init__(self):
        self.ops = []
        self.res = {}
        self.chan_count = {}
        self.bulk = set()

    def _add(self, eng, fn, reads, writes, kind, chan=None, extra=(), lhs=None):
        op = _Op()
        op.eng, op.fn, op.kind, op.chan = eng, fn, kind, chan
        op.needed, op.sig = False, None
        op.lhs = None if lhs is None else set(lhs)
        deps = {}
        depres = {}

        def put(d, k, r):
            if d not in deps:
                deps[d] = k
                depres[d] = {r}
            else:
                depres[d].add(r)

        for r in reads:
            st = self.res.setdefault(r, [None, []])
            if st[0] is not None:
                put(st[0], "raw", r)
            if r.startswith("ps"):
                for rd in st[1]:
                    if rd.eng != eng:
                        put(rd, "raw", r)
        for w in writes:
            st = self.res.setdefault(w, [None, []])
            if st[0] is not None:
                put(st[0], "waw", w)
            for rd in st[1]:
                put(rd, "war", w)
        op.depres = depres
        for r in reads:
            self.res[r][1].append(op)
        for w in writes:
            self.res[w] = [op, []]
        final = []
        for d, k in deps.items():
            if d is op:
                continue
            if d.eng == eng and d.kind == "c" and kind == "c":
                if eng == "pe" or k == "war":
                    continue
            if kind == "d" and d.kind == "d" and d.chan == chan and chan in self.bulk:
                continue
            final.append(d)
        for d in extra:
            if d not in final:
                final.append(d)
        op.deps = final
        for d in final:
            d.needed = True
        if kind == "d":
            n = self.chan_count.get(chan, 0) + 1
            self.chan_count[chan] = n
            op.sig = (("chan", chan), 16 * n)
        self.ops.append(op)
        return op

    def op(self, eng, fn, reads=(), writes=(), extra=(), lhs=None):
        return self._add(eng, fn, list(reads), list(writes), "c", extra=extra, lhs=lhs)

    def dma(self, eng, fn, reads=(), writes=(), chan=None, bulk=False, extra=()):
        assert chan is not None
        if bulk:
            self.bulk.add(chan)
        else:
            assert chan not in self.bulk
        return self._add(eng, fn, list(reads), list(writes), "d", chan, extra=extra)

    def barrier(self):
        last = {}
        lastd = {}
        for o in self.ops:
            if o.fn is None:
                continue
            if o.kind == "d":
                lastd[o.chan] = o
            else:
                last[o.eng] = o
        for e in self.ENGS:
            deps = list(last.values()) + list(lastd.values())
            self.join(e, deps)

    def join(self, eng, ops):
        op = _Op()
        op.eng, op.fn, op.kind, op.chan, op.needed, op.sig = eng, None, "c", None, False, None
        op.lhs, op.depres = None, {}
        op.deps = list(ops)
        for d in ops:
            d.needed = True
        self.ops.append(op)
        return op

    def emit(self, nc):
        cnt = {e: 0 for e in self.ENGS}
        for op in self.ops:
            if op.kind == "c" and op.needed:
                cnt[op.eng] += 1
                op.sig = (("eng", op.eng), cnt[op.eng])
            if op.kind == "d" and op.chan in self.bulk:
                op.sig = (("chan", op.chan), 16 * self.chan_count[op.chan])
        with contextlib.ExitStack() as st:
            sems = {}
            for e in self.ENGS:
                sems[("eng", e)] = st.enter_context(nc.semaphore("s_" + e))
            for c in self.chan_count:
                sems[("chan", c)] = st.enter_context(nc.semaphore("c_" + str(c)))
            block = st.enter_context(nc.Block())

            def body(ename):
                def f(eng):
                    waited = {}
                    for op in self.ops:
                        if op.eng != ename:
                            continue
                        inline = []
                        for d in op.deps:
                            key, val = d.sig
                            if waited.get(key, 0) < val:
                                waited[key] = val
                                rs = op.depres.get(d)
                                if (ename == "pe" and INLINE_WAITS and op.lhs is not None and op.fn is not None
                                        and rs is not None and not (rs & op.lhs)):
                                    inline.append((key, val))
                                else:
                                    eng.wait_ge(sems[key], val)
                        for key, val in inline[:-1]:
                            eng.wait_ge(sems[key], val)
                        if op.fn is None:
                            continue
                        inst = op.fn(eng)
                        if inline:
                            inst._wait_ge(sems[inline[-1][0]], inline[-1][1])
                        if op.kind == "d":
                            inst.then_inc(sems[op.sig[0]], 16)
                        elif op.needed:
                            inst.then_inc(sems[op.sig[0]], 1)
                return f

            block.tensor(body("pe"))
            block.scalar(body("act"))
            block.vector(body("dve"))
            block.gpsimd(body("pool"))
            block.sync(body("sp"))


class SB:
    def __init__(self, nc, nbytes):
        self.t = nc.alloc_sbuf_tensor("sb", [128, nbytes // 2], BF16)
        self.off = 0
        self.cap = nbytes

    def alloc(self, shape, dtype):
        assert shape[0] == 128
        n = int(np.prod(shape[1:]))
        size = 4 if dtype in (F32, I32) else 2
        nb = (n * size + 63) // 64 * 64
        assert self.off + nb <= self.cap, ("SBUF overflow", self.off, nb, self.cap)
        ap = self.t[:, self.off // 2:(self.off + n * size) // 2]
        self.off += nb
        if dtype != BF16:
            ap = ap.bitcast(dtype)
        if len(shape) == 3:
            ap = ap.rearrange("p (a b) -> p a b", b=shape[2])
        elif len(shape) == 4:
            ap = ap.rearrange("p (a b c) -> p a b c", b=shape[2], c=shape[3])
        return ap

    def mark(self):
        return self.off

    def release(self, m):
        self.off = m


class Ctx:
    pass


def own_blocks(c):
    out = []
    for g in range(4):
        out += [16 * g + c, 16 * g + 15 - c]
    return out


def build(n_kv_groups=16, phases=("K", "Q", "A", "R"), dbg=False, stop=99):
    nc = bass.Bass("TRN2", target_bir_lowering=False)
    P = Prog()
    C = Ctx()
    C.nc, C.P = nc, P
    C.stop = stop
    S_all = n_kv_groups * 512
    C.S_all = S_all

    def din(name, shape, dt=F32):
        return nc.dram_tensor(name, list(shape), dt, kind="ExternalInput").ap()

    scratch_kind = "ExternalOutput" if dbg else "Internal"

    def dscr(name, shape, dt=BF16):
        return nc.dram_tensor(name, list(shape), dt, kind=scratch_kind).ap()

    C.x_all = din("x_all", [S_all, D])
    C.pos_all = din("pos_all", [1, S_all], I32)
    C.g_pre = din("g_pre", [1, D])
    C.w_lat = din("w_lat", [D, 768])
    C.g_kv = din("g_kv", [512])
    C.w_k = din("w_k", [512, 2048])
    C.w_v = din("w_v", [512, 2048])
    C.ident = din("ident", [128, 128], BF16)
    C.ones = din("ones", [128, 128], BF16)
    C.ropec = din("ropec", [128, 2])
    C.kT_d = dscr("kT_d", [NH, 128, S_all])
    C.v_d = dscr("v_d", [NH, 128, S_all // 128, 128])
    C.krT_d = dscr("krT_d", [128, S_all])
    full = any(p in phases for p in ("Q", "A", "R"))
    if full:
        C.x_own = din("x_own", [1024, D])
        C.pos_own = din("pos_own", [1, 1024], I32)
        C.x_halo = din("x_halo", [128, D])
        C.mask16 = din("mask16", [128, 16], BF16)
        C.w_in = din("w_in", [D, 15424])
        C.g_qa = din("g_qa", [512])
        C.w_qn = din("w_qn", [512, 2048])
        C.w_qr = din("w_qr", [512, 1024])
        C.w_qrs = din("w_qrs", [512, 1024])
        C.conv_w = din("conv_w", [3, D])
        C.w_o_mla = din("w_o_mla", [D, D])
        C.w_o_conv = din("w_o_conv", [D, D])
        C.w_out = din("w_out", [D, D])
        C.g_post = din("g_post", [1, D])
        okind = "ExternalOutput"
        C.out = nc.dram_tensor("out", [1024, D], F32, kind=okind).ap()
        if dbg:
            C.dbg_attn = nc.dram_tensor("dbg_attn", [128, 16, 1024], BF16, kind=okind).ap()
            C.dbg_qn = nc.dram_tensor("dbg_qn", [128, 16, 1024], BF16, kind=okind).ap()
            C.dbg_qr = nc.dram_tensor("dbg_qr", [128, 8, 1024], BF16, kind=okind).ap()
            C.dbg_gated = nc.dram_tensor("dbg_gated", [128, 16, 1024], BF16, kind=okind).ap()
            C.dbg_gc = nc.dram_tensor("dbg_gc", [128, 16, 1024], BF16, kind=okind).ap()
            C.dbg_mg = nc.dram_tensor("dbg_mg", [128, 16, 1024], BF16, kind=okind).ap()

    C.sb = SB(nc, 206 * 1024)
    C.ps = [nc.alloc_psum_tensor("ps%d" % i, [128, 512], F32) for i in range(8)]
    C.ps_i = 0

    const_setup(C)
    if "K" in phases and C.stop >= 1:
        phase_K(C, n_kv_groups)
    if full:
        sb = C.sb
        C.attn = sb.alloc([128, 16, 1024], BF16)
        m0 = sb.mark()
        C.Qn = sb.alloc([128, 16, 1024], BF16)
        C.Qr = sb.alloc([128, 8, 1024], BF16)
        P.barrier()
        if "Q" in phases:
            phase_Q(C)
        if dbg and "Q" in phases:
            C.final_ops.append(P.dma("sp", lambda e: e.dma_start(out=C.dbg_qn, in_=C.Qn), reads=["Qn%d" % h for h in range(NH)], chan="dbg"))
            C.final_ops.append(P.dma("sp", lambda e: e.dma_start(out=C.dbg_qr, in_=C.Qr), reads=["Qr%d" % h for h in range(8)], chan="dbg"))
        P.barrier()
        if "A" in phases:
            phase_A(C)
        if dbg and "A" in phases:
            C.final_ops.append(P.dma("sp", lambda e: e.dma_start(out=C.dbg_attn, in_=C.attn),
                                     reads=["attn%d" % h for h in range(NH)], chan="dbg"))
        P.barrier()
        sb.release(m0)
        if "R" in phases:
            phase_R(C)

    P.join("sp", C.final_ops)
    P.emit(nc)
    return nc


def ps_next(C):
    i = C.ps_i
    C.ps_i = (i + 1) % 8
    return i


def const_setup(C):
    nc, P, sb = C.nc, C.P, C.sb
    C.final_ops = []
    C.ident_sb = sb.alloc([128, 128], BF16)
    C.ones_sb = sb.alloc([128, 128], BF16)
    C.ropec_sb = sb.alloc([128, 2], F32)
    C.eps_sb = sb.alloc([128, 1], F32)
    C.mhalf_sb = sb.alloc([128, 1], F32)
    C.nt = [(sb.alloc([128, 1], F32), sb.alloc([128, 1], F32)) for _ in range(4)]
    C.gb = sb.alloc([128, D], F32)
    P.dma("sp", lambda e: e.dma_start(out=C.ident_sb, in_=C.ident), writes=["ident"], chan="const", bulk=True)
    P.dma("sp", lambda e: e.dma_start(out=C.ones_sb, in_=C.ones), writes=["ones"], chan="const", bulk=True)
    P.dma("sp", lambda e: e.dma_start(out=C.ropec_sb, in_=C.ropec), writes=["ropec"], chan="const", bulk=True)
    P.dma("sp", lambda e: e.dma_start(out=C.gb, in_=C.g_pre.broadcast_to([128, D])), writes=["gb"], chan="const", bulk=True)
    P.op("dve", lambda e: e.memset(C.eps_sb, EPS), writes=["eps"])
    P.op("pool", lambda e: e.memset(C.mhalf_sb, -0.5), writes=["mhalf"])


def rope_tables(C, pos_src, n, Ct, St, tmp, tag):
    P = C.P
    pi_t, a, kf, r, m = tmp[:5]
    P.dma("sp", lambda e: e.dma_start(out=pi_t, in_=pos_src.broadcast_to([128, n])),
          writes=[tag + "pi"], chan=tag + "pi")
    P.op("dve", lambda e: e.tensor_copy(out=a, in_=pi_t), reads=[tag + "pi"], writes=[tag + "a"])
    P.op("dve", lambda e: e.tensor_scalar(out=a, in0=a, scalar1=C.ropec_sb[:, 0:1], scalar2=None, op0=ALU.mult),
         reads=[tag + "a", "ropec"], writes=[tag + "a"])
    P.op("dve", lambda e: e.tensor_scalar(out=kf, in0=a, scalar1=1.0 / TWO_PI, scalar2=None, op0=ALU.mult),
         reads=[tag + "a"], writes=[tag + "kf"])
    ki = pi_t
    P.op("dve", lambda e: e.tensor_copy(out=ki, in_=kf), reads=[tag + "kf"], writes=[tag + "pi"])
    P.op("dve", lambda e: e.tensor_copy(out=kf, in_=ki), reads=[tag + "pi"], writes=[tag + "kf"])
    C1 = 6.28125
    C2 = TWO_PI - C1
    P.op("dve", lambda e: e.scalar_tensor_tensor(out=r, in0=kf, scalar=-C1, in1=a, op0=ALU.mult, op1=ALU.add),
         reads=[tag + "kf", tag + "a"], writes=[tag + "r"])
    P.op("dve", lambda e: e.scalar_tensor_tensor(out=r, in0=kf, scalar=-C2, in1=r, op0=ALU.mult, op1=ALU.add),
         reads=[tag + "kf", tag + "r"], writes=[tag + "r"])

    def wrap(x):
        P.op("dve", lambda e: e.tensor_scalar(out=m, in0=x, scalar1=math.pi, scalar2=-TWO_PI, op0=ALU.is_gt, op1=ALU.mult),
             reads=[tag + "r"], writes=[tag + "m"])
        P.op("dve", lambda e: e.tensor_tensor(out=x, in0=x, in1=m, op=ALU.add),
             reads=[tag + "r", tag + "m"], writes=[tag + "r"])
        P.op("dve", lambda e: e.tensor_scalar(out=m, in0=x, scalar1=-math.pi, scalar2=TWO_PI, op0=ALU.is_lt, op1=ALU.mult),
             reads=[tag + "r"], writes=[tag + "m"])
        P.op("dve", lambda e: e.tensor_tensor(out=x, in0=x, in1=m, op=ALU.add),
             reads=[tag + "r", tag + "m"], writes=[tag + "r"])

    wrap(r)
    r2 = tmp[5] if len(tmp) > 5 else None
    if r2 is None:
        P.op("act", lambda e: e.activation(out=St, in_=r, func=AF.Sin, scale=C.ropec_sb[:, 1:2]),
             reads=[tag + "r", "ropec"], writes=[tag + "S"])
        P.op("dve", lambda e: e.tensor_scalar(out=r, in0=r, scalar1=math.pi / 2, scalar2=None, op0=ALU.add),
             reads=[tag + "r"], writes=[tag + "r"])
        wrap(r)
        P.op("act", lambda e: e.activation(out=Ct, in_=r, func=AF.Sin),
             reads=[tag + "r"], writes=[tag + "C"])
        return None
    P.op("dve", lambda e: e.tensor_scalar(out=r2, in0=r, scalar1=math.pi / 2, scalar2=None, op0=ALU.add),
         reads=[tag + "r"], writes=[tag + "r2"])
    P.op("dve", lambda e: e.tensor_scalar(out=m, in0=r2, scalar1=math.pi, scalar2=-TWO_PI, op0=ALU.is_gt, op1=ALU.mult),
         reads=[tag + "r2"], writes=[tag + "m"])
    P.op("dve", lambda e: e.tensor_tensor(out=r2, in0=r2, in1=m, op=ALU.add),
         reads=[tag + "r2", tag + "m"], writes=[tag + "r2"])

    def act_part():
        P.op("act", lambda e: e.activation(out=St, in_=r, func=AF.Sin, scale=C.ropec_sb[:, 1:2]),
             reads=[tag + "r", "ropec"], writes=[tag + "S"])
        P.op("act", lambda e: e.activation(out=Ct, in_=r2, func=AF.Sin),
             reads=[tag + "r2"], writes=[tag + "C"])
    return act_part


def front_end1(C, x_src, slot, bufs, xslot=None, no_act=False):
    P = C.P
    xb, junk, ss, sd, rstd, xs = bufs
    sl = "fe%d" % slot
    xr = "fe%dxb" % (slot if xslot is None else xslot)
    P.dma("sp", lambda e: e.dma_start(out=xb, in_=x_src), writes=[xr], chan=xr)
    if no_act:
        P.op("dve", lambda e: e.scalar_tensor_tensor(out=xs, in0=xb, scalar=1.0, in1=xb, op0=ALU.mult, op1=ALU.mult, accum_out=ss),
             reads=[xr], writes=[sl + "xs", sl + "ss"])
        P.op("dve", lambda e: e.tensor_scalar(out=sd, in0=ss, scalar1=1.0 / D, scalar2=EPS, op0=ALU.mult, op1=ALU.add),
             reads=[sl + "ss"], writes=[sl + "sd"])
        ti, ta = C.nt[slot % 4]
        P.op("dve", lambda e: e.tensor_scalar(out=ti.bitcast(I32), in0=sd.bitcast(I32), scalar1=1, scalar2=None,
                                              op0=ALU.arith_shift_right),
             reads=[sl + "sd"], writes=[sl + "ti"])
        P.op("dve", lambda e: e.tensor_scalar(out=rstd.bitcast(I32), in0=ti.bitcast(I32), scalar1=-1.0, scalar2=float(0x5f3759df),
                                              op0=ALU.mult, op1=ALU.add),
             reads=[sl + "ti"], writes=[sl + "rstd"])
        for _ in range(3):
            P.op("dve", lambda e: e.tensor_tensor(out=ta, in0=sd, in1=rstd, op=ALU.mult),
                 reads=[sl + "sd", sl + "rstd"], writes=[sl + "ta"])
            P.op("dve", lambda e: e.tensor_tensor(out=ta, in0=ta, in1=rstd, op=ALU.mult),
                 reads=[sl + "ta", sl + "rstd"], writes=[sl + "ta"])
            P.op("dve", lambda e: e.tensor_scalar(out=ta, in0=ta, scalar1=-0.5, scalar2=1.5, op0=ALU.mult, op1=ALU.add),
                 reads=[sl + "ta"], writes=[sl + "ta"])
            P.op("dve", lambda e: e.tensor_tensor(out=rstd, in0=rstd, in1=ta, op=ALU.mult),
                 reads=[sl + "ta", sl + "rstd"], writes=[sl + "rstd"])
        P.op("dve", lambda e: e.scalar_tensor_tensor(out=xs, in0=xb, scalar=rstd, in1=C.gb, op0=ALU.mult, op1=ALU.mult),
             reads=[xr, sl + "rstd", "gb"], writes=[sl + "xs"])
        return
    P.op("act", lambda e: e.activation(out=xs, in_=xb, func=AF.Square, accum_out=ss),
         reads=[xr], writes=[sl + "xs", sl + "ss"])
    P.op("act", lambda e: e.activation(out=sd, in_=ss, func=AF.Sqrt, scale=1.0 / D, bias=C.eps_sb),
         reads=[sl + "ss", "eps"], writes=[sl + "sd"])
    P.op("dve", lambda e: e.reciprocal(out=rstd, in_=sd), reads=[sl + "sd"], writes=[sl + "rstd"])
    P.op("dve", lambda e: e.scalar_tensor_tensor(out=xs, in0=xb, scalar=rstd, in1=C.gb, op0=ALU.mult, op1=ALU.mult),
         reads=[xr, sl + "rstd", "gb"], writes=[sl + "xs"])


def front_end2(C, slot, hT_dst, hres, xs, act_only=False):
    P = C.P
    sl = "fe%d" % slot
    for q in range(4):
        b = ps_next(C)
        pb = C.ps[b][:, :].bitcast(BF16)
        for kk in range(4):
            kc = 4 * q + kk
            P.op("pe", lambda e, kc=kc, kk=kk, pb=pb: e.transpose(out=pb[:, kk * 128:(kk + 1) * 128],
                                                               in_=xs[:, kc * 128:(kc + 1) * 128], identity=C.ident_sb),
                 reads=[sl + "xs", "ident"], writes=["ps%d" % b], lhs=[sl + "xs"])
        src = pb[:, 0:512].rearrange("p (a b) -> p a b", b=128)
        dst = hT_dst[:, 4 * q:4 * q + 4, :]
        if q % 2 == 0 and not act_only:
            P.op("dve", lambda e, src=src, dst=dst: e.tensor_copy(out=dst, in_=src),
                 reads=["ps%d" % b], writes=[hres + "_%d" % q])
        else:
            P.op("act", lambda e, src=src, dst=dst: e.copy(out=dst, in_=src),
                 reads=["ps%d" % b], writes=[hres + "_%d" % q])


def front_end(C, x_src, slot, hT_dst, hres, bufs):
    front_end1(C, x_src, slot, bufs)
    front_end2(C, slot, hT_dst, hres, bufs[5])


def phase_K(C, n_groups):
    nc, P, sb = C.nc, C.P, C.sb
    mk = sb.mark()
    S_all = C.S_all
    wlat = sb.alloc([128, 16, 768], BF16)
    wk = sb.alloc([128, 4, 2048], BF16)
    wv = sb.alloc([128, 4, 2048], BF16)
    gkv = sb.alloc([128, 4], F32)
    for h2 in range(2):
        P.dma("pool", lambda e, h2=h2: e.dma_start(out=wlat[:, 8 * h2:8 * h2 + 8, :],
                                                 in_=C.w_lat[1024 * h2:1024 * (h2 + 1), :].rearrange("(kc p) n -> p kc n", p=128)),
              writes=["wlat%d" % h2], chan="wlat", bulk=True)
    P.dma("pool", lambda e: e.dma_start(out=wk, in_=C.w_k.rearrange("(kc p) n -> p kc n", p=128)), writes=["wk"], chan="wkv", bulk=True)
    P.dma("pool", lambda e: e.dma_start(out=wv, in_=C.w_v.rearrange("(kc p) n -> p kc n", p=128)), writes=["wv"], chan="wkv", bulk=True)
    P.dma("sp", lambda e: e.dma_start(out=gkv, in_=C.g_kv.rearrange("(c p) -> p c", p=128), allow_slow_non_contiguous=True), writes=["gkv"], chan="const", bulk=True)

    xb = [sb.alloc([128, D], F32) for _ in range(2)]
    ssb = [sb.alloc([128, 1], F32) for _ in range(4)]
    sdb = [sb.alloc([128, 1], F32) for _ in range(4)]
    rsb = [sb.alloc([128, 1], F32) for _ in range(4)]
    xs = [sb.alloc([128, D], BF16) for _ in range(4)]
    hT = [sb.alloc([128, 16, 512], BF16) for _ in range(2)]
    sq = sb.alloc([128, 4, 512], BF16)
    junk = None
    craw = sb.alloc([128, 4, 512], F32)
    ckvn = [sb.alloc([128, 4, 512], BF16) for _ in range(2)]
    sdk = sb.alloc([128, 512], F32)
    rk = sb.alloc([128, 512], F32)
    Ct = sb.alloc([128, 512], F32)
    St = sb.alloc([128, 512], F32)
    rtmp = [sb.alloc([128, 512], I32)] + [sb.alloc([128, 512], F32) for _ in range(4)]
    t1 = sb.alloc([128, 512], F32)
    rtmp.append(t1)
    t2 = sb.alloc([128, 512], F32)
    krt = [sb.alloc([128, 512], BF16) for _ in range(2)]
    kst = [sb.alloc([128, 8, 512], BF16) for _ in range(2)]
    vst = sb.alloc([128, 16, 4, 128], BF16)

    def fe1_blk(g, tbk):
        t0 = g * 512 + tbk * 128
        front_end1(C, C.x_all[t0:t0 + 128, :], 20 + tbk, (xb[tbk % 2], junk, ssb[tbk], sdb[tbk], rsb[tbk], xs[tbk]), xslot=30 + tbk % 2, no_act=True)

    def fe1(g):
        for tbk in range(4):
            fe1_blk(g, tbk)

    def fe2(g):
        hs = g % 2
        for tbk in range(4):
            front_end2(C, 20 + tbk, hT[hs][:, :, tbk * 128:(tbk + 1) * 128], "hT%d_%d" % (hs, tbk), xs[tbk], act_only=True)

    def latents(g):
        hs = g % 2
        t0 = g * 512
        rope_act = rope_tables(C, C.pos_all[:, t0:t0 + 512], 512, Ct, St, rtmp, "rk")
        hres = lambda kc: ["hT%d_%d_%d" % (hs, tb, kc // 4) for tb in range(4)]
        for cb in range(6):
            b = ps_next(C)
            for kc in range(16):
                P.op("pe", lambda e, b=b, cb=cb, kc=kc: e.matmul(out=C.ps[b][:, :], lhsT=wlat[:, kc, cb * 128:(cb + 1) * 128],
                                                               rhs=hT[hs][:, kc, :], start=(kc == 0), stop=(kc == 15)),
                     reads=["wlat%d" % (kc // 8)] + hres(kc), writes=["ps%d" % b], lhs=["wlat%d" % (kc // 8)])
            if cb < 4:
                P.op("act", lambda e, b=b, cb=cb: e.copy(out=craw[:, cb, :], in_=C.ps[b][:, :]),
                     reads=["ps%d" % b], writes=["craw%d" % cb])
                P.op("act", lambda e, b=b, cb=cb: e.activation(out=sq[:, cb, :], in_=C.ps[b][:, :], func=AF.Square),
                     reads=["ps%d" % b], writes=["sq%d" % cb])
            elif cb == 4:
                rope_act()
                P.op("dve", lambda e, b=b: e.tensor_tensor(out=t1, in0=C.ps[b][:, :], in1=Ct, op=ALU.mult),
                     reads=["ps%d" % b, "rkC"], writes=["rkr2"])
            else:
                P.op("dve", lambda e, b=b: e.tensor_tensor(out=t2, in0=C.ps[b][:, :], in1=St, op=ALU.mult),
                     reads=["ps%d" % b, "rkS"], writes=["t2"])
                ks = g % 2
                P.op("pool", lambda e, ks=ks: e.tensor_tensor(out=krt[ks], in0=t1, in1=t2, op=ALU.add),
                     reads=["rkr2", "t2"], writes=["krt%d" % ks])
                C.final_ops.append(
                    P.dma("pool", lambda e, ks=ks, t0=t0: e.dma_start(out=C.krT_d[:, t0:t0 + 512], in_=krt[ks]),
                          reads=["krt%d" % ks], writes=["krT_d"], chan="krst%d" % ks))

    def ckv_norm(g):
        cs = g % 2
        b = ps_next(C)
        for cb in range(4):
            P.op("pe", lambda e, b=b, cb=cb: e.matmul(out=C.ps[b][:, :], lhsT=C.ones_sb, rhs=sq[:, cb, :],
                                                    start=(cb == 0), stop=(cb == 3)),
                 reads=["ones", "sq%d" % cb], writes=["ps%d" % b], lhs=["ones"])
        P.op("act", lambda e, b=b: e.activation(out=sdk, in_=C.ps[b][:, :], func=AF.Sqrt, scale=1.0 / 512, bias=C.eps_sb),
             reads=["ps%d" % b, "eps"], writes=["sdk"])
        P.op("dve", lambda e: e.reciprocal(out=rk, in_=sdk), reads=["sdk"], writes=["rk"])
        for cb in range(4):
            P.op("dve", lambda e, cb=cb, cs=cs: e.scalar_tensor_tensor(out=ckvn[cs][:, cb, :], in0=craw[:, cb, :],
                                                                     scalar=gkv[:, cb:cb + 1], in1=rk,
                                                                     op0=ALU.mult, op1=ALU.mult),
                 reads=["craw%d" % cb, "gkv", "rk"], writes=["ckvn%d_%d" % (cs, cb)])

    def k_proj(g):
        cs = g % 2
        t0 = g * 512
        ckres = ["ckvn%d_%d" % (cs, cb) for cb in range(4)]
        for h in range(NH):
            b = ps_next(C)
            for c4 in range(4):
                P.op("pe", lambda e, b=b, h=h, c4=c4: e.matmul(out=C.ps[b][:, :], lhsT=wk[:, c4, h * 128:(h + 1) * 128],
                                                             rhs=ckvn[cs][:, c4, :], start=(c4 == 0), stop=(c4 == 3)),
                     reads=["wk", ckres[c4]], writes=["ps%d" % b], lhs=["wk"])
            half = h // 8
            if True:
                P.op("act", lambda e, b=b, h=h, half=half: e.copy(out=kst[half][:, h % 8, :], in_=C.ps[b][:, :]),
                     reads=["ps%d" % b], writes=["kst%d" % half])
            else:
                P.op("dve", lambda e, b=b, h=h, half=half: e.tensor_copy(out=kst[half][:, h % 8, :], in_=C.ps[b][:, :]),
                     reads=["ps%d" % b], writes=["kst%d" % half])
            if h % 8 == 7:
                C.final_ops.append(
                    P.dma("pool", lambda e, half=half, t0=t0: e.dma_start(
                        out=C.kT_d[8 * half:8 * half + 8, :, t0:t0 + 512].rearrange("h d t -> d h t"), in_=kst[half]),
                        reads=["kst%d" % half], writes=["kT_d"], chan="kst%d" % half))
            yield

    def v_proj(g):
        cs = g % 2
        t0 = g * 512
        ckres = ["ckvn%d_%d" % (cs, cb) for cb in range(4)]
        for tbk in range(4):
            for cg in range(4):
                b = ps_next(C)
                for c4 in range(4):
                    P.op("pe", lambda e, b=b, cg=cg, c4=c4, tbk=tbk: e.matmul(
                        out=C.ps[b][:, :], lhsT=ckvn[cs][:, c4, tbk * 128:(tbk + 1) * 128],
                        rhs=wv[:, c4, cg * 512:(cg + 1) * 512], start=(c4 == 0), stop=(c4 == 3)),
                        reads=["wv", ckres[c4]], writes=["ps%d" % b], lhs=[ckres[c4]])
                src = C.ps[b][:, :].rearrange("p (h d) -> p h d", d=128)
                dst = vst[:, 4 * cg:4 * cg + 4, tbk, :]
                if True:
                    P.op("act", lambda e, src=src, dst=dst: e.copy(out=dst, in_=src),
                         reads=["ps%d" % b], writes=["vst"])
                else:
                    P.op("dve", lambda e, src=src, dst=dst: e.tensor_copy(out=dst, in_=src),
                         reads=["ps%d" % b], writes=["vst"])
                yield
        kb0 = t0 // 128
        C.final_ops.append(
            P.dma("pool", lambda e, kb0=kb0: e.dma_start(
                out=C.v_d[:, :, kb0:kb0 + 4, :].rearrange("h p k d -> p h k d"), in_=vst),
                reads=["vst"], writes=["v_d"], chan="vst"))

    G = n_groups
    fe1(0)
    fe2(0)
    latents(0)
    ckv_norm(0)
    if G > 1:
        fe1(1)
    for g in range(G):
        if g + 1 < G:
            fe2(g + 1)
            latents(g + 1)
        cnt = 0

        def tick():
            nonlocal cnt
            cnt += 1

        if g + 2 < G:
            fe1(g + 2)
        for _ in k_proj(g):
            tick()
        if g + 1 < G:
            ckv_norm(g + 1)
        for _ in v_proj(g):
            tick()
    sb.release(mk)


def make_consts():
    import ml_dtypes
    inv = np.power(np.float32(10000.0), -np.arange(0, 64, 2, dtype=np.float32) / np.float32(64)).astype(np.float32)
    ropec = np.zeros((128, 2), np.float32)
    for p in range(128):
        ropec[p, 0] = inv[p % 32]
        ropec[p, 1] = -1.0 if (p % 64) < 32 else 1.0
    return dict(ident=np.eye(128, dtype=np.float32).astype(ml_dtypes.bfloat16),
                ones=np.ones((128, 128), np.float32).astype(ml_dtypes.bfloat16),
                ropec=ropec)


def phase_Q(C):
    P, sb = C.P, C.sb
    mk = sb.mark()
    wqa = sb.alloc([128, 16, 512], BF16)
    wqn = sb.alloc([128, 4, 2048], BF16)
    wqr = sb.alloc([128, 4, 1024], BF16)
    wqrs = sb.alloc([128, 4, 1024], BF16)
    gqa = sb.alloc([128, 4], F32)
    P.dma("pool", lambda e: e.dma_start(out=wqa, in_=C.w_in[:, 0:512].rearrange("(kc p) n -> p kc n", p=128)),
          writes=["wqa"], chan="wq", bulk=True)
    P.dma("pool", lambda e: e.dma_start(out=wqn, in_=C.w_qn.rearrange("(kc p) n -> p kc n", p=128)), writes=["wqn"], chan="wq", bulk=True)
    P.dma("pool", lambda e: e.dma_start(out=wqr, in_=C.w_qr.rearrange("(kc p) n -> p kc n", p=128)), writes=["wqr"], chan="wq", bulk=True)
    P.dma("pool", lambda e: e.dma_start(out=wqrs, in_=C.w_qrs.rearrange("(kc p) n -> p kc n", p=128)), writes=["wqrs"], chan="wq", bulk=True)
    P.dma("sp", lambda e: e.dma_start(out=gqa, in_=C.g_qa.rearrange("(c p) -> p c", p=128), allow_slow_non_contiguous=True),
          writes=["gqa"], chan="constQ", bulk=True)
    xb = sb.alloc([128, D], F32)
    ssb = sb.alloc([128, 1], F32)
    sdb = sb.alloc([128, 1], F32)
    rsb = sb.alloc([128, 1], F32)
    xs = sb.alloc([128, D], BF16)
    hTq = sb.alloc([128, 16, 512], BF16)
    qraw = sb.alloc([128, 4, 512], F32)
    sq = sb.alloc([128, 4, 512], BF16)
    junk = None
    qan = sb.alloc([128, 4, 512], BF16)
    sdq = sb.alloc([128, 512], F32)
    rq = sb.alloc([128, 512], F32)
    Ct = sb.alloc([128, 512], F32)
    St = sb.alloc([128, 512], F32)
    rtmp = [sb.alloc([128, 512], I32)] + [sb.alloc([128, 512], F32) for _ in range(4)]
    t1 = sb.alloc([128, 512], F32)
    t2 = sb.alloc([128, 512], F32)

    def q_half(hf):
        t0 = 512 * hf
        for tbk in range(4):
            front_end(C, C.x_own[t0 + tbk * 128:t0 + (tbk + 1) * 128, :], 7,
                      hTq[:, :, tbk * 128:(tbk + 1) * 128], "hq_%d" % tbk, (xb, junk, ssb, sdb, rsb, xs))
        rope_tables(C, C.pos_own[:, t0:t0 + 512], 512, Ct, St, rtmp, "rq")
        hres = lambda kc: ["hq_%d_%d" % (tb, kc // 4) for tb in range(4)]
        for cb in range(4):
            b = ps_next(C)
            for kc in range(16):
                P.op("pe", lambda e, b=b, cb=cb, kc=kc: e.matmul(out=C.ps[b][:, :], lhsT=wqa[:, kc, cb * 128:(cb + 1) * 128],
                                                               rhs=hTq[:, kc, :], start=(kc == 0), stop=(kc == 15)),
                     reads=["wqa"] + hres(kc), writes=["ps%d" % b], lhs=["wqa"])
            P.op("dve", lambda e, b=b, cb=cb: e.tensor_copy(out=qraw[:, cb, :], in_=C.ps[b][:, :]),
                 reads=["ps%d" % b], writes=["qraw%d" % cb])
            P.op("act", lambda e, cb=cb: e.activation(out=sq[:, cb, :], in_=qraw[:, cb, :], func=AF.Square),
                 reads=["qraw%d" % cb], writes=["qsq%d" % cb])
        b = ps_next(C)
        for cb in range(4):
            P.op("pe", lambda e, b=b, cb=cb: e.matmul(out=C.ps[b][:, :], lhsT=C.ones_sb, rhs=sq[:, cb, :],
                                                    start=(cb == 0), stop=(cb == 3)),
                 reads=["ones", "qsq%d" % cb], writes=["ps%d" % b], lhs=["ones"])
        P.op("act", lambda e, b=b: e.activation(out=sdq, in_=C.ps[b][:, :], func=AF.Sqrt, scale=1.0 / 512, bias=C.eps_sb),
             reads=["ps%d" % b, "eps"], writes=["sdq"])
        P.op("dve", lambda e: e.reciprocal(out=rq, in_=sdq), reads=["sdq"], writes=["rq"])
        for cb in range(4):
            P.op("dve", lambda e, cb=cb: e.scalar_tensor_tensor(out=qan[:, cb, :], in0=qraw[:, cb, :], scalar=gqa[:, cb:cb + 1],
                                                             in1=rq, op0=ALU.mult, op1=ALU.mult),
                 reads=["qraw%d" % cb, "gqa", "rq"], writes=["qan%d" % cb])
        qres = ["qan%d" % cb for cb in range(4)]
        for h in range(NH):
            b = ps_next(C)
            for c4 in range(4):
                P.op("pe", lambda e, b=b, h=h, c4=c4: e.matmul(out=C.ps[b][:, :], lhsT=wqn[:, c4, h * 128:(h + 1) * 128],
                                                             rhs=qan[:, c4, :], start=(c4 == 0), stop=(c4 == 3)),
                     reads=["wqn", qres[c4]], writes=["ps%d" % b], lhs=["wqn"])
            if h % 2 == 0:
                P.op("act", lambda e, b=b, h=h: e.copy(out=C.Qn[:, h, t0:t0 + 512], in_=C.ps[b][:, :]),
                     reads=["ps%d" % b], writes=["Qn%d" % h])
            else:
                P.op("dve", lambda e, b=b, h=h: e.tensor_copy(out=C.Qn[:, h, t0:t0 + 512], in_=C.ps[b][:, :]),
                     reads=["ps%d" % b], writes=["Qn%d" % h])
        for hp in range(8):
            b1 = ps_next(C)
            for c4 in range(4):
                P.op("pe", lambda e, b1=b1, hp=hp, c4=c4: e.matmul(out=C.ps[b1][:, :], lhsT=wqr[:, c4, hp * 128:(hp + 1) * 128],
                                                                 rhs=qan[:, c4, :], start=(c4 == 0), stop=(c4 == 3)),
                     reads=["wqr", qres[c4]], writes=["ps%d" % b1], lhs=["wqr"])
            P.op("dve", lambda e, b1=b1: e.tensor_tensor(out=t1, in0=C.ps[b1][:, :], in1=Ct, op=ALU.mult),
                 reads=["ps%d" % b1, "rqC"], writes=["qt1"])
            b2 = ps_next(C)
            for c4 in range(4):
                P.op("pe", lambda e, b2=b2, hp=hp, c4=c4: e.matmul(out=C.ps[b2][:, :], lhsT=wqrs[:, c4, hp * 128:(hp + 1) * 128],
                                                                 rhs=qan[:, c4, :], start=(c4 == 0), stop=(c4 == 3)),
                     reads=["wqrs", qres[c4]], writes=["ps%d" % b2], lhs=["wqrs"])
            P.op("dve", lambda e, b2=b2: e.tensor_tensor(out=t2, in0=C.ps[b2][:, :], in1=St, op=ALU.mult),
                 reads=["ps%d" % b2, "rqS"], writes=["qt2"])
            P.op("pool", lambda e, hp=hp: e.tensor_tensor(out=C.Qr[:, hp, t0:t0 + 512], in0=t1, in1=t2, op=ALU.add),
                 reads=["qt1", "qt2"], writes=["Qr%d" % hp])

    for hf in range(2):
        q_half(hf)
    sb.release(mk)


def phase_A(C):
    P, sb = C.P, C.sb
    mk = sb.mark()
    S_all = C.S_all
    nkb = S_all // 128
    nq = nkb // 16
    KrT = [sb.alloc([128, S_all], BF16) for _ in range(2)]
    kring = [sb.alloc([128, 2048], BF16) for _ in range(4)]
    vring = [sb.alloc([128, 16, 128], BF16) for _ in range(4)]
    NPT = 6
    Pt = [sb.alloc([128, 512], BF16) for _ in range(NPT)]
    negm = sb.alloc([128, 16], BF16)
    Of = sb.alloc([128, 1024], F32)
    rl = sb.alloc([128, 1024], F32)
    accS = sb.alloc([128, 1024], F32)
    onesf = sb.alloc([128, 128], F32)
    stores = list(C.final_ops)
    P.op("pool", lambda e: e.memset(onesf, 1.0), writes=["onesf"])
    P.op("pool", lambda e: e.memset(KrT[0][64:128, :], 0.0), writes=["KrT0z"])
    P.op("pool", lambda e: e.memset(KrT[1][0:64, :], 0.0), writes=["KrT1z"])
    P.dma("sp", lambda e: e.dma_start(out=negm, in_=C.mask16), writes=["negm"], chan="constA", bulk=True)
    P.dma("sp", lambda e: e.dma_start(out=KrT[0][0:64, :], in_=C.krT_d[0:64, :]), writes=["KrT0"], chan="constA", bulk=True, extra=stores)
    P.dma("sp", lambda e: e.dma_start(out=KrT[1][64:128, :], in_=C.krT_d[64:128, :]), writes=["KrT1"], chan="constA", bulk=True, extra=stores)

    def load_q(i):
        h, q = divmod(i, nq)
        s = i % 4
        P.dma("sp", lambda e: e.dma_start(out=kring[s], in_=C.kT_d[h, :, 2048 * q:2048 * (q + 1)]),
              writes=["kq%d" % s], chan="kq%d" % s, extra=stores)
        P.dma("sp", lambda e: e.dma_start(out=vring[s], in_=C.v_d[h, :, 16 * q:16 * (q + 1), :]),
              writes=["vq%d" % s], chan="vq%d" % s, extra=stores)

    nload = NH * nq
    for i in range(min(4, nload)):
        load_q(i)

    units = []
    for h in range(NH):
        for kb in range(nkb):
            for half in range(2):
                lo = max(16 * kb, 512 * half)
                hi = 512 * (half + 1)
                if lo < hi:
                    units.append((h, kb, half, lo, hi))
    last_kb = {0: min(nkb - 1, 31), 1: nkb - 1}
    sbank = [5, 6, 7]
    NSB = 3
    ACCB = [3, 4]
    pool_cnt = {}

    def emit_S(u, ui):
        h, kb, half, lo, hi = u
        n = hi - lo
        b = sbank[ui % NSB]
        s = (h * nq + kb // 16) % 4
        kk = (kb % 16) * 128
        par = h % 2
        P.op("pe", lambda e: e.matmul(out=C.ps[b][:, 0:n], lhsT=kring[s][:, kk:kk + 128], rhs=C.Qn[:, h, lo:hi],
                                      start=True, stop=False),
             reads=["kq%d" % s, "Qn%d" % h], writes=["ps%d" % b], lhs=["kq%d" % s])
        P.op("pe", lambda e: e.matmul(out=C.ps[b][:, 0:n], lhsT=KrT[par][:, kb * 128:(kb + 1) * 128],
                                      rhs=C.Qr[:, h // 2, lo:hi], start=False, stop=True),
             reads=["KrT%d" % par, "KrT%dz" % par, "Qr%d" % (h // 2)], writes=["ps%d" % b], lhs=["KrT%d" % par, "KrT%dz" % par])

    def emit_PV(u, ui):
        h, kb, half, lo, hi = u
        n = hi - lo
        b = sbank[ui % NSB]
        pt = Pt[ui % NPT]
        pres = "Pt%d" % (ui % NPT)
        s = (h * nq + kb // 16) % 4
        hp = h % 2
        P.op("act", lambda e: e.activation(out=pt[:, 0:n], in_=C.ps[b][:, 0:n], func=AF.Exp, scale=SCALE),
             reads=["ps%d" % b], writes=[pres])
        if lo == 16 * kb:
            P.op("pool", lambda e: e.tensor_tensor(out=pt[:, 0:16], in0=pt[:, 0:16], in1=negm, op=ALU.mult),
                 reads=[pres, "negm"], writes=[pres])
        c0 = lo - 512 * half
        first = (kb == 0)
        last = (kb == last_kb[half])
        ob, lb = half, 2
        P.op("pe", lambda e: e.matmul(out=C.ps[ob][:, c0:c0 + n], lhsT=vring[s][:, kb % 16, :], rhs=pt[:, 0:n],
                                      start=first, stop=last),
             reads=["vq%d" % s, pres], writes=["ps%d" % ob], lhs=["vq%d" % s])
        ab = ACCB[half]
        if first:
            P.op("dve", lambda e: e.tensor_copy(out=C.ps[ab][:, c0:c0 + n], in_=pt[:, 0:n]),
                 reads=[pres], writes=["ps%d" % ab])
        else:
            P.op("dve", lambda e: e.tensor_tensor(out=C.ps[ab][:, c0:c0 + n], in0=C.ps[ab][:, c0:c0 + n], in1=pt[:, 0:n], op=ALU.add),
                 reads=[pres, "ps%d" % ab], writes=["ps%d" % ab])
        if last:
            hs_ = slice(512 * half, 512 * (half + 1))
            P.op("dve", lambda e: e.tensor_copy(out=accS[:, hs_], in_=C.ps[ab][:, :]), reads=["ps%d" % ab], writes=["accS%d" % half])
            P.op("pe", lambda e: e.matmul(out=C.ps[lb][:, :], lhsT=onesf, rhs=accS[:, hs_], start=True, stop=True),
                 reads=["onesf", "accS%d" % half], writes=["ps%d" % lb])
            P.op("dve", lambda e: e.reciprocal(out=rl[:, hs_], in_=C.ps[lb][:, :]), reads=["ps%d" % lb], writes=["rl%d" % half])
            P.op("dve", lambda e: e.tensor_tensor(out=C.attn[:, h, hs_], in0=C.ps[ob][:, :], in1=rl[:, hs_], op=ALU.mult),
                 reads=["ps%d" % ob, "rl%d" % half], writes=["attn%d" % h])
        if kb % 16 == 15 and half == 1:
            i = h * nq + kb // 16
            if i + 4 < nload:
                load_q(i + 4)

    LOOK = 2
    nu = len(units)
    for ui in range(min(LOOK, nu)):
        emit_S(units[ui], ui)
    for ui in range(nu):
        if ui + LOOK < nu:
            emit_S(units[ui + LOOK], ui + LOOK)
        emit_PV(units[ui], ui)
    sb.release(mk)


OFF_Z, OFF_CIN, OFF_BG, OFF_CG, OFF_ZC, OFF_GM, OFF_GC = 1088, 3136, 5184, 7232, 9280, 11328, 13376


def phase_R(C):
    P, sb = C.P, C.sb
    mg = sb.alloc([128, 16, 1024], BF16)
    mR = sb.mark()
    hT = sb.alloc([128, 16, 1024], BF16)
    hTh = sb.alloc([128, 16, 128], BF16)
    gc = sb.alloc([128, 16, 1024], BF16)
    wr = [sb.alloc([128, 16, 256], BF16) for _ in range(4)]
    convw = sb.alloc([128, 3, 16], F32)
    for k in range(3):
        P.dma("sp", lambda e, k=k: e.dma_start(out=convw[:, k, :], in_=C.conv_w[k].rearrange("(j p) -> p j", p=128),
                                             allow_slow_non_contiguous=True),
              writes=["convw"], chan="constR", bulk=True)
    mk = sb.mark()
    fbuf = []
    for _ in range(2):
        fbuf.append((sb.alloc([128, D], F32), None, sb.alloc([128, 1], F32), sb.alloc([128, 1], F32),
                     sb.alloc([128, 1], F32), sb.alloc([128, D], BF16)))
    srcs = [C.x_own[blk * 128:(blk + 1) * 128, :] for blk in range(8)] + [C.x_halo]
    dsts = [hT[:, :, blk * 128:(blk + 1) * 128] for blk in range(8)] + [hTh[:, :, :]]
    front_end1(C, srcs[0], 40, fbuf[0])
    for blk in range(9):
        if blk + 1 < 9:
            front_end1(C, srcs[blk + 1], 40 + (blk + 1) % 2, fbuf[(blk + 1) % 2])
        front_end2(C, 40 + blk % 2, dsts[blk], "hr_%d" % blk, fbuf[blk % 2][5])
    hres = lambda kc, half: ["hr_%d_%d" % (4 * half + tb, kc // 4) for tb in range(4)]
    hhres = lambda kc: ["hr_8_%d" % (kc // 4)]
    sb.release(mk)

    tiles = []
    for jp in range(8):
        tiles.append(C.w_in[:, OFF_Z + 256 * jp:OFF_Z + 256 * (jp + 1)])
    for jp in range(8):
        for off in (OFF_CIN, OFF_CG, OFF_BG, OFF_ZC):
            tiles.append(C.w_in[:, off + 256 * jp:off + 256 * (jp + 1)])
    for jp in range(8):
        tiles.append(C.w_o_mla[:, 256 * jp:256 * (jp + 1)])
        tiles.append(C.w_in[:, OFF_GM + 256 * jp:OFF_GM + 256 * (jp + 1)])
        tiles.append(C.w_in[:, OFF_GC + 256 * jp:OFF_GC + 256 * (jp + 1)])
        tiles.append(C.w_o_conv[:, 256 * jp:256 * (jp + 1)])
    st = {"next": 0}

    def issue(n=1):
        for _ in range(n):
            i = st["next"]
            if i >= len(tiles):
                return
            st["next"] = i + 1
            s_ = i % 4
            src = tiles[i].rearrange("(kc p) n -> p kc n", p=128)
            P.dma("pool", lambda e, s_=s_, src=src: e.dma_start(out=wr[s_], in_=src), writes=["wr%d" % s_], chan="wr%d" % s_)

    ti = {"i": 0}

    def take():
        i = ti["i"]
        ti["i"] = i + 1
        while st["next"] <= min(i + 3, len(tiles) - 1):
            issue(1)
        return wr[i % 4], "wr%d" % (i % 4)

    def mm(w, wres, sub, rhs_fn, rres_fn, n):
        b = ps_next(C)
        for kc in range(16):
            P.op("pe", lambda e, b=b, kc=kc: e.matmul(out=C.ps[b][:, 0:n], lhsT=w[:, kc, sub * 128:(sub + 1) * 128],
                                                    rhs=rhs_fn(kc), start=(kc == 0), stop=(kc == 15)),
                 reads=[wres] + rres_fn(kc), writes=["ps%d" % b], lhs=[wres])
        return b

    tA = sb.alloc([128, 2, 1024], F32)
    tB = sb.alloc([128, 2, 1024], F32)
    U = sb.alloc([128, 2, 64 * 18], F32)
    cinh = sb.alloc([128, 2, 128], F32)
    halves = [slice(0, 512), slice(512, 1024)]

    def own_mm(w, wres, sub, half):
        hsl = halves[half]
        return mm(w, wres, sub, lambda kc: hT[:, kc, hsl], lambda kc: hres(kc, half), 512)

    def r1(jp):
        w, wres = take()
        for sub in range(2):
            j = 2 * jp + sub
            for half in range(2):
                hsl = halves[half]
                b = own_mm(w, wres, sub, half)
                P.op("act", lambda e, b=b, hsl=hsl, sub=sub: e.activation(out=tA[:, sub, hsl], in_=C.ps[b][:, :], func=AF.Silu),
                     reads=["ps%d" % b], writes=["tA%d_%d" % (sub, half)])
                P.op("dve", lambda e, j=j, hsl=hsl, sub=sub: e.tensor_tensor(out=C.attn[:, j, hsl], in0=C.attn[:, j, hsl],
                                                                         in1=tA[:, sub, hsl], op=ALU.mult),
                     reads=["tA%d_%d" % (sub, half), "attn%d" % j], writes=["attn%d" % j])

    for jp in range(8):
        r1(jp)

    def Uv(sub):
        return U[:, sub, :].rearrange("p (m j) -> p m j", j=18)

    def r2(jp):
        w, wres = take()
        for sub in range(2):
            for half in range(2):
                hsl = halves[half]
                b = own_mm(w, wres, sub, half)
                P.op("act", lambda e, b=b, hsl=hsl, sub=sub: e.copy(out=tA[:, sub, hsl], in_=C.ps[b][:, :]),
                     reads=["ps%d" % b], writes=["tA%d_%d" % (sub, half)])
            b = mm(w, wres, sub, lambda kc: hTh[:, kc, :], hhres, 128)
            P.op("act", lambda e, b=b, sub=sub: e.copy(out=cinh[:, sub, :], in_=C.ps[b][:, 0:128]),
                 reads=["ps%d" % b], writes=["cinh%d" % sub])
        w, wres = take()
        for sub in range(2):
            j = 2 * jp + sub
            for half in range(2):
                hsl = halves[half]
                b = own_mm(w, wres, sub, half)
                P.op("dve", lambda e, b=b, half=half, hsl=hsl, sub=sub: e.tensor_tensor(
                    out=Uv(sub)[:, 32 * half:32 * (half + 1), 2:18], in0=C.ps[b][:, :].rearrange("p (m j) -> p m j", j=16),
                    in1=tA[:, sub, hsl].rearrange("p (m j) -> p m j", j=16), op=ALU.mult),
                    reads=["ps%d" % b, "tA%d_%d" % (sub, half)], writes=["Uo%d_%d" % (sub, half)])
            b = mm(w, wres, sub, lambda kc: hTh[:, kc, :], hhres, 128)
            P.op("dve", lambda e, b=b, sub=sub: e.tensor_tensor(out=Uv(sub)[:, :, 0:2],
                                                             in0=C.ps[b][:, 0:128].rearrange("p (m j) -> p m j", j=2),
                                                             in1=cinh[:, sub, :].rearrange("p (m j) -> p m j", j=2), op=ALU.mult),
                 reads=["ps%d" % b, "cinh%d" % sub], writes=["Uh%d" % sub])
            tB3 = tB[:, sub, :].rearrange("p (m j) -> p m j", j=16)
            ures = ["Uo%d_0" % sub, "Uo%d_1" % sub, "Uh%d" % sub]
            tres = "tB%d" % sub
            P.op("dve", lambda e, j=j, sub=sub, tB3=tB3: e.tensor_scalar(out=tB3, in0=Uv(sub)[:, :, 0:16], scalar1=convw[:, 0, j:j + 1],
                                                                     scalar2=None, op0=ALU.mult),
                 reads=ures + ["convw"], writes=[tres])
            P.op("dve", lambda e, j=j, sub=sub, tB3=tB3: e.scalar_tensor_tensor(out=tB3, in0=Uv(sub)[:, :, 1:17], scalar=convw[:, 1, j:j + 1],
                                                                            in1=tB3, op0=ALU.mult, op1=ALU.add),
                 reads=ures + ["convw", tres], writes=[tres])
            P.op("dve", lambda e, j=j, sub=sub, tB3=tB3: e.scalar_tensor_tensor(out=tB3, in0=Uv(sub)[:, :, 2:18], scalar=convw[:, 2, j:j + 1],
                                                                            in1=tB3, op0=ALU.mult, op1=ALU.add),
                 reads=ures + ["convw", tres], writes=[tres])
        w, wres = take()
        for sub in range(2):
            for half in range(2):
                hsl = halves[half]
                b = own_mm(w, wres, sub, half)
                P.op("dve", lambda e, b=b, hsl=hsl, sub=sub: e.tensor_tensor(out=tB[:, sub, hsl], in0=C.ps[b][:, :], in1=tB[:, sub, hsl], op=ALU.mult),
                     reads=["ps%d" % b, "tB%d" % sub], writes=["tB%d" % sub])
        w, wres = take()
        for sub in range(2):
            j = 2 * jp + sub
            for half in range(2):
                hsl = halves[half]
                b = own_mm(w, wres, sub, half)
                P.op("act", lambda e, b=b, hsl=hsl, sub=sub: e.activation(out=tA[:, sub, hsl], in_=C.ps[b][:, :], func=AF.Silu),
                     reads=["ps%d" % b], writes=["tA%d_%d" % (sub, half)])
                P.op("pool", lambda e, j=j, hsl=hsl, sub=sub: e.tensor_tensor(out=gc[:, j, hsl], in0=tB[:, sub, hsl], in1=tA[:, sub, hsl], op=ALU.mult),
                     reads=["tB%d" % sub, "tA%d_%d" % (sub, half)], writes=["gc%d" % j])

    for jp in range(8):
        r2(jp)

    allattn = ["attn%d" % j for j in range(16)]
    allgc = ["gc%d" % j for j in range(16)]

    def r3(jp):
        w, wres = take()
        for sub in range(2):
            for half in range(2):
                hsl = halves[half]
                b = mm(w, wres, sub, lambda kc, hsl=hsl: C.attn[:, kc, hsl], lambda kc: ["attn%d" % kc], 512)
                P.op("act", lambda e, b=b, hsl=hsl, sub=sub: e.copy(out=tB[:, sub, hsl], in_=C.ps[b][:, :]),
                     reads=["ps%d" % b], writes=["tB%d" % sub])
        w, wres = take()
        for sub in range(2):
            for half in range(2):
                hsl = halves[half]
                b = own_mm(w, wres, sub, half)
                P.op("act", lambda e, b=b, hsl=hsl, sub=sub: e.activation(out=tA[:, sub, hsl], in_=C.ps[b][:, :], func=AF.Sigmoid),
                     reads=["ps%d" % b], writes=["tA%d_%d" % (sub, half)])
                P.op("dve", lambda e, hsl=hsl, sub=sub: e.tensor_tensor(out=tB[:, sub, hsl], in0=tB[:, sub, hsl], in1=tA[:, sub, hsl], op=ALU.mult),
                     reads=["tB%d" % sub, "tA%d_%d" % (sub, half)], writes=["tB%d" % sub])
        w, wres = take()
        for sub in range(2):
            for half in range(2):
                hsl = halves[half]
                b = own_mm(w, wres, sub, half)
                P.op("act", lambda e, b=b, hsl=hsl, sub=sub: e.activation(out=tA[:, sub, hsl], in_=C.ps[b][:, :], func=AF.Sigmoid),
                     reads=["ps%d" % b], writes=["tA%d_%d" % (sub, half)])
        w, wres = take()
        for sub in range(2):
            j = 2 * jp + sub
            for half in range(2):
                hsl = halves[half]
                b = mm(w, wres, sub, lambda kc, hsl=hsl: gc[:, kc, hsl], lambda kc: ["gc%d" % kc], 512)
                P.op("dve", lambda e, b=b, hsl=hsl, sub=sub: e.tensor_tensor(out=tA[:, sub, hsl], in0=C.ps[b][:, :], in1=tA[:, sub, hsl], op=ALU.mult),
                     reads=["ps%d" % b, "tA%d_%d" % (sub, half)], writes=["tA%d_%d" % (sub, half)])
                P.op("pool", lambda e, j=j, hsl=hsl, sub=sub: e.tensor_tensor(out=mg[:, j, hsl], in0=tB[:, sub, hsl], in1=tA[:, sub, hsl], op=ALU.add),
                     reads=["tB%d" % sub, "tA%d_%d" % (sub, half)], writes=["mg%d" % j])

    for jp in range(8):
        r3(jp)

    if hasattr(C, "dbg_mg"):
        C.final_ops.append(P.dma("sp", lambda e: e.dma_start(out=C.dbg_gated, in_=C.attn), reads=allattn, chan="dbg"))
        C.final_ops.append(P.dma("sp", lambda e: e.dma_start(out=C.dbg_gc, in_=gc), reads=allgc, chan="dbg"))
        C.final_ops.append(P.dma("sp", lambda e: e.dma_start(out=C.dbg_mg, in_=mg), reads=["mg%d" % j for j in range(16)], chan="dbg"))
    P.barrier()
    sb.release(mR)
    wout = sb.alloc([128, 16, 2048], BF16)
    gpost = sb.alloc([128, D], F32)
    xr = [sb.alloc([128, D], F32) for _ in range(2)]
    ot = [sb.alloc([128, D], F32) for _ in range(2)]
    ssq = sb.alloc([128, 4], F32)
    sst = sb.alloc([128, 1], F32)
    sdo = sb.alloc([128, 1], F32)
    rso = sb.alloc([128, 1], F32)
    junk4 = sb.alloc([128, 512], BF16)
    for cg in range(4):
        P.dma("pool", lambda e, cg=cg: e.dma_start(out=wout[:, :, 512 * cg:512 * (cg + 1)],
                                                 in_=C.w_out[:, 512 * cg:512 * (cg + 1)].rearrange("(kc p) n -> p kc n", p=128)),
              writes=["wout%d" % cg], chan="wout%d" % cg)
    P.dma("sp", lambda e: e.dma_start(out=gpost, in_=C.g_post.broadcast_to([128, D])), writes=["gpost"], chan="constR4", bulk=True)
    allmg = ["mg%d" % j for j in range(16)]

    def r4(blk):
        s_ = blk % 2
        tsl = slice(128 * blk, 128 * (blk + 1))
        P.dma("sp", lambda e: e.dma_start(out=xr[s_], in_=C.x_own[tsl, :]), writes=["xr%d" % s_], chan="xr%d" % s_)
        banks = []
        for cg in range(4):
            b = ps_next(C)
            banks.append(b)
            for kc in range(16):
                P.op("pe", lambda e, b=b, kc=kc, cg=cg: e.matmul(out=C.ps[b][:, :], lhsT=mg[:, kc, tsl],
                                                               rhs=wout[:, kc, 512 * cg:512 * (cg + 1)],
                                                               start=(kc == 0), stop=(kc == 15)),
                     reads=["mg%d" % kc, "wout%d" % cg], writes=["ps%d" % b], lhs=["mg%d" % kc])
            P.op("act", lambda e, b=b, cg=cg: e.activation(out=junk4, in_=C.ps[b][:, :], func=AF.Square, accum_out=ssq[:, cg:cg + 1]),
                 reads=["ps%d" % b], writes=["junk4", "ssq%d" % cg])
        P.op("dve", lambda e: e.tensor_reduce(out=sst, in_=ssq, axis=mybir.AxisListType.X, op=ALU.add),
             reads=["ssq%d" % cg for cg in range(4)], writes=["sst"])
        P.op("act", lambda e: e.activation(out=sdo, in_=sst, func=AF.Sqrt, scale=1.0 / D, bias=C.eps_sb),
             reads=["sst", "eps"], writes=["sdo"])
        P.op("dve", lambda e: e.reciprocal(out=rso, in_=sdo), reads=["sdo"], writes=["rso"])
        for cg in range(4):
            b = banks[cg]
            csl = slice(512 * cg, 512 * (cg + 1))
            P.op("dve", lambda e, b=b, csl=csl: e.scalar_tensor_tensor(out=ot[s_][:, csl], in0=C.ps[b][:, :], scalar=rso,
                                                                     in1=gpost[:, csl], op0=ALU.mult, op1=ALU.mult),
                 reads=["ps%d" % b, "rso", "gpost"], writes=["ot%d_%d" % (s_, cg)])
            P.op("pool", lambda e, csl=csl: e.tensor_tensor(out=ot[s_][:, csl], in0=ot[s_][:, csl], in1=xr[s_][:, csl], op=ALU.add),
                 reads=["ot%d_%d" % (s_, cg), "xr%d" % s_], writes=["ot%d_%d" % (s_, cg)])
        C.final_ops.append(
            P.dma("sp", lambda e: e.dma_start(out=C.out[tsl, :], in_=ot[s_]),
                  reads=["ot%d_%d" % (s_, cg) for cg in range(4)], writes=["out"], chan="ot%d" % s_))

    for blk in range(8):
        r4(blk)


def _own_rows(c):
    return np.concatenate([np.arange(128 * m + 16 * c, 128 * m + 16 * c + 16) for m in range(64)])


def prep(x, positions, pre_norm_g, w_in, q_a_norm_g, w_q_b, kv_a_norm_g, w_kv_b,
         conv_w, w_o_mla, w_o_conv, w_out, post_norm_g, cores=range(NCORES)):
    import ml_dtypes
    x2 = np.ascontiguousarray(np.asarray(x, dtype=np.float32).reshape(S, D))
    pos = np.ascontiguousarray(np.asarray(positions, dtype=np.int32).reshape(1, S))
    w_in = np.asarray(w_in, dtype=np.float32)
    kr = w_in[:, 1024:1088]
    krs = np.concatenate([kr[:, 32:], kr[:, :32]], axis=1)
    w_lat = np.ascontiguousarray(np.concatenate([w_in[:, 512:1024], kr, kr, krs, krs], axis=1))
    wkv = np.asarray(w_kv_b, dtype=np.float32).reshape(512, NH, 256)
    w_k = np.ascontiguousarray(wkv[:, :, :128].reshape(512, 2048))
    w_v = np.ascontiguousarray(wkv[:, :, 128:].reshape(512, 2048))
    wq = np.asarray(w_q_b, dtype=np.float32).reshape(512, NH, 192)
    w_qn = np.ascontiguousarray(wq[:, :, :128].reshape(512, 2048))
    qr = wq[:, :, 128:]
    w_qr = np.ascontiguousarray(qr.reshape(512, 1024))
    w_qrs = np.ascontiguousarray(np.concatenate([qr[:, :, 32:], qr[:, :, :32]], axis=2).reshape(512, 1024))
    consts = make_consts()
    shared = dict(x_all=x2, pos_all=pos, g_pre=np.asarray(pre_norm_g, np.float32).reshape(1, D), w_lat=w_lat,
                  g_kv=np.asarray(kv_a_norm_g, np.float32), w_k=w_k, w_v=w_v, w_in=np.ascontiguousarray(w_in),
                  g_qa=np.asarray(q_a_norm_g, np.float32), w_qn=w_qn, w_qr=w_qr, w_qrs=w_qrs,
                  conv_w=np.ascontiguousarray(np.asarray(conv_w, np.float32)),
                  w_o_mla=np.ascontiguousarray(np.asarray(w_o_mla, np.float32)),
                  w_o_conv=np.ascontiguousarray(np.asarray(w_o_conv, np.float32)),
                  w_out=np.ascontiguousarray(np.asarray(w_out, np.float32)),
                  g_post=np.asarray(post_norm_g, np.float32).reshape(1, D), **consts)
    in_maps = []
    rows_all = []
    for c in cores:
        rows = _own_rows(c)
        rows_all.append(rows)
        x_halo = np.zeros((128, D), np.float32)
        for m in range(64):
            for t in range(2):
                g = 128 * m + 16 * c - 2 + t
                if g >= 0:
                    x_halo[2 * m + t] = x2[g]
        kk = np.arange(128)[:, None]
        jj = np.arange(16)[None, :]
        mask16 = (kk <= 16 * c + jj).astype(np.float32).astype(ml_dtypes.bfloat16)
        im = dict(shared)
        im.update(x_own=np.ascontiguousarray(x2[rows]), pos_own=np.ascontiguousarray(pos[:, rows]),
                  x_halo=x_halo, mask16=mask16)
        in_maps.append(im)
    return in_maps, rows_all


def kernel(x, positions, pre_norm_g, w_in, q_a_norm_g, w_q_b, kv_a_norm_g, w_kv_b,
           conv_w, w_o_mla, w_o_conv, w_out, post_norm_g):
    in_maps, rows_all = prep(x, positions, pre_norm_g, w_in, q_a_norm_g, w_q_b, kv_a_norm_g, w_kv_b,
                             conv_w, w_o_mla, w_o_conv, w_out, post_norm_g)
    nc = build()
    res = run_bass_kernel_spmd(nc, in_maps, core_ids=list(range(NCORES)))
    out = np.zeros((S, D), np.float32)
    for c in range(NCORES):
        out[rows_all[c]] = np.asarray(res.results[c]["out"], dtype=np.float32)
    return out.reshape(1, S, D)
```

```python
import contextlib
import math

import numpy as np
import concourse.bass as bass
import concourse.mybir as mybir
from concourse.bass_utils import run_bass_kernel_spmd

F32 = mybir.dt.float32
BF16 = mybir.dt.bfloat16
I32 = mybir.dt.int32
AF = mybir.ActivationFunctionType
ALU = mybir.AluOpType

D = 2048
S = 8192
NH = 16
EPS = 1e-6
NCORES = 8
SCALE = 1.0 / math.sqrt(192.0)
TWO_PI = 2.0 * math.pi
INLINE_WAITS = True
import os
FLAGS = os.environ.get("KFLAGS", "").split(",")


class _Op:
    __slots__ = ("eng", "fn", "deps", "kind", "chan", "sig", "needed", "lhs", "depres")


class Prog:
    ENGS = ("pe", "act", "dve", "pool", "sp")

    def __init__(self):
        self.ops = []
        self.res = {}
        self.chan_count = {}
        self.bulk = set()

    def _add(self, eng, fn, reads, writes, kind, chan=None, extra=(), lhs=None):
        op = _Op()
        op.eng, op.fn, op.kind, op.chan = eng, fn, kind, chan
        op.needed, op.sig = False, None
        op.lhs = None if lhs is None else set(lhs)
        deps = {}
        depres = {}

        def put(d, k, r):
            if d not in deps:
                deps[d] = k
                depres[d] = {r}
            else:
                depres[d].add(r)

        for r in reads:
            st = self.res.setdefault(r, [None, []])
            if st[0] is not None:
                put(st[0], "raw", r)
            if r.startswith("ps"):
                for rd in st[1]:
                    if rd.eng != eng:
                        put(rd, "raw", r)
        for w in writes:
            st = self.res.setdefault(w, [None, []])
            if st[0] is not None:
                put(st[0], "waw", w)
            for rd in st[1]:
                put(rd, "war", w)
        op.depres = depres
        for r in reads:
            self.res[r][1].append(op)
        for w in writes:
            self.res[w] = [op, []]
        final = []
        for d, k in deps.items():
            if d is op:
                continue
            if d.eng == eng and d.kind == "c" and kind == "c":
                if eng == "pe" or k == "war":
                    continue
            if kind == "d" and d.kind == "d" and d.chan == chan and chan in self.bulk:
                continue
            final.append(d)
        for d in extra:
            if d not in final:
                final.append(d)
        op.deps = final
        for d in final:
            d.needed = True
        if kind == "d":
            n = self.chan_count.get(chan, 0) + 1
            self.chan_count[chan] = n
            op.sig = (("chan", chan), 16 * n)
        self.ops.append(op)
        return op

    def op(self, eng, fn, reads=(), writes=(), extra=(), lhs=None):
        return self._add(eng, fn, list(reads), list(writes), "c", extra=extra, lhs=lhs)

    def dma(self, eng, fn, reads=(), writes=(), chan=None, bulk=False, extra=()):
        assert chan is not None
        if bulk:
            self.bulk.add(chan)
        else:
            assert chan not in self.bulk
        return self._add(eng, fn, list(reads), list(writes), "d", chan, extra=extra)

    def barrier(self):
        last = {}
        lastd = {}
        for o in self.ops:
            if o.fn is None:
                continue
            if o.kind == "d":
                lastd[o.chan] = o
            else:
                last[o.eng] = o
        for e in self.ENGS:
            deps = list(last.values()) + list(lastd.values())
            self.join(e, deps)

    def join(self, eng, ops):
        op = _Op()
        op.eng, op.fn, op.kind, op.chan, op.needed, op.sig = eng, None, "c", None, False, None
        op.lhs, op.depres = None, {}
        op.deps = list(ops)
        for d in ops:
            d.needed = True
        self.ops.append(op)
        return op

    def emit(self, nc):
        cnt = {e: 0 for e in self.ENGS}
        for op in self.ops:
            if op.kind == "c" and op.needed:
                cnt[op.eng] += 1
                op.sig = (("eng", op.eng), cnt[op.eng])
            if op.kind == "d" and op.chan in self.bulk:
                op.sig = (("chan", op.chan), 16 * self.chan_count[op.chan])
        with contextlib.ExitStack() as st:
            sems = {}
            for e in self.ENGS:
                sems[("eng", e)] = st.enter_context(nc.semaphore("s_" + e))
            for c in self.chan_count:
                sems[("chan", c)] = st.enter_context(nc.semaphore("c_" + str(c)))
            block = st.enter_context(nc.Block())

            def body(ename):
                def f(eng):
                    waited = {}
                    for op in self.ops:
                        if op.eng != ename:
                            continue
                        inline = []
                        for d in op.deps:
                            key, val = d.sig
                            if waited.get(key, 0) < val:
                                waited[key] = val
                                rs = op.depres.get(d)
                                if (ename == "pe" and INLINE_WAITS and op.lhs is not None and op.fn is not None
                                        and rs is not None and not (rs & op.lhs)):
                                    inline.append((key, val))
                                else:
                                    eng.wait_ge(sems[key], val)
                        for key, val in inline[:-1]:
                            eng.wait_ge(sems[key], val)
                        if op.fn is None:
                            continue
                        inst = op.fn(eng)
                        if inline:
                            inst._wait_ge(sems[inline[-1][0]], inline[-1][1])
                        if op.kind == "d":
                            inst.then_inc(sems[op.sig[0]], 16)
                        elif op.needed:
                            inst.then_inc(sems[op.sig[0]], 1)
                return f

            block.tensor(body("pe"))
            block.scalar(body("act"))
            block.vector(body("dve"))
            block.gpsimd(body("pool"))
            block.sync(body("sp"))


class SB:
    def __init__(self, nc, nbytes):
        self.t = nc.alloc_sbuf_tensor("sb", [128, nbytes // 2], BF16)
        self.off = 0
        self.cap = nbytes

    def alloc(self, shape, dtype):
        assert shape[0] == 128
        n = int(np.prod(shape[1:]))
        size = 4 if dtype in (F32, I32) else 2
        nb = (n * size + 63) // 64 * 64
        assert self.off + nb <= self.cap, ("SBUF overflow", self.off, nb, self.cap)
        ap = self.t[:, self.off // 2:(self.off + n * size) // 2]
        self.off += nb
        if dtype != BF16:
            ap = ap.bitcast(dtype)
        if len(shape) == 3:
            ap = ap.rearrange("p (a b) -> p a b", b=shape[2])
        elif len(shape) == 4:
            ap = ap.rearrange("p (a b c) -> p a b c", b=shape[2], c=shape[3])
        return ap

    def mark(self):
        return self.off

    def release(self, m):
        self.off = m


class Ctx:
    pass


def own_blocks(c):
    out = []
    for g in range(4):
        out += [16 * g + c, 16 * g + 15 - c]
    return out


def build(n_kv_groups=16, phases=("K", "Q", "A", "R"), dbg=False, stop=99):
    nc = bass.Bass("TRN2", target_bir_lowering=False)
    P = Prog()
    C = Ctx()
    C.nc, C.P = nc, P
    C.stop = stop
    S_all = n_kv_groups * 512
    C.S_all = S_all

    def din(name, shape, dt=F32):
        return nc.dram_tensor(name, list(shape), dt, kind="ExternalInput").ap()

    scratch_kind = "ExternalOutput" if dbg else "Internal"

    def dscr(name, shape, dt=BF16):
        return nc.dram_tensor(name, list(shape), dt, kind=scratch_kind).ap()

    C.x_all = din("x_all", [S_all, D])
    C.pos_all = din("pos_all", [1, S_all], I32)
    C.g_pre = din("g_pre", [1, D])
    C.w_lat = din("w_lat", [D, 768])
    C.g_kv = din("g_kv", [512])
    C.w_k = din("w_k", [512, 2048])
    C.w_v = din("w_v", [512, 2048])
    C.ident = din("ident", [128, 128], BF16)
    C.ones = din("ones", [128, 128], BF16)
    C.ropec = din("ropec", [128, 2])
    C.kT_d = dscr("kT_d", [NH, 128, S_all])
    C.v_d = dscr("v_d", [NH, 128, S_all // 128, 128])
    C.krT_d = dscr("krT_d", [128, S_all])
    full = any(p in phases for p in ("Q", "A", "R"))
    if full:
        C.x_own = din("x_own", [1024, D])
        C.pos_own = din("pos_own", [1, 1024], I32)
        C.x_halo = din("x_halo", [128, D])
        C.mask16 = din("mask16", [128, 16], BF16)
        C.w_in = din("w_in", [D, 15424])
        C.g_qa = din("g_qa", [512])
        C.w_qn = din("w_qn", [512, 2048])
        C.w_qr = din("w_qr", [512, 1024])
        C.w_qrs = din("w_qrs", [512, 1024])
        C.conv_w = din("conv_w", [3, D])
        C.w_o_mla = din("w_o_mla", [D, D])
        C.w_o_conv = din("w_o_conv", [D, D])
        C.w_out = din("w_out", [D, D])
        C.g_post = din("g_post", [1, D])
        okind = "ExternalOutput"
        C.out = nc.dram_tensor("out", [1024, D], F32, kind=okind).ap()
        if dbg:
            C.dbg_attn = nc.dram_tensor("dbg_attn", [128, 16, 1024], BF16, kind=okind).ap()
            C.dbg_qn = nc.dram_tensor("dbg_qn", [128, 16, 1024], BF16, kind=okind).ap()
            C.dbg_qr = nc.dram_tensor("dbg_qr", [128, 8, 1024], BF16, kind=okind).ap()
            C.dbg_gated = nc.dram_tensor("dbg_gated", [128, 16, 1024], BF16, kind=okind).ap()
            C.dbg_gc = nc.dram_tensor("dbg_gc", [128, 16, 1024], BF16, kind=okind).ap()
            C.dbg_mg = nc.dram_tensor("dbg_mg", [128, 16, 1024], BF16, kind=okind).ap()

    C.sb = SB(nc, 206 * 1024)
    C.ps = [nc.alloc_psum_tensor("ps%d" % i, [128, 512], F32) for i in range(8)]
    C.ps_i = 0

    const_setup(C)
    if "K" in phases and C.stop >= 1:
        phase_K(C, n_kv_groups)
    if full:
        sb = C.sb
        C.attn = sb.alloc([128, 16, 1024], BF16)
        m0 = sb.mark()
        C.Qn = sb.alloc([128, 16, 1024], BF16)
        C.Qr = sb.alloc([128, 8, 1024], BF16)
        P.barrier()
        if "Q" in phases:
            phase_Q(C)
        if dbg and "Q" in phases:
            C.final_ops.append(P.dma("sp", lambda e: e.dma_start(out=C.dbg_qn, in_=C.Qn), reads=["Qn%d" % h for h in range(NH)], chan="dbg"))
            C.final_ops.append(P.dma("sp", lambda e: e.dma_start(out=C.dbg_qr, in_=C.Qr), reads=["Qr%d" % h for h in range(8)], chan="dbg"))
        P.barrier()
        if "A" in phases:
            phase_A(C)
        if dbg and "A" in phases:
            C.final_ops.append(P.dma("sp", lambda e: e.dma_start(out=C.dbg_attn, in_=C.attn),
                                     reads=["attn%d" % h for h in range(NH)], chan="dbg"))
        P.barrier()
        sb.release(m0)
        if "R" in phases:
            phase_R(C)

    P.join("sp", C.final_ops)
    P.emit(nc)
    return nc


def ps_next(C):
    i = C.ps_i
    C.ps_i = (i + 1) % 8
    return i


def const_setup(C):
    nc, P, sb = C.nc, C.P, C.sb
    C.final_ops = []
    C.ident_sb = sb.alloc([128, 128], BF16)
    C.ones_sb = sb.alloc([128, 128], BF16)
    C.ropec_sb = sb.alloc([128, 2], F32)
    C.eps_sb = sb.alloc([128, 1], F32)
    C.mhalf_sb = sb.alloc([128, 1], F32)
    C.nt = [(sb.alloc([128, 1], F32), sb.alloc([128, 1], F32)) for _ in range(4)]
    C.gb = sb.alloc([128, D], F32)
    P.dma("sp", lambda e: e.dma_start(out=C.ident_sb, in_=C.ident), writes=["ident"], chan="const", bulk=True)
    P.dma("sp", lambda e: e.dma_start(out=C.ones_sb, in_=C.ones), writes=["ones"], chan="const", bulk=True)
    P.dma("sp", lambda e: e.dma_start(out=C.ropec_sb, in_=C.ropec), writes=["ropec"], chan="const", bulk=True)
    P.dma("sp", lambda e: e.dma_start(out=C.gb, in_=C.g_pre.broadcast_to([128, D])), writes=["gb"], chan="const", bulk=True)
    P.op("dve", lambda e: e.memset(C.eps_sb, EPS), writes=["eps"])
    P.op("pool", lambda e: e.memset(C.mhalf_sb, -0.5), writes=["mhalf"])


def rope_tables(C, pos_src, n, Ct, St, tmp, tag):
    P = C.P
    pi_t, a, kf, r, m = tmp[:5]
    P.dma("sp", lambda e: e.dma_start(out=pi_t, in_=pos_src.broadcast_to([128, n])),
          writes=[tag + "pi"], chan=tag + "pi")
    P.op("dve", lambda e: e.tensor_copy(out=a, in_=pi_t), reads=[tag + "pi"], writes=[tag + "a"])
    P.op("dve", lambda e: e.tensor_scalar(out=a, in0=a, scalar1=C.ropec_sb[:, 0:1], scalar2=None, op0=ALU.mult),
         reads=[tag + "a", "ropec"], writes=[tag + "a"])
    P.op("dve", lambda e: e.tensor_scalar(out=kf, in0=a, scalar1=1.0 / TWO_PI, scalar2=None, op0=ALU.mult),
         reads=[tag + "a"], writes=[tag + "kf"])
    ki = pi_t
    P.op("dve", lambda e: e.tensor_copy(out=ki, in_=kf), reads=[tag + "kf"], writes=[tag + "pi"])
    P.op("dve", lambda e: e.tensor_copy(out=kf, in_=ki), reads=[tag + "pi"], writes=[tag + "kf"])
    C1 = 6.28125
    C2 = TWO_PI - C1
    P.op("dve", lambda e: e.scalar_tensor_tensor(out=r, in0=kf, scalar=-C1, in1=a, op0=ALU.mult, op1=ALU.add),
         reads=[tag + "kf", tag + "a"], writes=[tag + "r"])
    P.op("dve", lambda e: e.scalar_tensor_tensor(out=r, in0=kf, scalar=-C2, in1=r, op0=ALU.mult, op1=ALU.add),
         reads=[tag + "kf", tag + "r"], writes=[tag + "r"])

    def wrap(x):
        P.op("dve", lambda e: e.tensor_scalar(out=m, in0=x, scalar1=math.pi, scalar2=-TWO_PI, op0=ALU.is_gt, op1=ALU.mult),
             reads=[tag + "r"], writes=[tag + "m"])
        P.op("dve", lambda e: e.tensor_tensor(out=x, in0=x, in1=m, op=ALU.add),
             reads=[tag + "r", tag + "m"], writes=[tag + "r"])
        P.op("dve", lambda e: e.tensor_scalar(out=m, in0=x, scalar1=-math.pi, scalar2=TWO_PI, op0=ALU.is_lt, op1=ALU.mult),
             reads=[tag + "r"], writes=[tag + "m"])
        P.op("dve", lambda e: e.tensor_tensor(out=x, in0=x, in1=m, op=ALU.add),
             reads=[tag + "r", tag + "m"], writes=[tag + "r"])

    wrap(r)
    r2 = tmp[5] if len(tmp) > 5 else None
    if r2 is None:
        P.op("act", lambda e: e.activation(out=St, in_=r, func=AF.Sin, scale=C.ropec_sb[:, 1:2]),
             reads=[tag + "r", "ropec"], writes=[tag + "S"])
        P.op("dve", lambda e: e.tensor_scalar(out=r, in0=r, scalar1=math.pi / 2, scalar2=None, op0=ALU.add),
             reads=[tag + "r"], writes=[tag + "r"])
        wrap(r)
        P.op("act", lambda e: e.activation(out=Ct, in_=r, func=AF.Sin),
             reads=[tag + "r"], writes=[tag + "C"])
        return None
    P.op("dve", lambda e: e.tensor_scalar(out=r2, in0=r, scalar1=math.pi / 2, scalar2=None, op0=ALU.add),
         reads=[tag + "r"], writes=[tag + "r2"])
    P.op("dve", lambda e: e.tensor_scalar(out=m, in0=r2, scalar1=math.pi, scalar2=-TWO_PI, op0=ALU.is_gt, op1=ALU.mult),
         reads=[tag + "r2"], writes=[tag + "m"])
    P.op("dve", lambda e: e.tensor_tensor(out=r2, in0=r2, in1=m, op=ALU.add),
         reads=[tag + "r2", tag + "m"], writes=[tag + "r2"])

    def act_part():
        P.op("act", lambda e: e.activation(out=St, in_=r, func=AF.Sin, scale=C.ropec_sb[:, 1:2]),
             reads=[tag + "r", "ropec"], writes=[tag + "S"])
        P.op("act", lambda e: e.activation(out=Ct, in_=r2, func=AF.Sin),
             reads=[tag + "r2"], writes=[tag + "C"])
    return act_part


def front_end1(C, x_src, slot, bufs, xslot=None, no_act=False):
    P = C.P
    xb, junk, ss, sd, rstd, xs = bufs
    sl = "fe%d" % slot
    xr = "fe%dxb" % (slot if xslot is None else xslot)
    P.dma("sp", lambda e: e.dma_start(out=xb, in_=x_src), writes=[xr], chan=xr)
    if no_act:
        P.op("dve", lambda e: e.scalar_tensor_tensor(out=xs, in0=xb, scalar=1.0, in1=xb, op0=ALU.mult, op1=ALU.mult, accum_out=ss),
             reads=[xr], writes=[sl + "xs", sl + "ss"])
        P.op("dve", lambda e: e.tensor_scalar(out=sd, in0=ss, scalar1=1.0 / D, scalar2=EPS, op0=ALU.mult, op1=ALU.add),
             reads=[sl + "ss"], writes=[sl + "sd"])
        ti, ta = C.nt[slot % 4]
        P.op("dve", lambda e: e.tensor_scalar(out=ti.bitcast(I32), in0=sd.bitcast(I32), scalar1=1, scalar2=None,
                                              op0=ALU.arith_shift_right),
             reads=[sl + "sd"], writes=[sl + "ti"])
        P.op("dve", lambda e: e.tensor_scalar(out=rstd.bitcast(I32), in0=ti.bitcast(I32), scalar1=-1.0, scalar2=float(0x5f3759df),
                                              op0=ALU.mult, op1=ALU.add),
             reads=[sl + "ti"], writes=[sl + "rstd"])
        for _ in range(3):
            P.op("dve", lambda e: e.tensor_tensor(out=ta, in0=sd, in1=rstd, op=ALU.mult),
                 reads=[sl + "sd", sl + "rstd"], writes=[sl + "ta"])
            P.op("dve", lambda e: e.tensor_tensor(out=ta, in0=ta, in1=rstd, op=ALU.mult),
                 reads=[sl + "ta", sl + "rstd"], writes=[sl + "ta"])
            P.op("dve", lambda e: e.tensor_scalar(out=ta, in0=ta, scalar1=-0.5, scalar2=1.5, op0=ALU.mult, op1=ALU.add),
                 reads=[sl + "ta"], writes=[sl + "ta"])
            P.op("dve", lambda e: e.tensor_tensor(out=rstd, in0=rstd, in1=ta, op=ALU.mult),
                 reads=[sl + "ta", sl + "rstd"], writes=[sl + "rstd"])
        P.op("dve", lambda e: e.scalar_tensor_tensor(out=xs, in0=xb, scalar=rstd, in1=C.gb, op0=ALU.mult, op1=ALU.mult),
             reads=[xr, sl + "rstd", "gb"], writes=[sl + "xs"])
        return
    P.op("act", lambda e: e.activation(out=xs, in_=xb, func=AF.Square, accum_out=ss),
         reads=[xr], writes=[sl + "xs", sl + "ss"])
    P.op("act", lambda e: e.activation(out=sd, in_=ss, func=AF.Sqrt, scale=1.0 / D, bias=C.eps_sb),
         reads=[sl + "ss", "eps"], writes=[sl + "sd"])
    P.op("dve", lambda e: e.reciprocal(out=rstd, in_=sd), reads=[sl + "sd"], writes=[sl + "rstd"])
    P.op("dve", lambda e: e.scalar_tensor_tensor(out=xs, in0=xb, scalar=rstd, in1=C.gb, op0=ALU.mult, op1=ALU.mult),
         reads=[xr, sl + "rstd", "gb"], writes=[sl + "xs"])


def front_end2(C, slot, hT_dst, hres, xs, act_only=False):
    P = C.P
    sl = "fe%d" % slot
    for q in range(4):
        b = ps_next(C)
        pb = C.ps[b][:, :].bitcast(BF16)
        for kk in range(4):
            kc = 4 * q + kk
            P.op("pe", lambda e, kc=kc, kk=kk, pb=pb: e.transpose(out=pb[:, kk * 128:(kk + 1) * 128],
                                                               in_=xs[:, kc * 128:(kc + 1) * 128], identity=C.ident_sb),
                 reads=[sl + "xs", "ident"], writes=["ps%d" % b], lhs=[sl + "xs"])
        src = pb[:, 0:512].rearrange("p (a b) -> p a b", b=128)
        dst = hT_dst[:, 4 * q:4 * q + 4, :]
        if q % 2 == 0 and not act_only:
            P.op("dve", lambda e, src=src, dst=dst: e.tensor_copy(out=dst, in_=src),
                 reads=["ps%d" % b], writes=[hres + "_%d" % q])
        else:
            P.op("act", lambda e, src=src, dst=dst: e.copy(out=dst, in_=src),
                 reads=["ps%d" % b], writes=[hres + "_%d" % q])


def front_end(C, x_src, slot, hT_dst, hres, bufs):
    front_end1(C, x_src, slot, bufs)
    front_end2(C, slot, hT_dst, hres, bufs[5])


def phase_K(C, n_groups):
    nc, P, sb = C.nc, C.P, C.sb
    mk = sb.mark()
    S_all = C.S_all
    wlat = sb.alloc([128, 16, 768], BF16)
    wk = sb.alloc([128, 4, 2048], BF16)
    wv = sb.alloc([128, 4, 2048], BF16)
    gkv = sb.alloc([128, 4], F32)
    for h2 in range(2):
        P.dma("pool", lambda e, h2=h2: e.dma_start(out=wlat[:, 8 * h2:8 * h2 + 8, :],
                                                 in_=C.w_lat[1024 * h2:1024 * (h2 + 1), :].rearrange("(kc p) n -> p kc n", p=128)),
              writes=["wlat%d" % h2], chan="wlat", bulk=True)
    P.dma("pool", lambda e: e.dma_start(out=wk, in_=C.w_k.rearrange("(kc p) n -> p kc n", p=128)), writes=["wk"], chan="wkv", bulk=True)
    P.dma("pool", lambda e: e.dma_start(out=wv, in_=C.w_v.rearrange("(kc p) n -> p kc n", p=128)), writes=["wv"], chan="wkv", bulk=True)
    P.dma("sp", lambda e: e.dma_start(out=gkv, in_=C.g_kv.rearrange("(c p) -> p c", p=128), allow_slow_non_contiguous=True), writes=["gkv"], chan="const", bulk=True)

    xb = [sb.alloc([128, D], F32) for _ in range(2)]
    ssb = [sb.alloc([128, 1], F32) for _ in range(4)]
    sdb = [sb.alloc([128, 1], F32) for _ in range(4)]
    rsb = [sb.alloc([128, 1], F32) for _ in range(4)]
    xs = [sb.alloc([128, D], BF16) for _ in range(4)]
    hT = [sb.alloc([128, 16, 512], BF16) for _ in range(2)]
    sq = sb.alloc([128, 4, 512], BF16)
    junk = None
    craw = sb.alloc([128, 4, 512], F32)
    ckvn = [sb.alloc([128, 4, 512], BF16) for _ in range(2)]
    sdk = sb.alloc([128, 512], F32)
    rk = sb.alloc([128, 512], F32)
    Ct = sb.alloc([128, 512], F32)
    St = sb.alloc([128, 512], F32)
    rtmp = [sb.alloc([128, 512], I32)] + [sb.alloc([128, 512], F32) for _ in range(4)]
    t1 = sb.alloc([128, 512], F32)
    rtmp.append(t1)
    t2 = sb.alloc([128, 512], F32)
    krt = [sb.alloc([128, 512], BF16) for _ in range(2)]
    kst = [sb.alloc([128, 8, 512], BF16) for _ in range(2)]
    vst = sb.alloc([128, 16, 4, 128], BF16)

    def fe1_blk(g, tbk):
        t0 = g * 512 + tbk * 128
        front_end1(C, C.x_all[t0:t0 + 128, :], 20 + tbk, (xb[tbk % 2], junk, ssb[tbk], sdb[tbk], rsb[tbk], xs[tbk]), xslot=30 + tbk % 2, no_act=True)

    def fe1(g):
        for tbk in range(4):
            fe1_blk(g, tbk)

    def fe2(g):
        hs = g % 2
        for tbk in range(4):
            front_end2(C, 20 + tbk, hT[hs][:, :, tbk * 128:(tbk + 1) * 128], "hT%d_%d" % (hs, tbk), xs[tbk], act_only=True)

    def latents(g):
        hs = g % 2
        t0 = g * 512
        rope_act = rope_tables(C, C.pos_all[:, t0:t0 + 512], 512, Ct, St, rtmp, "rk")
        hres = lambda kc: ["hT%d_%d_%d" % (hs, tb, kc // 4) for tb in range(4)]
        for cb in range(6):
            b = ps_next(C)
            for kc in range(16):
                P.op("pe", lambda e, b=b, cb=cb, kc=kc: e.matmul(out=C.ps[b][:, :], lhsT=wlat[:, kc, cb * 128:(cb + 1) * 128],
                                                               rhs=hT[hs][:, kc, :], start=(kc == 0), stop=(kc == 15)),
                     reads=["wlat%d" % (kc // 8)] + hres(kc), writes=["ps%d" % b], lhs=["wlat%d" % (kc // 8)])
            if cb < 4:
                P.op("act", lambda e, b=b, cb=cb: e.copy(out=craw[:, cb, :], in_=C.ps[b][:, :]),
                     reads=["ps%d" % b], writes=["craw%d" % cb])
                P.op("act", lambda e, b=b, cb=cb: e.activation(out=sq[:, cb, :], in_=C.ps[b][:, :], func=AF.Square),
                     reads=["ps%d" % b], writes=["sq%d" % cb])
            elif cb == 4:
                rope_act()
                P.op("dve", lambda e, b=b: e.tensor_tensor(out=t1, in0=C.ps[b][:, :], in1=Ct, op=ALU.mult),
                     reads=["ps%d" % b, "rkC"], writes=["rkr2"])
            else:
                P.op("dve", lambda e, b=b: e.tensor_tensor(out=t2, in0=C.ps[b][:, :], in1=St, op=ALU.mult),
                     reads=["ps%d" % b, "rkS"], writes=["t2"])
                ks = g % 2
                P.op("pool", lambda e, ks=ks: e.tensor_tensor(out=krt[ks], in0=t1, in1=t2, op=ALU.add),
                     reads=["rkr2", "t2"], writes=["krt%d" % ks])
                C.final_ops.append(
                    P.dma("pool", lambda e, ks=ks, t0=t0: e.dma_start(out=C.krT_d[:, t0:t0 + 512], in_=krt[ks]),
                          reads=["krt%d" % ks], writes=["krT_d"], chan="krst%d" % ks))

    def ckv_norm(g):
        cs = g % 2
        b = ps_next(C)
        for cb in range(4):
            P.op("pe", lambda e, b=b, cb=cb: e.matmul(out=C.ps[b][:, :], lhsT=C.ones_sb, rhs=sq[:, cb, :],
                                                    start=(cb == 0), stop=(cb == 3)),
                 reads=["ones", "sq%d" % cb], writes=["ps%d" % b], lhs=["ones"])
        P.op("act", lambda e, b=b: e.activation(out=sdk, in_=C.ps[b][:, :], func=AF.Sqrt, scale=1.0 / 512, bias=C.eps_sb),
             reads=["ps%d" % b, "eps"], writes=["sdk"])
        P.op("dve", lambda e: e.reciprocal(out=rk, in_=sdk), reads=["sdk"], writes=["rk"])
        for cb in range(4):
            P.op("dve", lambda e, cb=cb, cs=cs: e.scalar_tensor_tensor(out=ckvn[cs][:, cb, :], in0=craw[:, cb, :],
                                                                     scalar=gkv[:, cb:cb + 1], in1=rk,
                                                                     op0=ALU.mult, op1=ALU.mult),
                 reads=["craw%d" % cb, "gkv", "rk"], writes=["ckvn%d_%d" % (cs, cb)])

    def k_proj(g):
        cs = g % 2
        t0 = g * 512
        ckres = ["ckvn%d_%d" % (cs, cb) for cb in range(4)]
        for h in range(NH):
            b = ps_next(C)
            for c4 in range(4):
                P.op("pe", lambda e, b=b, h=h, c4=c4: e.matmul(out=C.ps[b][:, :], lhsT=wk[:, c4, h * 128:(h + 1) * 128],
                                                             rhs=ckvn[cs][:, c4, :], start=(c4 == 0), stop=(c4 == 3)),
                     reads=["wk", ckres[c4]], writes=["ps%d" % b], lhs=["wk"])
            half = h // 8
            if True:
                P.op("act", lambda e, b=b, h=h, half=half: e.copy(out=kst[half][:, h % 8, :], in_=C.ps[b][:, :]),
                     reads=["ps%d" % b], writes=["kst%d" % half])
            else:
                P.op("dve", lambda e, b=b, h=h, half=half: e.tensor_copy(out=kst[half][:, h % 8, :], in_=C.ps[b][:, :]),
                     reads=["ps%d" % b], writes=["kst%d" % half])
            if h % 8 == 7:
                C.final_ops.append(
                    P.dma("pool", lambda e, half=half, t0=t0: e.dma_start(
                        out=C.kT_d[8 * half:8 * half + 8, :, t0:t0 + 512].rearrange("h d t -> d h t"), in_=kst[half]),
                        reads=["kst%d" % half], writes=["kT_d"], chan="kst%d" % half))
            yield

    def v_proj(g):
        cs = g % 2
        t0 = g * 512
        ckres = ["ckvn%d_%d" % (cs, cb) for cb in range(4)]
        for tbk in range(4):
            for cg in range(4):
                b = ps_next(C)
                for c4 in range(4):
                    P.op("pe", lambda e, b=b, cg=cg, c4=c4, tbk=tbk: e.matmul(
                        out=C.ps[b][:, :], lhsT=ckvn[cs][:, c4, tbk * 128:(tbk + 1) * 128],
                        rhs=wv[:, c4, cg * 512:(cg + 1) * 512], start=(c4 == 0), stop=(c4 == 3)),
                        reads=["wv", ckres[c4]], writes=["ps%d" % b], lhs=[ckres[c4]])
                src = C.ps[b][:, :].rearrange("p (h d) -> p h d", d=128)
                dst = vst[:, 4 * cg:4 * cg + 4, tbk, :]
                if True:
                    P.op("act", lambda e, src=src, dst=dst: e.copy(out=dst, in_=src),
                         reads=["ps%d" % b], writes=["vst"])
                else:
                    P.op("dve", lambda e, src=src, dst=dst: e.tensor_copy(out=dst, in_=src),
                         reads=["ps%d" % b], writes=["vst"])
                yield
        kb0 = t0 // 128
        C.final_ops.append(
            P.dma("pool", lambda e, kb0=kb0: e.dma_start(
                out=C.v_d[:, :, kb0:kb0 + 4, :].rearrange("h p k d -> p h k d"), in_=vst),
                reads=["vst"], writes=["v_d"], chan="vst"))

    G = n_groups
    fe1(0)
    fe2(0)
    latents(0)
    ckv_norm(0)
    if G > 1:
        fe1(1)
    for g in range(G):
        if g + 1 < G:
            fe2(g + 1)
            latents(g + 1)
        cnt = 0

        def tick():
            nonlocal cnt
            cnt += 1

        if g + 2 < G:
            fe1(g + 2)
        for _ in k_proj(g):
            tick()
        if g + 1 < G:
            ckv_norm(g + 1)
        for _ in v_proj(g):
            tick()
    sb.release(mk)


def make_consts():
    import ml_dtypes
    inv = np.power(np.float32(10000.0), -np.arange(0, 64, 2, dtype=np.float32) / np.float32(64)).astype(np.float32)
    ropec = np.zeros((128, 2), np.float32)
    for p in range(128):
        ropec[p, 0] = inv[p % 32]
        ropec[p, 1] = -1.0 if (p % 64) < 32 else 1.0
    return dict(ident=np.eye(128, dtype=np.float32).astype(ml_dtypes.bfloat16),
                ones=np.ones((128, 128), np.float32).astype(ml_dtypes.bfloat16),
                ropec=ropec)


def phase_Q(C):
    P, sb = C.P, C.sb
    mk = sb.mark()
    wqa = sb.alloc([128, 16, 512], BF16)
    wqn = sb.alloc([128, 4, 2048], BF16)
    wqr = sb.alloc([128, 4, 1024], BF16)
    wqrs = sb.alloc([128, 4, 1024], BF16)
    gqa = sb.alloc([128, 4], F32)
    P.dma("pool", lambda e: e.dma_start(out=wqa, in_=C.w_in[:, 0:512].rearrange("(kc p) n -> p kc n", p=128)),
          writes=["wqa"], chan="wq", bulk=True)
    P.dma("pool", lambda e: e.dma_start(out=wqn, in_=C.w_qn.rearrange("(kc p) n -> p kc n", p=128)), writes=["wqn"], chan="wq", bulk=True)
    P.dma("pool", lambda e: e.dma_start(out=wqr, in_=C.w_qr.rearrange("(kc p) n -> p kc n", p=128)), writes=["wqr"], chan="wq", bulk=True)
    P.dma("pool", lambda e: e.dma_start(out=wqrs, in_=C.w_qrs.rearrange("(kc p) n -> p kc n", p=128)), writes=["wqrs"], chan="wq", bulk=True)
    P.dma("sp", lambda e: e.dma_start(out=gqa, in_=C.g_qa.rearrange("(c p) -> p c", p=128), allow_slow_non_contiguous=True),
          writes=["gqa"], chan="constQ", bulk=True)
    xb = sb.alloc([128, D], F32)
    ssb = sb.alloc([128, 1], F32)
    sdb = sb.alloc([128, 1], F32)
    rsb = sb.alloc([128, 1], F32)
    xs = sb.alloc([128, D], BF16)
    hTq = sb.alloc([128, 16, 512], BF16)
    qraw = sb.alloc([128, 4, 512], F32)
    sq = sb.alloc([128, 4, 512], BF16)
    junk = None
    qan = sb.alloc([128, 4, 512], BF16)
    sdq = sb.alloc([128, 512], F32)
    rq = sb.alloc([128, 512], F32)
    Ct = sb.alloc([128, 512], F32)
    St = sb.alloc([128, 512], F32)
    rtmp = [sb.alloc([128, 512], I32)] + [sb.alloc([128, 512], F32) for _ in range(4)]
    t1 = sb.alloc([128, 512], F32)
    t2 = sb.alloc([128, 512], F32)

    def q_half(hf):
        t0 = 512 * hf
        for tbk in range(4):
            front_end(C, C.x_own[t0 + tbk * 128:t0 + (tbk + 1) * 128, :], 7,
                      hTq[:, :, tbk * 128:(tbk + 1) * 128], "hq_%d" % tbk, (xb, junk, ssb, sdb, rsb, xs))
        rope_tables(C, C.pos_own[:, t0:t0 + 512], 512, Ct, St, rtmp, "rq")
        hres = lambda kc: ["hq_%d_%d" % (tb, kc // 4) for tb in range(4)]
        for cb in range(4):
            b = ps_next(C)
            for kc in range(16):
                P.op("pe", lambda e, b=b, cb=cb, kc=kc: e.matmul(out=C.ps[b][:, :], lhsT=wqa[:, kc, cb * 128:(cb + 1) * 128],
                                                               rhs=hTq[:, kc, :], start=(kc == 0), stop=(kc == 15)),
                     reads=["wqa"] + hres(kc), writes=["ps%d" % b], lhs=["wqa"])
            P.op("dve", lambda e, b=b, cb=cb: e.tensor_copy(out=qraw[:, cb, :], in_=C.ps[b][:, :]),
                 reads=["ps%d" % b], writes=["qraw%d" % cb])
            P.op("act", lambda e, cb=cb: e.activation(out=sq[:, cb, :], in_=qraw[:, cb, :], func=AF.Square),
                 reads=["qraw%d" % cb], writes=["qsq%d" % cb])
        b = ps_next(C)
        for cb in range(4):
            P.op("pe", lambda e, b=b, cb=cb: e.matmul(out=C.ps[b][:, :], lhsT=C.ones_sb, rhs=sq[:, cb, :],
                                                    start=(cb == 0), stop=(cb == 3)),
                 reads=["ones", "qsq%d" % cb], writes=["ps%d" % b], lhs=["ones"])
        P.op("act", lambda e, b=b: e.activation(out=sdq, in_=C.ps[b][:, :], func=AF.Sqrt, scale=1.0 / 512, bias=C.eps_sb),
             reads=["ps%d" % b, "eps"], writes=["sdq"])
        P.op("dve", lambda e: e.reciprocal(out=rq, in_=sdq), reads=["sdq"], writes=["rq"])
        for cb in range(4):
            P.op("dve", lambda e, cb=cb: e.scalar_tensor_tensor(out=qan[:, cb, :], in0=qraw[:, cb, :], scalar=gqa[:, cb:cb + 1],
                                                             in1=rq, op0=ALU.mult, op1=ALU.mult),
                 reads=["qraw%d" % cb, "gqa", "rq"], writes=["qan%d" % cb])
        qres = ["qan%d" % cb for cb in range(4)]
        for h in range(NH):
            b = ps_next(C)
            for c4 in range(4):
                P.op("pe", lambda e, b=b, h=h, c4=c4: e.matmul(out=C.ps[b][:, :], lhsT=wqn[:, c4, h * 128:(h + 1) * 128],
                                                             rhs=qan[:, c4, :], start=(c4 == 0), stop=(c4 == 3)),
                     reads=["wqn", qres[c4]], writes=["ps%d" % b], lhs=["wqn"])
            if h % 2 == 0:
                P.op("act", lambda e, b=b, h=h: e.copy(out=C.Qn[:, h, t0:t0 + 512], in_=C.ps[b][:, :]),
                     reads=["ps%d" % b], writes=["Qn%d" % h])
            else:
                P.op("dve", lambda e, b=b, h=h: e.tensor_copy(out=C.Qn[:, h, t0:t0 + 512], in_=C.ps[b][:, :]),
                     reads=["ps%d" % b], writes=["Qn%d" % h])
        for hp in range(8):
            b1 = ps_next(C)
            for c4 in range(4):
                P.op("pe", lambda e, b1=b1, hp=hp, c4=c4: e.matmul(out=C.ps[b1][:, :], lhsT=wqr[:, c4, hp * 128:(hp + 1) * 128],
                                                                 rhs=qan[:, c4, :], start=(c4 == 0), stop=(c4 == 3)),
                     reads=["wqr", qres[c4]], writes=["ps%d" % b1], lhs=["wqr"])
            P.op("dve", lambda e, b1=b1: e.tensor_tensor(out=t1, in0=C.ps[b1][:, :], in1=Ct, op=ALU.mult),
                 reads=["ps%d" % b1, "rqC"], writes=["qt1"])
            b2 = ps_next(C)
            for c4 in range(4):
                P.op("pe", lambda e, b2=b2, hp=hp, c4=c4: e.matmul(out=C.ps[b2][:, :], lhsT=wqrs[:, c4, hp * 128:(hp + 1) * 128],
                                                                 rhs=qan[:, c4, :], start=(c4 == 0), stop=(c4 == 3)),
                     reads=["wqrs", qres[c4]], writes=["ps%d" % b2], lhs=["wqrs"])
            P.op("dve", lambda e, b2=b2: e.tensor_tensor(out=t2, in0=C.ps[b2][:, :], in1=St, op=ALU.mult),
                 reads=["ps%d" % b2, "rqS"], writes=["qt2"])
            P.op("pool", lambda e, hp=hp: e.tensor_tensor(out=C.Qr[:, hp, t0:t0 + 512], in0=t1, in1=t2, op=ALU.add),
                 reads=["qt1", "qt2"], writes=["Qr%d" % hp])

    for hf in range(2):
        q_half(hf)
    sb.release(mk)


def phase_A(C):
    P, sb = C.P, C.sb
    mk = sb.mark()
    S_all = C.S_all
    nkb = S_all // 128
    nq = nkb // 16
    KrT = [sb.alloc([128, S_all], BF16) for _ in range(2)]
    kring = [sb.alloc([128, 2048], BF16) for _ in range(4)]
    vring = [sb.alloc([128, 16, 128], BF16) for _ in range(4)]
    NPT = 16
    Pt = [sb.alloc([128, 512], BF16) for _ in range(NPT)]
    negm = sb.alloc([128, 16], BF16)
    Of = sb.alloc([128, 1024], F32)
    rl = sb.alloc([128, 1024], F32)
    accS = sb.alloc([128, 1024], F32)
    onesf = sb.alloc([128, 128], F32)
    stores = list(C.final_ops)
    P.op("pool", lambda e: e.memset(onesf, 1.0), writes=["onesf"])
    P.op("pool", lambda e: e.memset(KrT[0][64:128, :], 0.0), writes=["KrT0z"])
    P.op("pool", lambda e: e.memset(KrT[1][0:64, :], 0.0), writes=["KrT1z"])
    P.dma("sp", lambda e: e.dma_start(out=negm, in_=C.mask16), writes=["negm"], chan="constA", bulk=True)
    P.dma("sp", lambda e: e.dma_start(out=KrT[0][0:64, :], in_=C.krT_d[0:64, :]), writes=["KrT0"], chan="constA", bulk=True, extra=stores)
    P.dma("sp", lambda e: e.dma_start(out=KrT[1][64:128, :], in_=C.krT_d[64:128, :]), writes=["KrT1"], chan="constA", bulk=True, extra=stores)

    def load_q(i):
        h, q = divmod(i, nq)
        s = i % 4
        P.dma("sp", lambda e: e.dma_start(out=kring[s], in_=C.kT_d[h, :, 2048 * q:2048 * (q + 1)]),
              writes=["kq%d" % s], chan="kq%d" % s, extra=stores)
        P.dma("sp", lambda e: e.dma_start(out=vring[s], in_=C.v_d[h, :, 16 * q:16 * (q + 1), :]),
              writes=["vq%d" % s], chan="vq%d" % s, extra=stores)

    nload = NH * nq
    for i in range(min(4, nload)):
        load_q(i)

    units = []
    for h in range(NH):
        for kb in range(nkb):
            for half in range(2):
                lo = max(16 * kb, 512 * half)
                hi = 512 * (half + 1)
                if lo < hi:
                    units.append((h, kb, half, lo, hi))
    last_kb = {0: min(nkb - 1, 31), 1: nkb - 1}
    sbank = [5, 6, 7]
    NSB = 3
    ACCB = [3, 4]
    pool_cnt = {}

    def emit_S(u, ui):
        h, kb, half, lo, hi = u
        n = hi - lo
        b = sbank[ui % NSB]
        s = (h * nq + kb // 16) % 4
        kk = (kb % 16) * 128
        par = h % 2
        P.op("pe", lambda e: e.matmul(out=C.ps[b][:, 0:n], lhsT=kring[s][:, kk:kk + 128], rhs=C.Qn[:, h, lo:hi],
                                      start=True, stop=False),
             reads=["kq%d" % s, "Qn%d" % h], writes=["ps%d" % b], lhs=["kq%d" % s])
        P.op("pe", lambda e: e.matmul(out=C.ps[b][:, 0:n], lhsT=KrT[par][:, kb * 128:(kb + 1) * 128],
                                      rhs=C.Qr[:, h // 2, lo:hi], start=False, stop=True),
             reads=["KrT%d" % par, "KrT%dz" % par, "Qr%d" % (h // 2)], writes=["ps%d" % b], lhs=["KrT%d" % par, "KrT%dz" % par])

    def emit_PV(u, ui):
        h, kb, half, lo, hi = u
        n = hi - lo
        b = sbank[ui % NSB]
        pt = Pt[ui % NPT]
        pres = "Pt%d" % (ui % NPT)
        s = (h * nq + kb // 16) % 4
        hp = h % 2
        P.op("act", lambda e: e.activation(out=pt[:, 0:n], in_=C.ps[b][:, 0:n], func=AF.Exp, scale=SCALE),
             reads=["ps%d" % b], writes=[pres])
        if lo == 16 * kb:
            P.op("pool", lambda e: e.tensor_tensor(out=pt[:, 0:16], in0=pt[:, 0:16], in1=negm, op=ALU.mult),
                 reads=[pres, "negm"], writes=[pres])
        c0 = lo - 512 * half
        first = (kb == 0)
        last = (kb == last_kb[half])
        ob, lb = half, 2
        P.op("pe", lambda e: e.matmul(out=C.ps[ob][:, c0:c0 + n], lhsT=vring[s][:, kb % 16, :], rhs=pt[:, 0:n],
                                      start=first, stop=last),
             reads=["vq%d" % s, pres], writes=["ps%d" % ob], lhs=["vq%d" % s])
        ab = ACCB[half]
        if first:
            P.op("dve", lambda e: e.tensor_copy(out=C.ps[ab][:, c0:c0 + n], in_=pt[:, 0:n]),
                 reads=[pres], writes=["ps%d" % ab])
        else:
            P.op("dve", lambda e: e.tensor_tensor(out=C.ps[ab][:, c0:c0 + n], in0=C.ps[ab][:, c0:c0 + n], in1=pt[:, 0:n], op=ALU.add),
                 reads=[pres, "ps%d" % ab], writes=["ps%d" % ab])
        if last:
            hs_ = slice(512 * half, 512 * (half + 1))
            P.op("dve", lambda e: e.tensor_copy(out=accS[:, hs_], in_=C.ps[ab][:, :]), reads=["ps%d" % ab], writes=["accS%d" % half])
            P.op("pe", lambda e: e.matmul(out=C.ps[lb][:, :], lhsT=onesf, rhs=accS[:, hs_], start=True, stop=True),
                 reads=["onesf", "accS%d" % half], writes=["ps%d" % lb])
            P.op("dve", lambda e: e.reciprocal(out=rl[:, hs_], in_=C.ps[lb][:, :]), reads=["ps%d" % lb], writes=["rl%d" % half])
            P.op("dve", lambda e: e.tensor_tensor(out=C.attn[:, h, hs_], in0=C.ps[ob][:, :], in1=rl[:, hs_], op=ALU.mult),
                 reads=["ps%d" % ob, "rl%d" % half], writes=["attn%d" % h])
        if kb % 16 == 15 and half == 1:
            i = h * nq + kb // 16
            if i + 4 < nload:
                load_q(i + 4)

    LOOK = 2
    nu = len(units)
    for ui in range(min(LOOK, nu)):
        emit_S(units[ui], ui)
    for ui in range(nu):
        if ui + LOOK < nu:
            emit_S(units[ui + LOOK], ui + LOOK)
        emit_PV(units[ui], ui)
    sb.release(mk)


OFF_Z, OFF_CIN, OFF_BG, OFF_CG, OFF_ZC, OFF_GM, OFF_GC = 1088, 3136, 5184, 7232, 9280, 11328, 13376


def phase_R(C):
    P, sb = C.P, C.sb
    mg = sb.alloc([128, 16, 1024], BF16)
    mR = sb.mark()
    hT = sb.alloc([128, 16, 1024], BF16)
    hTh = sb.alloc([128, 16, 128], BF16)
    gc = sb.alloc([128, 16, 1024], BF16)
    wr = [sb.alloc([128, 16, 256], BF16) for _ in range(4)]
    convw = sb.alloc([128, 3, 16], F32)
    for k in range(3):
        P.dma("sp", lambda e, k=k: e.dma_start(out=convw[:, k, :], in_=C.conv_w[k].rearrange("(j p) -> p j", p=128),
                                             allow_slow_non_contiguous=True),
              writes=["convw"], chan="constR", bulk=True)
    mk = sb.mark()
    fbuf = []
    for _ in range(2):
        fbuf.append((sb.alloc([128, D], F32), None, sb.alloc([128, 1], F32), sb.alloc([128, 1], F32),
                     sb.alloc([128, 1], F32), sb.alloc([128, D], BF16)))
    srcs = [C.x_own[blk * 128:(blk + 1) * 128, :] for blk in range(8)] + [C.x_halo]
    dsts = [hT[:, :, blk * 128:(blk + 1) * 128] for blk in range(8)] + [hTh[:, :, :]]
    front_end1(C, srcs[0], 40, fbuf[0])
    for blk in range(9):
        if blk + 1 < 9:
            front_end1(C, srcs[blk + 1], 40 + (blk + 1) % 2, fbuf[(blk + 1) % 2])
        front_end2(C, 40 + blk % 2, dsts[blk], "hr_%d" % blk, fbuf[blk % 2][5])
    hres = lambda kc, half: ["hr_%d_%d" % (4 * half + tb, kc // 4) for tb in range(4)]
    hhres = lambda kc: ["hr_8_%d" % (kc // 4)]
    sb.release(mk)

    tiles = []
    for jp in range(8):
        tiles.append(C.w_in[:, OFF_Z + 256 * jp:OFF_Z + 256 * (jp + 1)])
    for jp in range(8):
        for off in (OFF_CIN, OFF_CG, OFF_BG, OFF_ZC):
            tiles.append(C.w_in[:, off + 256 * jp:off + 256 * (jp + 1)])
    for jp in range(8):
        tiles.append(C.w_o_mla[:, 256 * jp:256 * (jp + 1)])
        tiles.append(C.w_in[:, OFF_GM + 256 * jp:OFF_GM + 256 * (jp + 1)])
        tiles.append(C.w_in[:, OFF_GC + 256 * jp:OFF_GC + 256 * (jp + 1)])
        tiles.append(C.w_o_conv[:, 256 * jp:256 * (jp + 1)])
    st = {"next": 0}

    def issue(n=1):
        for _ in range(n):
            i = st["next"]
            if i >= len(tiles):
                return
            st["next"] = i + 1
            s_ = i % 4
            src = tiles[i].rearrange("(kc p) n -> p kc n", p=128)
            P.dma("pool", lambda e, s_=s_, src=src: e.dma_start(out=wr[s_], in_=src), writes=["wr%d" % s_], chan="wr%d" % s_)

    ti = {"i": 0}

    def take():
        i = ti["i"]
        ti["i"] = i + 1
        while st["next"] <= min(i + 3, len(tiles) - 1):
            issue(1)
        return wr[i % 4], "wr%d" % (i % 4)

    def mm(w, wres, sub, rhs_fn, rres_fn, n):
        b = ps_next(C)
        for kc in range(16):
            P.op("pe", lambda e, b=b, kc=kc: e.matmul(out=C.ps[b][:, 0:n], lhsT=w[:, kc, sub * 128:(sub + 1) * 128],
                                                    rhs=rhs_fn(kc), start=(kc == 0), stop=(kc == 15)),
                 reads=[wres] + rres_fn(kc), writes=["ps%d" % b], lhs=[wres])
        return b

    tA = sb.alloc([128, 2, 1024], F32)
    tB = sb.alloc([128, 2, 1024], F32)
    U = sb.alloc([128, 2, 64 * 18], F32)
    cinh = sb.alloc([128, 2, 128], F32)
    halves = [slice(0, 512), slice(512, 1024)]

    def own_mm(w, wres, sub, half):
        hsl = halves[half]
        return mm(w, wres, sub, lambda kc: hT[:, kc, hsl], lambda kc: hres(kc, half), 512)

    def r1(jp):
        w, wres = take()
        for sub in range(2):
            j = 2 * jp + sub
            for half in range(2):
                hsl = halves[half]
                b = own_mm(w, wres, sub, half)
                P.op("act", lambda e, b=b, hsl=hsl, sub=sub: e.activation(out=tA[:, sub, hsl], in_=C.ps[b][:, :], func=AF.Silu),
                     reads=["ps%d" % b], writes=["tA%d_%d" % (sub, half)])
                P.op("dve", lambda e, j=j, hsl=hsl, sub=sub: e.tensor_tensor(out=C.attn[:, j, hsl], in0=C.attn[:, j, hsl],
                                                                         in1=tA[:, sub, hsl], op=ALU.mult),
                     reads=["tA%d_%d" % (sub, half), "attn%d" % j], writes=["attn%d" % j])

    for jp in range(8):
        r1(jp)

    def Uv(sub):
        return U[:, sub, :].rearrange("p (m j) -> p m j", j=18)

    def r2(jp):
        w, wres = take()
        for sub in range(2):
            for half in range(2):
                hsl = halves[half]
                b = own_mm(w, wres, sub, half)
                P.op("act", lambda e, b=b, hsl=hsl, sub=sub: e.copy(out=tA[:, sub, hsl], in_=C.ps[b][:, :]),
                     reads=["ps%d" % b], writes=["tA%d_%d" % (sub, half)])
            b = mm(w, wres, sub, lambda kc: hTh[:, kc, :], hhres, 128)
            P.op("act", lambda e, b=b, sub=sub: e.copy(out=cinh[:, sub, :], in_=C.ps[b][:, 0:128]),
                 reads=["ps%d" % b], writes=["cinh%d" % sub])
        w, wres = take()
        for sub in range(2):
            j = 2 * jp + sub
            for half in range(2):
                hsl = halves[half]
                b = own_mm(w, wres, sub, half)
                P.op("dve", lambda e, b=b, half=half, hsl=hsl, sub=sub: e.tensor_tensor(
                    out=Uv(sub)[:, 32 * half:32 * (half + 1), 2:18], in0=C.ps[b][:, :].rearrange("p (m j) -> p m j", j=16),
                    in1=tA[:, sub, hsl].rearrange("p (m j) -> p m j", j=16), op=ALU.mult),
                    reads=["ps%d" % b, "tA%d_%d" % (sub, half)], writes=["Uo%d_%d" % (sub, half)])
            b = mm(w, wres, sub, lambda kc: hTh[:, kc, :], hhres, 128)
            P.op("dve", lambda e, b=b, sub=sub: e.tensor_tensor(out=Uv(sub)[:, :, 0:2],
                                                             in0=C.ps[b][:, 0:128].rearrange("p (m j) -> p m j", j=2),
                                                             in1=cinh[:, sub, :].rearrange("p (m j) -> p m j", j=2), op=ALU.mult),
                 reads=["ps%d" % b, "cinh%d" % sub], writes=["Uh%d" % sub])
            tB3 = tB[:, sub, :].rearrange("p (m j) -> p m j", j=16)
            ures = ["Uo%d_0" % sub, "Uo%d_1" % sub, "Uh%d" % sub]
            tres = "tB%d" % sub
            P.op("dve", lambda e, j=j, sub=sub, tB3=tB3: e.tensor_scalar(out=tB3, in0=Uv(sub)[:, :, 0:16], scalar1=convw[:, 0, j:j + 1],
                                                                     scalar2=None, op0=ALU.mult),
                 reads=ures + ["convw"], writes=[tres])
            P.op("dve", lambda e, j=j, sub=sub, tB3=tB3: e.scalar_tensor_tensor(out=tB3, in0=Uv(sub)[:, :, 1:17], scalar=convw[:, 1, j:j + 1],
                                                                            in1=tB3, op0=ALU.mult, op1=ALU.add),
                 reads=ures + ["convw", tres], writes=[tres])
            P.op("dve", lambda e, j=j, sub=sub, tB3=tB3: e.scalar_tensor_tensor(out=tB3, in0=Uv(sub)[:, :, 2:18], scalar=convw[:, 2, j:j + 1],
                                                                            in1=tB3, op0=ALU.mult, op1=ALU.add),
                 reads=ures + ["convw", tres], writes=[tres])
        w, wres = take()
        for sub in range(2):
            for half in range(2):
                hsl = halves[half]
                b = own_mm(w, wres, sub, half)
                P.op("dve", lambda e, b=b, hsl=hsl, sub=sub: e.tensor_tensor(out=tB[:, sub, hsl], in0=C.ps[b][:, :], in1=tB[:, sub, hsl], op=ALU.mult),
                     reads=["ps%d" % b, "tB%d" % sub], writes=["tB%d" % sub])
        w, wres = take()
        for sub in range(2):
            j = 2 * jp + sub
            for half in range(2):
                hsl = halves[half]
                b = own_mm(w, wres, sub, half)
                P.op("act", lambda e, b=b, hsl=hsl, sub=sub: e.activation(out=tA[:, sub, hsl], in_=C.ps[b][:, :], func=AF.Silu),
                     reads=["ps%d" % b], writes=["tA%d_%d" % (sub, half)])
                P.op("pool", lambda e, j=j, hsl=hsl, sub=sub: e.tensor_tensor(out=gc[:, j, hsl], in0=tB[:, sub, hsl], in1=tA[:, sub, hsl], op=ALU.mult),
                     reads=["tB%d" % sub, "tA%d_%d" % (sub, half)], writes=["gc%d" % j])

    for jp in range(8):
        r2(jp)

    allattn = ["attn%d" % j for j in range(16)]
    allgc = ["gc%d" % j for j in range(16)]

    def r3(jp):
        w, wres = take()
        for sub in range(2):
            for half in range(2):
                hsl = halves[half]
                b = mm(w, wres, sub, lambda kc, hsl=hsl: C.attn[:, kc, hsl], lambda kc: ["attn%d" % kc], 512)
                P.op("act", lambda e, b=b, hsl=hsl, sub=sub: e.copy(out=tB[:, sub, hsl], in_=C.ps[b][:, :]),
                     reads=["ps%d" % b], writes=["tB%d" % sub])
        w, wres = take()
        for sub in range(2):
            for half in range(2):
                hsl = halves[half]
                b = own_mm(w, wres, sub, half)
                P.op("act", lambda e, b=b, hsl=hsl, sub=sub: e.activation(out=tA[:, sub, hsl], in_=C.ps[b][:, :], func=AF.Sigmoid),
                     reads=["ps%d" % b], writes=["tA%d_%d" % (sub, half)])
                P.op("dve", lambda e, hsl=hsl, sub=sub: e.tensor_tensor(out=tB[:, sub, hsl], in0=tB[:, sub, hsl], in1=tA[:, sub, hsl], op=ALU.mult),
                     reads=["tB%d" % sub, "tA%d_%d" % (sub, half)], writes=["tB%d" % sub])
        w, wres = take()
        for sub in range(2):
            for half in range(2):
                hsl = halves[half]
                b = own_mm(w, wres, sub, half)
                P.op("act", lambda e, b=b, hsl=hsl, sub=sub: e.activation(out=tA[:, sub, hsl], in_=C.ps[b][:, :], func=AF.Sigmoid),
                     reads=["ps%d" % b], writes=["tA%d_%d" % (sub, half)])
        w, wres = take()
        for sub in range(2):
            j = 2 * jp + sub
            for half in range(2):
                hsl = halves[half]
                b = mm(w, wres, sub, lambda kc, hsl=hsl: gc[:, kc, hsl], lambda kc: ["gc%d" % kc], 512)
                P.op("dve", lambda e, b=b, hsl=hsl, sub=sub: e.tensor_tensor(out=tA[:, sub, hsl], in0=C.ps[b][:, :], in1=tA[:, sub, hsl], op=ALU.mult),
                     reads=["ps%d" % b, "tA%d_%d" % (sub, half)], writes=["tA%d_%d" % (sub, half)])
                P.op("pool", lambda e, j=j, hsl=hsl, sub=sub: e.tensor_tensor(out=mg[:, j, hsl], in0=tB[:, sub, hsl], in1=tA[:, sub, hsl], op=ALU.add),
                     reads=["tB%d" % sub, "tA%d_%d" % (sub, half)], writes=["mg%d" % j])

    for jp in range(8):
        r3(jp)

    if hasattr(C, "dbg_mg"):
        C.final_ops.append(P.dma("sp", lambda e: e.dma_start(out=C.dbg_gated, in_=C.attn), reads=allattn, chan="dbg"))
        C.final_ops.append(P.dma("sp", lambda e: e.dma_start(out=C.dbg_gc, in_=gc), reads=allgc, chan="dbg"))
        C.final_ops.append(P.dma("sp", lambda e: e.dma_start(out=C.dbg_mg, in_=mg), reads=["mg%d" % j for j in range(16)], chan="dbg"))
    P.barrier()
    sb.release(mR)
    wout = sb.alloc([128, 16, 2048], BF16)
    gpost = sb.alloc([128, D], F32)
    xr = [sb.alloc([128, D], F32) for _ in range(2)]
    ot = [sb.alloc([128, D], F32) for _ in range(2)]
    ssq = sb.alloc([128, 4], F32)
    sst = sb.alloc([128, 1], F32)
    sdo = sb.alloc([128, 1], F32)
    rso = sb.alloc([128, 1], F32)
    junk4 = sb.alloc([128, 512], BF16)
    for cg in range(4):
        P.dma("pool", lambda e, cg=cg: e.dma_start(out=wout[:, :, 512 * cg:512 * (cg + 1)],
                                                 in_=C.w_out[:, 512 * cg:512 * (cg + 1)].rearrange("(kc p) n -> p kc n", p=128)),
              writes=["wout%d" % cg], chan="wout%d" % cg)
    P.dma("sp", lambda e: e.dma_start(out=gpost, in_=C.g_post.broadcast_to([128, D])), writes=["gpost"], chan="constR4", bulk=True)
    allmg = ["mg%d" % j for j in range(16)]

    def r4(blk):
        s_ = blk % 2
        tsl = slice(128 * blk, 128 * (blk + 1))
        P.dma("sp", lambda e: e.dma_start(out=xr[s_], in_=C.x_own[tsl, :]), writes=["xr%d" % s_], chan="xr%d" % s_)
        banks = []
        for cg in range(4):
            b = ps_next(C)
            banks.append(b)
            for kc in range(16):
                P.op("pe", lambda e, b=b, kc=kc, cg=cg: e.matmul(out=C.ps[b][:, :], lhsT=mg[:, kc, tsl],
                                                               rhs=wout[:, kc, 512 * cg:512 * (cg + 1)],
                                                               start=(kc == 0), stop=(kc == 15)),
                     reads=["mg%d" % kc, "wout%d" % cg], writes=["ps%d" % b], lhs=["mg%d" % kc])
            P.op("act", lambda e, b=b, cg=cg: e.activation(out=junk4, in_=C.ps[b][:, :], func=AF.Square, accum_out=ssq[:, cg:cg + 1]),
                 reads=["ps%d" % b], writes=["junk4", "ssq%d" % cg])
        P.op("dve", lambda e: e.tensor_reduce(out=sst, in_=ssq, axis=mybir.AxisListType.X, op=ALU.add),
             reads=["ssq%d" % cg for cg in range(4)], writes=["sst"])
        P.op("act", lambda e: e.activation(out=sdo, in_=sst, func=AF.Sqrt, scale=1.0 / D, bias=C.eps_sb),
             reads=["sst", "eps"], writes=["sdo"])
        P.op("dve", lambda e: e.reciprocal(out=rso, in_=sdo), reads=["sdo"], writes=["rso"])
        for cg in range(4):
            b = banks[cg]
            csl = slice(512 * cg, 512 * (cg + 1))
            P.op("dve", lambda e, b=b, csl=csl: e.scalar_tensor_tensor(out=ot[s_][:, csl], in0=C.ps[b][:, :], scalar=rso,
                                                                     in1=gpost[:, csl], op0=ALU.mult, op1=ALU.mult),
                 reads=["ps%d" % b, "rso", "gpost"], writes=["ot%d_%d" % (s_, cg)])
            P.op("pool", lambda e, csl=csl: e.tensor_tensor(out=ot[s_][:, csl], in0=ot[s_][:, csl], in1=xr[s_][:, csl], op=ALU.add),
                 reads=["ot%d_%d" % (s_, cg), "xr%d" % s_], writes=["ot%d_%d" % (s_, cg)])
        C.final_ops.append(
            P.dma("sp", lambda e: e.dma_start(out=C.out[tsl, :], in_=ot[s_]),
                  reads=["ot%d_%d" % (s_, cg) for cg in range(4)], writes=["out"], chan="ot%d" % s_))

    for blk in range(8):
        r4(blk)


def _own_rows(c):
    return np.concatenate([np.arange(128 * m + 16 * c, 128 * m + 16 * c + 16) for m in range(64)])


def prep(x, positions, pre_norm_g, w_in, q_a_norm_g, w_q_b, kv_a_norm_g, w_kv_b,
         conv_w, w_o_mla, w_o_conv, w_out, post_norm_g, cores=range(NCORES)):
    import ml_dtypes
    x2 = np.ascontiguousarray(np.asarray(x, dtype=np.float32).reshape(S, D))
    pos = np.ascontiguousarray(np.asarray(positions, dtype=np.int32).reshape(1, S))
    w_in = np.asarray(w_in, dtype=np.float32)
    kr = w_in[:, 1024:1088]
    krs = np.concatenate([kr[:, 32:], kr[:, :32]], axis=1)
    w_lat = np.ascontiguousarray(np.concatenate([w_in[:, 512:1024], kr, kr, krs, krs], axis=1))
    wkv = np.asarray(w_kv_b, dtype=np.float32).reshape(512, NH, 256)
    w_k = np.ascontiguousarray(wkv[:, :, :128].reshape(512, 2048))
    w_v = np.ascontiguousarray(wkv[:, :, 128:].reshape(512, 2048))
    wq = np.asarray(w_q_b, dtype=np.float32).reshape(512, NH, 192)
    w_qn = np.ascontiguousarray(wq[:, :, :128].reshape(512, 2048))
    qr = wq[:, :, 128:]
    w_qr = np.ascontiguousarray(qr.reshape(512, 1024))
    w_qrs = np.ascontiguousarray(np.concatenate([qr[:, :, 32:], qr[:, :, :32]], axis=2).reshape(512, 1024))
    consts = make_consts()
    shared = dict(x_all=x2, pos_all=pos, g_pre=np.asarray(pre_norm_g, np.float32).reshape(1, D), w_lat=w_lat,
                  g_kv=np.asarray(kv_a_norm_g, np.float32), w_k=w_k, w_v=w_v, w_in=np.ascontiguousarray(w_in),
                  g_qa=np.asarray(q_a_norm_g, np.float32), w_qn=w_qn, w_qr=w_qr, w_qrs=w_qrs,
                  conv_w=np.ascontiguousarray(np.asarray(conv_w, np.float32)),
                  w_o_mla=np.ascontiguousarray(np.asarray(w_o_mla, np.float32)),
                  w_o_conv=np.ascontiguousarray(np.asarray(w_o_conv, np.float32)),
                  w_out=np.ascontiguousarray(np.asarray(w_out, np.float32)),
                  g_post=np.asarray(post_norm_g, np.float32).reshape(1, D), **consts)
    in_maps = []
    rows_all = []
    for c in cores:
        rows = _own_rows(c)
        rows_all.append(rows)
        x_halo = np.zeros((128, D), np.float32)
        for m in range(64):
            for t in range(2):
                g = 128 * m + 16 * c - 2 + t
                if g >= 0:
                    x_halo[2 * m + t] = x2[g]
        kk = np.arange(128)[:, None]
        jj = np.arange(16)[None, :]
        mask16 = (kk <= 16 * c + jj).astype(np.float32).astype(ml_dtypes.bfloat16)
        im = dict(shared)
        im.update(x_own=np.ascontiguousarray(x2[rows]), pos_own=np.ascontiguousarray(pos[:, rows]),
                  x_halo=x_halo, mask16=mask16)
        in_maps.append(im)
    return in_maps, rows_all


def kernel(x, positions, pre_norm_g, w_in, q_a_norm_g, w_q_b, kv_a_norm_g, w_kv_b,
           conv_w, w_o_mla, w_o_conv, w_out, post_norm_g):
    in_maps, rows_all = prep(x, positions, pre_norm_g, w_in, q_a_norm_g, w_q_b, kv_a_norm_g, w_kv_b,
                             conv_w, w_o_mla, w_o_conv, w_out, post_norm_g)
    nc = build()
    res = run_bass_kernel_spmd(nc, in_maps, core_ids=list(range(NCORES)))
    out = np.zeros((S, D), np.float32)
    for c in range(NCORES):
        out[rows_all[c]] = np.asarray(res.results[c]["out"], dtype=np.float32)
    return out.reshape(1, S, D)
```

```python
import contextlib
import math

import numpy as np
import concourse.bass as bass
import concourse.mybir as mybir
from concourse.bass_utils import run_bass_kernel_spmd

F32 = mybir.dt.float32
BF16 = mybir.dt.bfloat16
I32 = mybir.dt.int32
AF = mybir.ActivationFunctionType
ALU = mybir.AluOpType

D = 2048
S = 8192
NH = 16
EPS = 1e-6
NCORES = 8
SCALE = 1.0 / math.sqrt(192.0)
TWO_PI = 2.0 * math.pi
INLINE_WAITS = True
import os
FLAGS = os.environ.get("KFLAGS", "").split(",")


class _Op:
    __slots__ = ("eng", "fn", "deps", "kind", "chan", "sig", "needed", "lhs", "depres")


class Prog:
    ENGS = ("pe", "act", "dve", "pool", "sp")

    def __init__(self):
        self.ops = []
        self.res = {}
        self.chan_count = {}
        self.bulk = set()

    def _add(self, eng, fn, reads, writes, kind, chan=None, extra=(), lhs=None):
        op = _Op()
        op.eng, op.fn, op.kind, op.chan = eng, fn, kind, chan
        op.needed, op.sig = False, None
        op.lhs = None if lhs is None else set(lhs)
        deps = {}
        depres = {}

        def put(d, k, r):
            if d not in deps:
                deps[d] = k
                depres[d] = {r}
            else:
                depres[d].add(r)

        for r in reads:
            st = self.res.setdefault(r, [None, []])
            if st[0] is not None:
                put(st[0], "raw", r)
            if r.startswith("ps"):
                for rd in st[1]:
                    if rd.eng != eng:
                        put(rd, "raw", r)
        for w in writes:
            st = self.res.setdefault(w, [None, []])
            if st[0] is not None:
                put(st[0], "waw", w)
            for rd in st[1]:
                put(rd, "war", w)
        op.depres = depres
        for r in reads:
            self.res[r][1].append(op)
        for w in writes:
            self.res[w] = [op, []]
        final = []
        for d, k in deps.items():
            if d is op:
                continue
            if d.eng == eng and d.kind == "c" and kind == "c":
                if eng == "pe" or k == "war":
                    continue
            if kind == "d" and d.kind == "d" and d.chan == chan and chan in self.bulk:
                continue
            final.append(d)
        for d in extra:
            if d not in final:
                final.append(d)
        op.deps = final
        for d in final:
            d.needed = True
        if kind == "d":
            n = self.chan_count.get(chan, 0) + 1
            self.chan_count[chan] = n
            op.sig = (("chan", chan), 16 * n)
        self.ops.append(op)
        return op

    def op(self, eng, fn, reads=(), writes=(), extra=(), lhs=None):
        return self._add(eng, fn, list(reads), list(writes), "c", extra=extra, lhs=lhs)

    def dma(self, eng, fn, reads=(), writes=(), chan=None, bulk=False, extra=()):
        assert chan is not None
        if bulk:
            self.bulk.add(chan)
        else:
            assert chan not in self.bulk
        return self._add(eng, fn, list(reads), list(writes), "d", chan, extra=extra)

    def barrier(self):
        last = {}
        lastd = {}
        for o in self.ops:
            if o.fn is None:
                continue
            if o.kind == "d":
                lastd[o.chan] = o
            else:
                last[o.eng] = o
        for e in self.ENGS:
            deps = list(last.values()) + list(lastd.values())
            self.join(e, deps)

    def join(self, eng, ops):
        op = _Op()
        op.eng, op.fn, op.kind, op.chan, op.needed, op.sig = eng, None, "c", None, False, None
        op.lhs, op.depres = None, {}
        op.deps = list(ops)
        for d in ops:
            d.needed = True
        self.ops.append(op)
        return op

    def emit(self, nc):
        cnt = {e: 0 for e in self.ENGS}
        for op in self.ops:
            if op.kind == "c" and op.needed:
                cnt[op.eng] += 1
                op.sig = (("eng", op.eng), cnt[op.eng])
            if op.kind == "d" and op.chan in self.bulk:
                op.sig = (("chan", op.chan), 16 * self.chan_count[op.chan])
        with contextlib.ExitStack() as st:
            sems = {}
            for e in self.ENGS:
                sems[("eng", e)] = st.enter_context(nc.semaphore("s_" + e))
            for c in self.chan_count:
                sems[("chan", c)] = st.enter_context(nc.semaphore("c_" + str(c)))
            block = st.enter_context(nc.Block())

            def body(ename):
                def f(eng):
                    waited = {}
                    for op in self.ops:
                        if op.eng != ename:
                            continue
                        inline = []
                        for d in op.deps:
                            key, val = d.sig
                            if waited.get(key, 0) < val:
                                waited[key] = val
                                rs = op.depres.get(d)
                                if (ename == "pe" and INLINE_WAITS and op.lhs is not None and op.fn is not None
                                        and rs is not None and not (rs & op.lhs)):
                                    inline.append((key, val))
                                else:
                                    eng.wait_ge(sems[key], val)
                        for key, val in inline[:-1]:
                            eng.wait_ge(sems[key], val)
                        if op.fn is None:
                            continue
                        inst = op.fn(eng)
                        if inline:
                            inst._wait_ge(sems[inline[-1][0]], inline[-1][1])
                        if op.kind == "d":
                            inst.then_inc(sems[op.sig[0]], 16)
                        elif op.needed:
                            inst.then_inc(sems[op.sig[0]], 1)
                return f

            block.tensor(body("pe"))
            block.scalar(body("act"))
            block.vector(body("dve"))
            block.gpsimd(body("pool"))
            block.sync(body("sp"))


class SB:
    def __init__(self, nc, nbytes):
        self.t = nc.alloc_sbuf_tensor("sb", [128, nbytes // 2], BF16)
        self.off = 0
        self.cap = nbytes

    def alloc(self, shape, dtype):
        assert shape[0] == 128
        n = int(np.prod(shape[1:]))
        size = 4 if dtype in (F32, I32) else 2
        nb = (n * size + 63) // 64 * 64
        assert self.off + nb <= self.cap, ("SBUF overflow", self.off, nb, self.cap)
        ap = self.t[:, self.off // 2:(self.off + n * size) // 2]
        self.off += nb
        if dtype != BF16:
            ap = ap.bitcast(dtype)
        if len(shape) == 3:
            ap = ap.rearrange("p (a b) -> p a b", b=shape[2])
        elif len(shape) == 4:
            ap = ap.rearrange("p (a b c) -> p a b c", b=shape[2], c=shape[3])
        return ap

    def mark(self):
        return self.off

    def release(self, m):
        self.off = m


class Ctx:
    pass


def own_blocks(c):
    out = []
    for g in range(4):
        out += [16 * g + c, 16 * g + 15 - c]
    return out


def build(n_kv_groups=16, phases=("K", "Q", "A", "R"), dbg=False, stop=99):
    nc = bass.Bass("TRN2", target_bir_lowering=False)
    P = Prog()
    C = Ctx()
    C.nc, C.P = nc, P
    C.stop = stop
    S_all = n_kv_groups * 512
    C.S_all = S_all

    def din(name, shape, dt=F32):
        return nc.dram_tensor(name, list(shape), dt, kind="ExternalInput").ap()

    scratch_kind = "ExternalOutput" if dbg else "Internal"

    def dscr(name, shape, dt=BF16):
        return nc.dram_tensor(name, list(shape), dt, kind=scratch_kind).ap()

    C.x_all = din("x_all", [S_all, D])
    C.pos_all = din("pos_all", [1, S_all], I32)
    C.g_pre = din("g_pre", [1, D])
    C.w_lat = din("w_lat", [D, 768])
    C.g_kv = din("g_kv", [512])
    C.w_k = din("w_k", [512, 2048])
    C.w_v = din("w_v", [512, 2048])
    C.ident = din("ident", [128, 128], BF16)
    C.ones = din("ones", [128, 128], BF16)
    C.ropec = din("ropec", [128, 2])
    C.kT_d = dscr("kT_d", [NH, 128, S_all])
    C.v_d = dscr("v_d", [NH, 128, S_all // 128, 128])
    C.krT_d = dscr("krT_d", [128, S_all])
    full = any(p in phases for p in ("Q", "A", "R"))
    if full:
        C.x_own = din("x_own", [1024, D])
        C.pos_own = din("pos_own", [1, 1024], I32)
        C.x_halo = din("x_halo", [128, D])
        C.mask16 = din("mask16", [128, 16], BF16)
        C.w_in = din("w_in", [D, 15424])
        C.g_qa = din("g_qa", [512])
        C.w_qn = din("w_qn", [512, 2048])
        C.w_qr = din("w_qr", [512, 1024])
        C.w_qrs = din("w_qrs", [512, 1024])
        C.conv_w = din("conv_w", [3, D])
        C.w_o_mla = din("w_o_mla", [D, D])
        C.w_o_conv = din("w_o_conv", [D, D])
        C.w_out = din("w_out", [D, D])
        C.g_post = din("g_post", [1, D])
        okind = "ExternalOutput"
        C.out = nc.dram_tensor("out", [1024, D], F32, kind=okind).ap()
        if dbg:
            C.dbg_attn = nc.dram_tensor("dbg_attn", [128, 16, 1024], BF16, kind=okind).ap()
            C.dbg_qn = nc.dram_tensor("dbg_qn", [128, 16, 1024], BF16, kind=okind).ap()
            C.dbg_qr = nc.dram_tensor("dbg_qr", [128, 8, 1024], BF16, kind=okind).ap()
            C.dbg_gated = nc.dram_tensor("dbg_gated", [128, 16, 1024], BF16, kind=okind).ap()
            C.dbg_gc = nc.dram_tensor("dbg_gc", [128, 16, 1024], BF16, kind=okind).ap()
            C.dbg_mg = nc.dram_tensor("dbg_mg", [128, 16, 1024], BF16, kind=okind).ap()

    C.sb = SB(nc, 206 * 1024)
    C.ps = [nc.alloc_psum_tensor("ps%d" % i, [128, 512], F32) for i in range(8)]
    C.ps_i = 0

    const_setup(C)
    if "K" in phases and C.stop >= 1:
        phase_K(C, n_kv_groups)
    if full:
        sb = C.sb
        C.attn = sb.alloc([128, 16, 1024], BF16)
        m0 = sb.mark()
        C.Qn = sb.alloc([128, 16, 1024], BF16)
        C.Qr = sb.alloc([128, 8, 1024], BF16)
        P.barrier()
        if "Q" in phases:
            phase_Q(C)
        if dbg and "Q" in phases:
            C.final_ops.append(P.dma("sp", lambda e: e.dma_start(out=C.dbg_qn, in_=C.Qn), reads=["Qn%d" % h for h in range(NH)], chan="dbg"))
            C.final_ops.append(P.dma("sp", lambda e: e.dma_start(out=C.dbg_qr, in_=C.Qr), reads=["Qr%d" % h for h in range(8)], chan="dbg"))
        P.barrier()
        if "A" in phases:
            phase_A(C)
        if dbg and "A" in phases:
            C.final_ops.append(P.dma("sp", lambda e: e.dma_start(out=C.dbg_attn, in_=C.attn),
                                     reads=["attn%d" % h for h in range(NH)], chan="dbg"))
        P.barrier()
        sb.release(m0)
        if "R" in phases:
            phase_R(C)

    P.join("sp", C.final_ops)
    P.emit(nc)
    return nc


def ps_next(C):
    i = C.ps_i
    C.ps_i = (i + 1) % 8
    return i


def const_setup(C):
    nc, P, sb = C.nc, C.P, C.sb
    C.final_ops = []
    C.ident_sb = sb.alloc([128, 128], BF16)
    C.ones_sb = sb.alloc([128, 128], BF16)
    C.ropec_sb = sb.alloc([128, 2], F32)
    C.eps_sb = sb.alloc([128, 1], F32)
    C.mhalf_sb = sb.alloc([128, 1], F32)
    C.nt = [(sb.alloc([128, 1], F32), sb.alloc([128, 1], F32)) for _ in range(4)]
    C.gb = sb.alloc([128, D], F32)
    P.dma("sp", lambda e: e.dma_start(out=C.ident_sb, in_=C.ident), writes=["ident"], chan="const", bulk=True)
    P.dma("sp", lambda e: e.dma_start(out=C.ones_sb, in_=C.ones), writes=["ones"], chan="const", bulk=True)
    P.dma("sp", lambda e: e.dma_start(out=C.ropec_sb, in_=C.ropec), writes=["ropec"], chan="const", bulk=True)
    P.dma("sp", lambda e: e.dma_start(out=C.gb, in_=C.g_pre.broadcast_to([128, D])), writes=["gb"], chan="const", bulk=True)
    P.op("dve", lambda e: e.memset(C.eps_sb, EPS), writes=["eps"])
    P.op("pool", lambda e: e.memset(C.mhalf_sb, -0.5), writes=["mhalf"])


def rope_tables(C, pos_src, n, Ct, St, tmp, tag):
    P = C.P
    pi_t, a, kf, r, m = tmp[:5]
    P.dma("sp", lambda e: e.dma_start(out=pi_t, in_=pos_src.broadcast_to([128, n])),
          writes=[tag + "pi"], chan=tag + "pi")
    P.op("dve", lambda e: e.tensor_copy(out=a, in_=pi_t), reads=[tag + "pi"], writes=[tag + "a"])
    P.op("dve", lambda e: e.tensor_scalar(out=a, in0=a, scalar1=C.ropec_sb[:, 0:1], scalar2=None, op0=ALU.mult),
         reads=[tag + "a", "ropec"], writes=[tag + "a"])
    P.op("dve", lambda e: e.tensor_scalar(out=kf, in0=a, scalar1=1.0 / TWO_PI, scalar2=None, op0=ALU.mult),
         reads=[tag + "a"], writes=[tag + "kf"])
    ki = pi_t
    P.op("dve", lambda e: e.tensor_copy(out=ki, in_=kf), reads=[tag + "kf"], writes=[tag + "pi"])
    P.op("dve", lambda e: e.tensor_copy(out=kf, in_=ki), reads=[tag + "pi"], writes=[tag + "kf"])
    C1 = 6.28125
    C2 = TWO_PI - C1
    P.op("dve", lambda e: e.scalar_tensor_tensor(out=r, in0=kf, scalar=-C1, in1=a, op0=ALU.mult, op1=ALU.add),
         reads=[tag + "kf", tag + "a"], writes=[tag + "r"])
    P.op("dve", lambda e: e.scalar_tensor_tensor(out=r, in0=kf, scalar=-C2, in1=r, op0=ALU.mult, op1=ALU.add),
         reads=[tag + "kf", tag + "r"], writes=[tag + "r"])

    def wrap(x):
        P.op("dve", lambda e: e.tensor_scalar(out=m, in0=x, scalar1=math.pi, scalar2=-TWO_PI, op0=ALU.is_gt, op1=ALU.mult),
             reads=[tag + "r"], writes=[tag + "m"])
        P.op("dve", lambda e: e.tensor_tensor(out=x, in0=x, in1=m, op=ALU.add),
             reads=[tag + "r", tag + "m"], writes=[tag + "r"])
        P.op("dve", lambda e: e.tensor_scalar(out=m, in0=x, scalar1=-math.pi, scalar2=TWO_PI, op0=ALU.is_lt, op1=ALU.mult),
             reads=[tag + "r"], writes=[tag + "m"])
        P.op("dve", lambda e: e.tensor_tensor(out=x, in0=x, in1=m, op=ALU.add),
             reads=[tag + "r", tag + "m"], writes=[tag + "r"])

    wrap(r)
    r2 = tmp[5] if len(tmp) > 5 else None
    if r2 is None:
        P.op("act", lambda e: e.activation(out=St, in_=r, func=AF.Sin, scale=C.ropec_sb[:, 1:2]),
             reads=[tag + "r", "ropec"], writes=[tag + "S"])
        P.op("dve", lambda e: e.tensor_scalar(out=r, in0=r, scalar1=math.pi / 2, scalar2=None, op0=ALU.add),
             reads=[tag + "r"], writes=[tag + "r"])
        wrap(r)
        P.op("act", lambda e: e.activation(out=Ct, in_=r, func=AF.Sin),
             reads=[tag + "r"], writes=[tag + "C"])
        return None
    P.op("dve", lambda e: e.tensor_scalar(out=r2, in0=r, scalar1=math.pi / 2, scalar2=None, op0=ALU.add),
         reads=[tag + "r"], writes=[tag + "r2"])
    P.op("dve", lambda e: e.tensor_scalar(out=m, in0=r2, scalar1=math.pi, scalar2=-TWO_PI, op0=ALU.is_gt, op1=ALU.mult),
         reads=[tag + "r2"], writes=[tag + "m"])
    P.op("dve", lambda e: e.tensor_tensor(out=r2, in0=r2, in1=m, op=ALU.add),
         reads=[tag + "r2", tag + "m"], writes=[tag + "r2"])

    def act_part():
        P.op("act", lambda e: e.activation(out=St, in_=r, func=AF.Sin, scale=C.ropec_sb[:, 1:2]),
             reads=[tag + "r", "ropec"], writes=[tag + "S"])
        P.op("act", lambda e: e.activation(out=Ct, in_=r2, func=AF.Sin),
             reads=[tag + "r2"], writes=[tag + "C"])
    return act_part


def front_end1(C, x_src, slot, bufs, xslot=None, no_act=False):
    P = C.P
    xb, junk, ss, sd, rstd, xs = bufs
    sl = "fe%d" % slot
    xr = "fe%dxb" % (slot if xslot is None else xslot)
    P.dma("sp", lambda e: e.dma_start(out=xb, in_=x_src), writes=[xr], chan=xr)
    if no_act:
        P.op("dve", lambda e: e.scalar_tensor_tensor(out=xs, in0=xb, scalar=1.0, in1=xb, op0=ALU.mult, op1=ALU.mult, accum_out=ss),
             reads=[xr], writes=[sl + "xs", sl + "ss"])
        P.op("dve", lambda e: e.tensor_scalar(out=sd, in0=ss, scalar1=1.0 / D, scalar2=EPS, op0=ALU.mult, op1=ALU.add),
             reads=[sl + "ss"], writes=[sl + "sd"])
        ti, ta = C.nt[slot % 4]
        P.op("dve", lambda e: e.tensor_scalar(out=ti.bitcast(I32), in0=sd.bitcast(I32), scalar1=1, scalar2=None,
                                              op0=ALU.arith_shift_right),
             reads=[sl + "sd"], writes=[sl + "ti"])
        P.op("dve", lambda e: e.tensor_scalar(out=rstd.bitcast(I32), in0=ti.bitcast(I32), scalar1=-1.0, scalar2=float(0x5f3759df),
                                              op0=ALU.mult, op1=ALU.add),
             reads=[sl + "ti"], writes=[sl + "rstd"])
        for _ in range(3):
            P.op("dve", lambda e: e.tensor_tensor(out=ta, in0=sd, in1=rstd, op=ALU.mult),
                 reads=[sl + "sd", sl + "rstd"], writes=[sl + "ta"])
            P.op("dve", lambda e: e.tensor_tensor(out=ta, in0=ta, in1=rstd, op=ALU.mult),
                 reads=[sl + "ta", sl + "rstd"], writes=[sl + "ta"])
            P.op("dve", lambda e: e.tensor_scalar(out=ta, in0=ta, scalar1=-0.5, scalar2=1.5, op0=ALU.mult, op1=ALU.add),
                 reads=[sl + "ta"], writes=[sl + "ta"])
            P.op("dve", lambda e: e.tensor_tensor(out=rstd, in0=rstd, in1=ta, op=ALU.mult),
                 reads=[sl + "ta", sl + "rstd"], writes=[sl + "rstd"])
        P.op("dve", lambda e: e.scalar_tensor_tensor(out=xs, in0=xb, scalar=rstd, in1=C.gb, op0=ALU.mult, op1=ALU.mult),
             reads=[xr, sl + "rstd", "gb"], writes=[sl + "xs"])
        return
    P.op("act", lambda e: e.activation(out=xs, in_=xb, func=AF.Square, accum_out=ss),
         reads=[xr], writes=[sl + "xs", sl + "ss"])
    P.op("act", lambda e: e.activation(out=sd, in_=ss, func=AF.Sqrt, scale=1.0 / D, bias=C.eps_sb),
         reads=[sl + "ss", "eps"], writes=[sl + "sd"])
    P.op("dve", lambda e: e.reciprocal(out=rstd, in_=sd), reads=[sl + "sd"], writes=[sl + "rstd"])
    P.op("dve", lambda e: e.scalar_tensor_tensor(out=xs, in0=xb, scalar=rstd, in1=C.gb, op0=ALU.mult, op1=ALU.mult),
         reads=[xr, sl + "rstd", "gb"], writes=[sl + "xs"])


def front_end2(C, slot, hT_dst, hres, xs, act_only=False):
    P = C.P
    sl = "fe%d" % slot
    for q in range(4):
        b = ps_next(C)
        pb = C.ps[b][:, :].bitcast(BF16)
        for kk in range(4):
            kc = 4 * q + kk
            P.op("pe", lambda e, kc=kc, kk=kk, pb=pb: e.transpose(out=pb[:, kk * 128:(kk + 1) * 128],
                                                               in_=xs[:, kc * 128:(kc + 1) * 128], identity=C.ident_sb),
                 reads=[sl + "xs", "ident"], writes=["ps%d" % b], lhs=[sl + "xs"])
        src = pb[:, 0:512].rearrange("p (a b) -> p a b", b=128)
        dst = hT_dst[:, 4 * q:4 * q + 4, :]
        if q % 2 == 0 and not act_only:
            P.op("dve", lambda e, src=src, dst=dst: e.tensor_copy(out=dst, in_=src),
                 reads=["ps%d" % b], writes=[hres + "_%d" % q])
        else:
            P.op("act", lambda e, src=src, dst=dst: e.copy(out=dst, in_=src),
                 reads=["ps%d" % b], writes=[hres + "_%d" % q])


def front_end(C, x_src, slot, hT_dst, hres, bufs):
    front_end1(C, x_src, slot, bufs)
    front_end2(C, slot, hT_dst, hres, bufs[5])


def phase_K(C, n_groups):
    nc, P, sb = C.nc, C.P, C.sb
    mk = sb.mark()
    S_all = C.S_all
    wlat = sb.alloc([128, 16, 768], BF16)
    wk = sb.alloc([128, 4, 2048], BF16)
    wv = sb.alloc([128, 4, 2048], BF16)
    gkv = sb.alloc([128, 4], F32)
    for h2 in range(2):
        P.dma("pool", lambda e, h2=h2: e.dma_start(out=wlat[:, 8 * h2:8 * h2 + 8, :],
                                                 in_=C.w_lat[1024 * h2:1024 * (h2 + 1), :].rearrange("(kc p) n -> p kc n", p=128)),
              writes=["wlat%d" % h2], chan="wlat", bulk=True)
    P.dma("pool", lambda e: e.dma_start(out=wk, in_=C.w_k.rearrange("(kc p) n -> p kc n", p=128)), writes=["wk"], chan="wkv", bulk=True)
    P.dma("pool", lambda e: e.dma_start(out=wv, in_=C.w_v.rearrange("(kc p) n -> p kc n", p=128)), writes=["wv"], chan="wkv", bulk=True)
    P.dma("sp", lambda e: e.dma_start(out=gkv, in_=C.g_kv.rearrange("(c p) -> p c", p=128), allow_slow_non_contiguous=True), writes=["gkv"], chan="const", bulk=True)

    xb = [sb.alloc([128, D], F32) for _ in range(2)]
    ssb = [sb.alloc([128, 1], F32) for _ in range(4)]
    sdb = [sb.alloc([128, 1], F32) for _ in range(4)]
    rsb = [sb.alloc([128, 1], F32) for _ in range(4)]
    xs = [sb.alloc([128, D], BF16) for _ in range(4)]
    hT = [sb.alloc([128, 16, 512], BF16) for _ in range(2)]
    sq = sb.alloc([128, 4, 512], BF16)
    junk = None
    craw = sb.alloc([128, 4, 512], F32)
    ckvn = [sb.alloc([128, 4, 512], BF16) for _ in range(2)]
    sdk = sb.alloc([128, 512], F32)
    rk = sb.alloc([128, 512], F32)
    Ct = sb.alloc([128, 512], F32)
    St = sb.alloc([128, 512], F32)
    rtmp = [sb.alloc([128, 512], I32)] + [sb.alloc([128, 512], F32) for _ in range(4)]
    t1 = sb.alloc([128, 512], F32)
    rtmp.append(t1)
    t2 = sb.alloc([128, 512], F32)
    krt = [sb.alloc([128, 512], BF16) for _ in range(2)]
    kst = [sb.alloc([128, 8, 512], BF16) for _ in range(2)]
    vst = sb.alloc([128, 16, 4, 128], BF16)

    def fe1_blk(g, tbk):
        t0 = g * 512 + tbk * 128
        front_end1(C, C.x_all[t0:t0 + 128, :], 20 + tbk, (xb[tbk % 2], junk, ssb[tbk], sdb[tbk], rsb[tbk], xs[tbk]), xslot=30 + tbk % 2, no_act=True)

    def fe1(g):
        for tbk in range(4):
            fe1_blk(g, tbk)

    def fe2(g):
        hs = g % 2
        for tbk in range(4):
            front_end2(C, 20 + tbk, hT[hs][:, :, tbk * 128:(tbk + 1) * 128], "hT%d_%d" % (hs, tbk), xs[tbk], act_only=True)

    def latents(g):
        hs = g % 2
        t0 = g * 512
        rope_act = rope_tables(C, C.pos_all[:, t0:t0 + 512], 512, Ct, St, rtmp, "rk")
        hres = lambda kc: ["hT%d_%d_%d" % (hs, tb, kc // 4) for tb in range(4)]
        for cb in range(6):
            b = ps_next(C)
            for kc in range(16):
                P.op("pe", lambda e, b=b, cb=cb, kc=kc: e.matmul(out=C.ps[b][:, :], lhsT=wlat[:, kc, cb * 128:(cb + 1) * 128],
                                                               rhs=hT[hs][:, kc, :], start=(kc == 0), stop=(kc == 15)),
                     reads=["wlat%d" % (kc // 8)] + hres(kc), writes=["ps%d" % b], lhs=["wlat%d" % (kc // 8)])
            if cb < 4:
                P.op("act", lambda e, b=b, cb=cb: e.copy(out=craw[:, cb, :], in_=C.ps[b][:, :]),
                     reads=["ps%d" % b], writes=["craw%d" % cb])
                P.op("act", lambda e, b=b, cb=cb: e.activation(out=sq[:, cb, :], in_=C.ps[b][:, :], func=AF.Square),
                     reads=["ps%d" % b], writes=["sq%d" % cb])
            elif cb == 4:
                rope_act()
                P.op("dve", lambda e, b=b: e.tensor_tensor(out=t1, in0=C.ps[b][:, :], in1=Ct, op=ALU.mult),
                     reads=["ps%d" % b, "rkC"], writes=["rkr2"])
            else:
                P.op("dve", lambda e, b=b: e.tensor_tensor(out=t2, in0=C.ps[b][:, :], in1=St, op=ALU.mult),
                     reads=["ps%d" % b, "rkS"], writes=["t2"])
                ks = g % 2
                P.op("pool", lambda e, ks=ks: e.tensor_tensor(out=krt[ks], in0=t1, in1=t2, op=ALU.add),
                     reads=["rkr2", "t2"], writes=["krt%d" % ks])
                C.final_ops.append(
                    P.dma("pool", lambda e, ks=ks, t0=t0: e.dma_start(out=C.krT_d[:, t0:t0 + 512], in_=krt[ks]),
                          reads=["krt%d" % ks], writes=["krT_d"], chan="krst%d" % ks))

    def ckv_norm(g):
        cs = g % 2
        b = ps_next(C)
        for cb in range(4):
            P.op("pe", lambda e, b=b, cb=cb: e.matmul(out=C.ps[b][:, :], lhsT=C.ones_sb, rhs=sq[:, cb, :],
                                                    start=(cb == 0), stop=(cb == 3)),
                 reads=["ones", "sq%d" % cb], writes=["ps%d" % b], lhs=["ones"])
        P.op("act", lambda e, b=b: e.activation(out=sdk, in_=C.ps[b][:, :], func=AF.Sqrt, scale=1.0 / 512, bias=C.eps_sb),
             reads=["ps%d" % b, "eps"], writes=["sdk"])
        P.op("dve", lambda e: e.reciprocal(out=rk, in_=sdk), reads=["sdk"], writes=["rk"])
        for cb in range(4):
            P.op("dve", lambda e, cb=cb, cs=cs: e.scalar_tensor_tensor(out=ckvn[cs][:, cb, :], in0=craw[:, cb, :],
                                                                     scalar=gkv[:, cb:cb + 1], in1=rk,
                                                                     op0=ALU.mult, op1=ALU.mult),
                 reads=["craw%d" % cb, "gkv", "rk"], writes=["ckvn%d_%d" % (cs, cb)])

    def k_proj(g):
        cs = g % 2
        t0 = g * 512
        ckres = ["ckvn%d_%d" % (cs, cb) for cb in range(4)]
        for h in range(NH):
            b = ps_next(C)
            for c4 in range(4):
                P.op("pe", lambda e, b=b, h=h, c4=c4: e.matmul(out=C.ps[b][:, :], lhsT=wk[:, c4, h * 128:(h + 1) * 128],
                                                             rhs=ckvn[cs][:, c4, :], start=(c4 == 0), stop=(c4 == 3)),
                     reads=["wk", ckres[c4]], writes=["ps%d" % b], lhs=["wk"])
            half = h // 8
            if True:
                P.op("act", lambda e, b=b, h=h, half=half: e.copy(out=kst[half][:, h % 8, :], in_=C.ps[b][:, :]),
                     reads=["ps%d" % b], writes=["kst%d" % half])
            else:
                P.op("dve", lambda e, b=b, h=h, half=half: e.tensor_copy(out=kst[half][:, h % 8, :], in_=C.ps[b][:, :]),
                     reads=["ps%d" % b], writes=["kst%d" % half])
            if h % 8 == 7:
                C.final_ops.append(
                    P.dma("pool", lambda e, half=half, t0=t0: e.dma_start(
                        out=C.kT_d[8 * half:8 * half + 8, :, t0:t0 + 512].rearrange("h d t -> d h t"), in_=kst[half]),
                        reads=["kst%d" % half], writes=["kT_d"], chan="kst%d" % half))
            yield

    def v_proj(g):
        cs = g % 2
        t0 = g * 512
        ckres = ["ckvn%d_%d" % (cs, cb) for cb in range(4)]
        for tbk in range(4):
            for cg in range(4):
                b = ps_next(C)
                for c4 in range(4):
                    P.op("pe", lambda e, b=b, cg=cg, c4=c4, tbk=tbk: e.matmul(
                        out=C.ps[b][:, :], lhsT=ckvn[cs][:, c4, tbk * 128:(tbk + 1) * 128],
                        rhs=wv[:, c4, cg * 512:(cg + 1) * 512], start=(c4 == 0), stop=(c4 == 3)),
                        reads=["wv", ckres[c4]], writes=["ps%d" % b], lhs=[ckres[c4]])
                src = C.ps[b][:, :].rearrange("p (h d) -> p h d", d=128)
                dst = vst[:, 4 * cg:4 * cg + 4, tbk, :]
                if True:
                    P.op("act", lambda e, src=src, dst=dst: e.copy(out=dst, in_=src),
                         reads=["ps%d" % b], writes=["vst"])
                else:
                    P.op("dve", lambda e, src=src, dst=dst: e.tensor_copy(out=dst, in_=src),
                         reads=["ps%d" % b], writes=["vst"])
                yield
        kb0 = t0 // 128
        C.final_ops.append(
            P.dma("pool", lambda e, kb0=kb0: e.dma_start(
                out=C.v_d[:, :, kb0:kb0 + 4, :].rearrange("h p k d -> p h k d"), in_=vst),
                reads=["vst"], writes=["v_d"], chan="vst"))

    G = n_groups
    fe1(0)
    fe2(0)
    latents(0)
    ckv_norm(0)
    if G > 1:
        fe1(1)
    for g in range(G):
        if g + 1 < G:
            fe2(g + 1)
            latents(g + 1)
        cnt = 0

        def tick():
            nonlocal cnt
            cnt += 1

        if g + 2 < G:
            fe1(g + 2)
        for _ in k_proj(g):
            tick()
        if g + 1 < G:
            ckv_norm(g + 1)
        for _ in v_proj(g):
            tick()
    sb.release(mk)


def make_consts():
    import ml_dtypes
    inv = np.power(np.float32(10000.0), -np.arange(0, 64, 2, dtype=np.float32) / np.float32(64)).astype(np.float32)
    ropec = np.zeros((128, 2), np.float32)
    for p in range(128):
        ropec[p, 0] = inv[p % 32]
        ropec[p, 1] = -1.0 if (p % 64) < 32 else 1.0
    return dict(ident=np.eye(128, dtype=np.float32).astype(ml_dtypes.bfloat16),
                ones=np.ones((128, 128), np.float32).astype(ml_dtypes.bfloat16),
                ropec=ropec)


def phase_Q(C):
    P, sb = C.P, C.sb
    mk = sb.mark()
    wqa = sb.alloc([128, 16, 512], BF16)
    wqn = sb.alloc([128, 4, 2048], BF16)
    wqr = sb.alloc([128, 4, 1024], BF16)
    wqrs = sb.alloc([128, 4, 1024], BF16)
    gqa = sb.alloc([128, 4], F32)
    P.dma("pool", lambda e: e.dma_start(out=wqa, in_=C.w_in[:, 0:512].rearrange("(kc p) n -> p kc n", p=128)),
          writes=["wqa"], chan="cwqa")
    P.dma("pool", lambda e: e.dma_start(out=wqn, in_=C.w_qn.rearrange("(kc p) n -> p kc n", p=128)), writes=["wqn"], chan="cwqn")
    P.dma("pool", lambda e: e.dma_start(out=wqr, in_=C.w_qr.rearrange("(kc p) n -> p kc n", p=128)), writes=["wqr"], chan="cwqr")
    P.dma("pool", lambda e: e.dma_start(out=wqrs, in_=C.w_qrs.rearrange("(kc p) n -> p kc n", p=128)), writes=["wqrs"], chan="cwqrs")
    P.dma("sp", lambda e: e.dma_start(out=gqa, in_=C.g_qa.rearrange("(c p) -> p c", p=128), allow_slow_non_contiguous=True),
          writes=["gqa"], chan="constQ", bulk=True)
    xb = sb.alloc([128, D], F32)
    ssb = sb.alloc([128, 1], F32)
    sdb = sb.alloc([128, 1], F32)
    rsb = sb.alloc([128, 1], F32)
    xs = sb.alloc([128, D], BF16)
    hTq = sb.alloc([128, 16, 512], BF16)
    qraw = sb.alloc([128, 4, 512], F32)
    sq = sb.alloc([128, 4, 512], BF16)
    junk = None
    qan = sb.alloc([128, 4, 512], BF16)
    sdq = sb.alloc([128, 512], F32)
    rq = sb.alloc([128, 512], F32)
    Ct = sb.alloc([128, 512], F32)
    St = sb.alloc([128, 512], F32)
    rtmp = [sb.alloc([128, 512], I32)] + [sb.alloc([128, 512], F32) for _ in range(4)]
    t1 = sb.alloc([128, 512], F32)
    t2 = sb.alloc([128, 512], F32)

    def q_half(hf):
        t0 = 512 * hf
        for tbk in range(4):
            front_end(C, C.x_own[t0 + tbk * 128:t0 + (tbk + 1) * 128, :], 7,
                      hTq[:, :, tbk * 128:(tbk + 1) * 128], "hq_%d" % tbk, (xb, junk, ssb, sdb, rsb, xs))
        rope_tables(C, C.pos_own[:, t0:t0 + 512], 512, Ct, St, rtmp, "rq")
        hres = lambda kc: ["hq_%d_%d" % (tb, kc // 4) for tb in range(4)]
        for cb in range(4):
            b = ps_next(C)
            for kc in range(16):
                P.op("pe", lambda e, b=b, cb=cb, kc=kc: e.matmul(out=C.ps[b][:, :], lhsT=wqa[:, kc, cb * 128:(cb + 1) * 128],
                                                               rhs=hTq[:, kc, :], start=(kc == 0), stop=(kc == 15)),
                     reads=["wqa"] + hres(kc), writes=["ps%d" % b], lhs=["wqa"])
            P.op("dve", lambda e, b=b, cb=cb: e.tensor_copy(out=qraw[:, cb, :], in_=C.ps[b][:, :]),
                 reads=["ps%d" % b], writes=["qraw%d" % cb])
            P.op("act", lambda e, cb=cb: e.activation(out=sq[:, cb, :], in_=qraw[:, cb, :], func=AF.Square),
                 reads=["qraw%d" % cb], writes=["qsq%d" % cb])
        b = ps_next(C)
        for cb in range(4):
            P.op("pe", lambda e, b=b, cb=cb: e.matmul(out=C.ps[b][:, :], lhsT=C.ones_sb, rhs=sq[:, cb, :],
                                                    start=(cb == 0), stop=(cb == 3)),
                 reads=["ones", "qsq%d" % cb], writes=["ps%d" % b], lhs=["ones"])
        P.op("act", lambda e, b=b: e.activation(out=sdq, in_=C.ps[b][:, :], func=AF.Sqrt, scale=1.0 / 512, bias=C.eps_sb),
             reads=["ps%d" % b, "eps"], writes=["sdq"])
        P.op("dve", lambda e: e.reciprocal(out=rq, in_=sdq), reads=["sdq"], writes=["rq"])
        for cb in range(4):
            P.op("dve", lambda e, cb=cb: e.scalar_tensor_tensor(out=qan[:, cb, :], in0=qraw[:, cb, :], scalar=gqa[:, cb:cb + 1],
                                                             in1=rq, op0=ALU.mult, op1=ALU.mult),
                 reads=["qraw%d" % cb, "gqa", "rq"], writes=["qan%d" % cb])
        qres = ["qan%d" % cb for cb in range(4)]
        for h in range(NH):
            b = ps_next(C)
            for c4 in range(4):
                P.op("pe", lambda e, b=b, h=h, c4=c4: e.matmul(out=C.ps[b][:, :], lhsT=wqn[:, c4, h * 128:(h + 1) * 128],
                                                             rhs=qan[:, c4, :], start=(c4 == 0), stop=(c4 == 3)),
                     reads=["wqn", qres[c4]], writes=["ps%d" % b], lhs=["wqn"])
            if h % 2 == 0:
                P.op("act", lambda e, b=b, h=h: e.copy(out=C.Qn[:, h, t0:t0 + 512], in_=C.ps[b][:, :]),
                     reads=["ps%d" % b], writes=["Qn%d" % h])
            else:
                P.op("dve", lambda e, b=b, h=h: e.tensor_copy(out=C.Qn[:, h, t0:t0 + 512], in_=C.ps[b][:, :]),
                     reads=["ps%d" % b], writes=["Qn%d" % h])
        for hp in range(8):
            b1 = ps_next(C)
            for c4 in range(4):
                P.op("pe", lambda e, b1=b1, hp=hp, c4=c4: e.matmul(out=C.ps[b1][:, :], lhsT=wqr[:, c4, hp * 128:(hp + 1) * 128],
                                                                 rhs=qan[:, c4, :], start=(c4 == 0), stop=(c4 == 3)),
                     reads=["wqr", qres[c4]], writes=["ps%d" % b1], lhs=["wqr"])
            P.op("dve", lambda e, b1=b1: e.tensor_tensor(out=t1, in0=C.ps[b1][:, :], in1=Ct, op=ALU.mult),
                 reads=["ps%d" % b1, "rqC"], writes=["qt1"])
            b2 = ps_next(C)
            for c4 in range(4):
                P.op("pe", lambda e, b2=b2, hp=hp, c4=c4: e.matmul(out=C.ps[b2][:, :], lhsT=wqrs[:, c4, hp * 128:(hp + 1) * 128],
                                                                 rhs=qan[:, c4, :], start=(c4 == 0), stop=(c4 == 3)),
                     reads=["wqrs", qres[c4]], writes=["ps%d" % b2], lhs=["wqrs"])
            P.op("dve", lambda e, b2=b2: e.tensor_tensor(out=t2, in0=C.ps[b2][:, :], in1=St, op=ALU.mult),
                 reads=["ps%d" % b2, "rqS"], writes=["qt2"])
            P.op("pool", lambda e, hp=hp: e.tensor_tensor(out=C.Qr[:, hp, t0:t0 + 512], in0=t1, in1=t2, op=ALU.add),
                 reads=["qt1", "qt2"], writes=["Qr%d" % hp])

    for hf in range(2):
        q_half(hf)
    sb.release(mk)


def phase_A(C):
    P, sb = C.P, C.sb
    mk = sb.mark()
    S_all = C.S_all
    nkb = S_all // 128
    nq = nkb // 16
    KrT = [sb.alloc([128, S_all], BF16) for _ in range(2)]
    kring = [sb.alloc([128, 2048], BF16) for _ in range(4)]
    vring = [sb.alloc([128, 16, 128], BF16) for _ in range(4)]
    NPT = 16
    Pt = [sb.alloc([128, 512], BF16) for _ in range(NPT)]
    negm = sb.alloc([128, 16], BF16)
    Of = sb.alloc([128, 1024], F32)
    rl = sb.alloc([128, 1024], F32)
    accS = sb.alloc([128, 1024], F32)
    onesf = sb.alloc([128, 128], F32)
    stores = list(C.final_ops)
    P.op("pool", lambda e: e.memset(onesf, 1.0), writes=["onesf"])
    P.op("pool", lambda e: e.memset(KrT[0][64:128, :], 0.0), writes=["KrT0z"])
    P.op("pool", lambda e: e.memset(KrT[1][0:64, :], 0.0), writes=["KrT1z"])
    P.dma("sp", lambda e: e.dma_start(out=negm, in_=C.mask16), writes=["negm"], chan="constA", bulk=True)
    P.dma("sp", lambda e: e.dma_start(out=KrT[0][0:64, :], in_=C.krT_d[0:64, :]), writes=["KrT0"], chan="constA", bulk=True, extra=stores)
    P.dma("sp", lambda e: e.dma_start(out=KrT[1][64:128, :], in_=C.krT_d[64:128, :]), writes=["KrT1"], chan="constA", bulk=True, extra=stores)

    def load_q(i):
        h, q = divmod(i, nq)
        s = i % 4
        P.dma("sp", lambda e: e.dma_start(out=kring[s], in_=C.kT_d[h, :, 2048 * q:2048 * (q + 1)]),
              writes=["kq%d" % s], chan="kq%d" % s, extra=stores)
        P.dma("sp", lambda e: e.dma_start(out=vring[s], in_=C.v_d[h, :, 16 * q:16 * (q + 1), :]),
              writes=["vq%d" % s], chan="vq%d" % s, extra=stores)

    nload = NH * nq
    for i in range(min(4, nload)):
        load_q(i)

    units = []
    for h in range(NH):
        for kb in range(nkb):
            for half in range(2):
                lo = max(16 * kb, 512 * half)
                hi = 512 * (half + 1)
                if lo < hi:
                    units.append((h, kb, half, lo, hi))
    last_kb = {0: min(nkb - 1, 31), 1: nkb - 1}
    sbank = [5, 6, 7]
    NSB = 3
    ACCB = [3, 4]
    pool_cnt = {}

    def emit_S(u, ui):
        h, kb, half, lo, hi = u
        n = hi - lo
        b = sbank[ui % NSB]
        s = (h * nq + kb // 16) % 4
        kk = (kb % 16) * 128
        par = h % 2
        P.op("pe", lambda e: e.matmul(out=C.ps[b][:, 0:n], lhsT=kring[s][:, kk:kk + 128], rhs=C.Qn[:, h, lo:hi],
                                      start=True, stop=False),
             reads=["kq%d" % s, "Qn%d" % h], writes=["ps%d" % b], lhs=["kq%d" % s])
        P.op("pe", lambda e: e.matmul(out=C.ps[b][:, 0:n], lhsT=KrT[par][:, kb * 128:(kb + 1) * 128],
                                      rhs=C.Qr[:, h // 2, lo:hi], start=False, stop=True),
             reads=["KrT%d" % par, "KrT%dz" % par, "Qr%d" % (h // 2)], writes=["ps%d" % b], lhs=["KrT%d" % par, "KrT%dz" % par])

    def emit_PV(u, ui):
        h, kb, half, lo, hi = u
        n = hi - lo
        b = sbank[ui % NSB]
        pt = Pt[ui % NPT]
        pres = "Pt%d" % (ui % NPT)
        s = (h * nq + kb // 16) % 4
        hp = h % 2
        P.op("act", lambda e: e.activation(out=pt[:, 0:n], in_=C.ps[b][:, 0:n], func=AF.Exp, scale=SCALE),
             reads=["ps%d" % b], writes=[pres])
        if lo == 16 * kb:
            P.op("pool", lambda e: e.tensor_tensor(out=pt[:, 0:16], in0=pt[:, 0:16], in1=negm, op=ALU.mult),
                 reads=[pres, "negm"], writes=[pres])
        c0 = lo - 512 * half
        first = (kb == 0)
        last = (kb == last_kb[half])
        ob, lb = half, 2
        P.op("pe", lambda e: e.matmul(out=C.ps[ob][:, c0:c0 + n], lhsT=vring[s][:, kb % 16, :], rhs=pt[:, 0:n],
                                      start=first, stop=last),
             reads=["vq%d" % s, pres], writes=["ps%d" % ob], lhs=["vq%d" % s])
        ab = ACCB[half]
        if first:
            P.op("dve", lambda e: e.tensor_copy(out=C.ps[ab][:, c0:c0 + n], in_=pt[:, 0:n]),
                 reads=[pres], writes=["ps%d" % ab])
        else:
            P.op("dve", lambda e: e.tensor_tensor(out=C.ps[ab][:, c0:c0 + n], in0=C.ps[ab][:, c0:c0 + n], in1=pt[:, 0:n], op=ALU.add),
                 reads=[pres, "ps%d" % ab], writes=["ps%d" % ab])
        if last:
            hs_ = slice(512 * half, 512 * (half + 1))
            P.op("dve", lambda e: e.tensor_copy(out=accS[:, hs_], in_=C.ps[ab][:, :]), reads=["ps%d" % ab], writes=["accS%d" % half])
            P.op("pe", lambda e: e.matmul(out=C.ps[lb][:, :], lhsT=onesf, rhs=accS[:, hs_], start=True, stop=True),
                 reads=["onesf", "accS%d" % half], writes=["ps%d" % lb])
            P.op("dve", lambda e: e.reciprocal(out=rl[:, hs_], in_=C.ps[lb][:, :]), reads=["ps%d" % lb], writes=["rl%d" % half])
            P.op("dve", lambda e: e.tensor_tensor(out=C.attn[:, h, hs_], in0=C.ps[ob][:, :], in1=rl[:, hs_], op=ALU.mult),
                 reads=["ps%d" % ob, "rl%d" % half], writes=["attn%d" % h])
        if kb % 16 == 15 and half == 1:
            i = h * nq + kb // 16
            if i + 4 < nload:
                load_q(i + 4)

    LOOK = 2
    nu = len(units)
    for ui in range(min(LOOK, nu)):
        emit_S(units[ui], ui)
    for ui in range(nu):
        if ui + LOOK < nu:
            emit_S(units[ui + LOOK], ui + LOOK)
        emit_PV(units[ui], ui)
    sb.release(mk)


OFF_Z, OFF_CIN, OFF_BG, OFF_CG, OFF_ZC, OFF_GM, OFF_GC = 1088, 3136, 5184, 7232, 9280, 11328, 13376


def phase_R(C):
    P, sb = C.P, C.sb
    mg = sb.alloc([128, 16, 1024], BF16)
    mR = sb.mark()
    hT = sb.alloc([128, 16, 1024], BF16)
    hTh = sb.alloc([128, 16, 128], BF16)
    gc = sb.alloc([128, 16, 1024], BF16)
    wr = [sb.alloc([128, 16, 256], BF16) for _ in range(4)]
    convw = sb.alloc([128, 3, 16], F32)
    for k in range(3):
        P.dma("sp", lambda e, k=k: e.dma_start(out=convw[:, k, :], in_=C.conv_w[k].rearrange("(j p) -> p j", p=128),
                                             allow_slow_non_contiguous=True),
              writes=["convw"], chan="constR", bulk=True)
    mk = sb.mark()
    fbuf = []
    for _ in range(2):
        fbuf.append((sb.alloc([128, D], F32), None, sb.alloc([128, 1], F32), sb.alloc([128, 1], F32),
                     sb.alloc([128, 1], F32), sb.alloc([128, D], BF16)))
    srcs = [C.x_own[blk * 128:(blk + 1) * 128, :] for blk in range(8)] + [C.x_halo]
    dsts = [hT[:, :, blk * 128:(blk + 1) * 128] for blk in range(8)] + [hTh[:, :, :]]
    front_end1(C, srcs[0], 40, fbuf[0])
    for blk in range(9):
        if blk + 1 < 9:
            front_end1(C, srcs[blk + 1], 40 + (blk + 1) % 2, fbuf[(blk + 1) % 2])
        front_end2(C, 40 + blk % 2, dsts[blk], "hr_%d" % blk, fbuf[blk % 2][5])
    hres = lambda kc, half: ["hr_%d_%d" % (4 * half + tb, kc // 4) for tb in range(4)]
    hhres = lambda kc: ["hr_8_%d" % (kc // 4)]
    sb.release(mk)

    tiles = []
    for jp in range(8):
        tiles.append(C.w_in[:, OFF_Z + 256 * jp:OFF_Z + 256 * (jp + 1)])
    for jp in range(8):
        for off in (OFF_CIN, OFF_CG, OFF_BG, OFF_ZC):
            tiles.append(C.w_in[:, off + 256 * jp:off + 256 * (jp + 1)])
    for jp in range(8):
        tiles.append(C.w_o_mla[:, 256 * jp:256 * (jp + 1)])
        tiles.append(C.w_in[:, OFF_GM + 256 * jp:OFF_GM + 256 * (jp + 1)])
        tiles.append(C.w_in[:, OFF_GC + 256 * jp:OFF_GC + 256 * (jp + 1)])
        tiles.append(C.w_o_conv[:, 256 * jp:256 * (jp + 1)])
    st = {"next": 0}

    def issue(n=1):
        for _ in range(n):
            i = st["next"]
            if i >= len(tiles):
                return
            st["next"] = i + 1
            s_ = i % 4
            src = tiles[i].rearrange("(kc p) n -> p kc n", p=128)
            P.dma("pool", lambda e, s_=s_, src=src: e.dma_start(out=wr[s_], in_=src), writes=["wr%d" % s_], chan="wr%d" % s_)

    ti = {"i": 0}

    def take():
        i = ti["i"]
        ti["i"] = i + 1
        while st["next"] <= min(i + 3, len(tiles) - 1):
            issue(1)
        return wr[i % 4], "wr%d" % (i % 4)

    def mm(w, wres, sub, rhs_fn, rres_fn, n):
        b = ps_next(C)
        for kc in range(16):
            P.op("pe", lambda e, b=b, kc=kc: e.matmul(out=C.ps[b][:, 0:n], lhsT=w[:, kc, sub * 128:(sub + 1) * 128],
                                                    rhs=rhs_fn(kc), start=(kc == 0), stop=(kc == 15)),
                 reads=[wres] + rres_fn(kc), writes=["ps%d" % b], lhs=[wres])
        return b

    tA = sb.alloc([128, 2, 1024], F32)
    tB = sb.alloc([128, 2, 1024], F32)
    U = sb.alloc([128, 2, 64 * 18], F32)
    cinh = sb.alloc([128, 2, 128], F32)
    halves = [slice(0, 512), slice(512, 1024)]

    def own_mm(w, wres, sub, half):
        hsl = halves[half]
        return mm(w, wres, sub, lambda kc: hT[:, kc, hsl], lambda kc: hres(kc, half), 512)

    def r1(jp):
        w, wres = take()
        for sub in range(2):
            j = 2 * jp + sub
            for half in range(2):
                hsl = halves[half]
                b = own_mm(w, wres, sub, half)
                P.op("act", lambda e, b=b, hsl=hsl, sub=sub: e.activation(out=tA[:, sub, hsl], in_=C.ps[b][:, :], func=AF.Silu),
                     reads=["ps%d" % b], writes=["tA%d_%d" % (sub, half)])
                P.op("dve", lambda e, j=j, hsl=hsl, sub=sub: e.tensor_tensor(out=C.attn[:, j, hsl], in0=C.attn[:, j, hsl],
                                                                         in1=tA[:, sub, hsl], op=ALU.mult),
                     reads=["tA%d_%d" % (sub, half), "attn%d" % j], writes=["attn%d" % j])

    for jp in range(8):
        r1(jp)

    def Uv(sub):
        return U[:, sub, :].rearrange("p (m j) -> p m j", j=18)

    def r2(jp):
        w, wres = take()
        for sub in range(2):
            for half in range(2):
                hsl = halves[half]
                b = own_mm(w, wres, sub, half)
                P.op("act", lambda e, b=b, hsl=hsl, sub=sub: e.copy(out=tA[:, sub, hsl], in_=C.ps[b][:, :]),
                     reads=["ps%d" % b], writes=["tA%d_%d" % (sub, half)])
            b = mm(w, wres, sub, lambda kc: hTh[:, kc, :], hhres, 128)
            P.op("act", lambda e, b=b, sub=sub: e.copy(out=cinh[:, sub, :], in_=C.ps[b][:, 0:128]),
                 reads=["ps%d" % b], writes=["cinh%d" % sub])
        w, wres = take()
        for sub in range(2):
            j = 2 * jp + sub
            for half in range(2):
                hsl = halves[half]
                b = own_mm(w, wres, sub, half)
                P.op("dve", lambda e, b=b, half=half, hsl=hsl, sub=sub: e.tensor_tensor(
                    out=Uv(sub)[:, 32 * half:32 * (half + 1), 2:18], in0=C.ps[b][:, :].rearrange("p (m j) -> p m j", j=16),
                    in1=tA[:, sub, hsl].rearrange("p (m j) -> p m j", j=16), op=ALU.mult),
                    reads=["ps%d" % b, "tA%d_%d" % (sub, half)], writes=["Uo%d_%d" % (sub, half)])
            b = mm(w, wres, sub, lambda kc: hTh[:, kc, :], hhres, 128)
            P.op("dve", lambda e, b=b, sub=sub: e.tensor_tensor(out=Uv(sub)[:, :, 0:2],
                                                             in0=C.ps[b][:, 0:128].rearrange("p (m j) -> p m j", j=2),
                                                             in1=cinh[:, sub, :].rearrange("p (m j) -> p m j", j=2), op=ALU.mult),
                 reads=["ps%d" % b, "cinh%d" % sub], writes=["Uh%d" % sub])
            tB3 = tB[:, sub, :].rearrange("p (m j) -> p m j", j=16)
            ures = ["Uo%d_0" % sub, "Uo%d_1" % sub, "Uh%d" % sub]
            tres = "tB%d" % sub
            P.op("dve", lambda e, j=j, sub=sub, tB3=tB3: e.tensor_scalar(out=tB3, in0=Uv(sub)[:, :, 0:16], scalar1=convw[:, 0, j:j + 1],
                                                                     scalar2=None, op0=ALU.mult),
                 reads=ures + ["convw"], writes=[tres])
            P.op("dve", lambda e, j=j, sub=sub, tB3=tB3: e.scalar_tensor_tensor(out=tB3, in0=Uv(sub)[:, :, 1:17], scalar=convw[:, 1, j:j + 1],
                                                                            in1=tB3, op0=ALU.mult, op1=ALU.add),
                 reads=ures + ["convw", tres], writes=[tres])
            P.op("dve", lambda e, j=j, sub=sub, tB3=tB3: e.scalar_tensor_tensor(out=tB3, in0=Uv(sub)[:, :, 2:18], scalar=convw[:, 2, j:j + 1],
                                                                            in1=tB3, op0=ALU.mult, op1=ALU.add),
                 reads=ures + ["convw", tres], writes=[tres])
        w, wres = take()
        for sub in range(2):
            for half in range(2):
                hsl = halves[half]
                b = own_mm(w, wres, sub, half)
                P.op("dve", lambda e, b=b, hsl=hsl, sub=sub: e.tensor_tensor(out=tB[:, sub, hsl], in0=C.ps[b][:, :], in1=tB[:, sub, hsl], op=ALU.mult),
                     reads=["ps%d" % b, "tB%d" % sub], writes=["tB%d" % sub])
        w, wres = take()
        for sub in range(2):
            j = 2 * jp + sub
            for half in range(2):
                hsl = halves[half]
                b = own_mm(w, wres, sub, half)
                P.op("act", lambda e, b=b, hsl=hsl, sub=sub: e.activation(out=tA[:, sub, hsl], in_=C.ps[b][:, :], func=AF.Silu),
                     reads=["ps%d" % b], writes=["tA%d_%d" % (sub, half)])
                P.op("pool", lambda e, j=j, hsl=hsl, sub=sub: e.tensor_tensor(out=gc[:, j, hsl], in0=tB[:, sub, hsl], in1=tA[:, sub, hsl], op=ALU.mult),
                     reads=["tB%d" % sub, "tA%d_%d" % (sub, half)], writes=["gc%d" % j])

    for jp in range(8):
        r2(jp)

    allattn = ["attn%d" % j for j in range(16)]
    allgc = ["gc%d" % j for j in range(16)]

    def r3(jp):
        w, wres = take()
        for sub in range(2):
            for half in range(2):
                hsl = halves[half]
                b = mm(w, wres, sub, lambda kc, hsl=hsl: C.attn[:, kc, hsl], lambda kc: ["attn%d" % kc], 512)
                P.op("act", lambda e, b=b, hsl=hsl, sub=sub: e.copy(out=tB[:, sub, hsl], in_=C.ps[b][:, :]),
                     reads=["ps%d" % b], writes=["tB%d" % sub])
        w, wres = take()
        for sub in range(2):
            for half in range(2):
                hsl = halves[half]
                b = own_mm(w, wres, sub, half)
                P.op("act", lambda e, b=b, hsl=hsl, sub=sub: e.activation(out=tA[:, sub, hsl], in_=C.ps[b][:, :], func=AF.Sigmoid),
                     reads=["ps%d" % b], writes=["tA%d_%d" % (sub, half)])
                P.op("dve", lambda e, hsl=hsl, sub=sub: e.tensor_tensor(out=tB[:, sub, hsl], in0=tB[:, sub, hsl], in1=tA[:, sub, hsl], op=ALU.mult),
                     reads=["tB%d" % sub, "tA%d_%d" % (sub, half)], writes=["tB%d" % sub])
        w, wres = take()
        for sub in range(2):
            for half in range(2):
                hsl = halves[half]
                b = own_mm(w, wres, sub, half)
                P.op("act", lambda e, b=b, hsl=hsl, sub=sub: e.activation(out=tA[:, sub, hsl], in_=C.ps[b][:, :], func=AF.Sigmoid),
                     reads=["ps%d" % b], writes=["tA%d_%d" % (sub, half)])
        w, wres = take()
        for sub in range(2):
            j = 2 * jp + sub
            for half in range(2):
                hsl = halves[half]
                b = mm(w, wres, sub, lambda kc, hsl=hsl: gc[:, kc, hsl], lambda kc: ["gc%d" % kc], 512)
                P.op("dve", lambda e, b=b, hsl=hsl, sub=sub: e.tensor_tensor(out=tA[:, sub, hsl], in0=C.ps[b][:, :], in1=tA[:, sub, hsl], op=ALU.mult),
                     reads=["ps%d" % b, "tA%d_%d" % (sub, half)], writes=["tA%d_%d" % (sub, half)])
                P.op("pool", lambda e, j=j, hsl=hsl, sub=sub: e.tensor_tensor(out=mg[:, j, hsl], in0=tB[:, sub, hsl], in1=tA[:, sub, hsl], op=ALU.add),
                     reads=["tB%d" % sub, "tA%d_%d" % (sub, half)], writes=["mg%d" % j])

    for jp in range(8):
        r3(jp)

    if hasattr(C, "dbg_mg"):
        C.final_ops.append(P.dma("sp", lambda e: e.dma_start(out=C.dbg_gated, in_=C.attn), reads=allattn, chan="dbg"))
        C.final_ops.append(P.dma("sp", lambda e: e.dma_start(out=C.dbg_gc, in_=gc), reads=allgc, chan="dbg"))
        C.final_ops.append(P.dma("sp", lambda e: e.dma_start(out=C.dbg_mg, in_=mg), reads=["mg%d" % j for j in range(16)], chan="dbg"))
    P.barrier()
    sb.release(mR)
    wout = sb.alloc([128, 16, 2048], BF16)
    gpost = sb.alloc([128, D], F32)
    xr = [sb.alloc([128, D], F32) for _ in range(2)]
    ot = [sb.alloc([128, D], F32) for _ in range(2)]
    ssq = sb.alloc([128, 4], F32)
    sst = sb.alloc([128, 1], F32)
    sdo = sb.alloc([128, 1], F32)
    rso = sb.alloc([128, 1], F32)
    junk4 = sb.alloc([128, 512], BF16)
    for cg in range(4):
        P.dma("pool", lambda e, cg=cg: e.dma_start(out=wout[:, :, 512 * cg:512 * (cg + 1)],
                                                 in_=C.w_out[:, 512 * cg:512 * (cg + 1)].rearrange("(kc p) n -> p kc n", p=128)),
              writes=["wout%d" % cg], chan="wout%d" % cg)
    P.dma("sp", lambda e: e.dma_start(out=gpost, in_=C.g_post.broadcast_to([128, D])), writes=["gpost"], chan="constR4", bulk=True)
    allmg = ["mg%d" % j for j in range(16)]

    def r4(blk):
        s_ = blk % 2
        tsl = slice(128 * blk, 128 * (blk + 1))
        P.dma("sp", lambda e: e.dma_start(out=xr[s_], in_=C.x_own[tsl, :]), writes=["xr%d" % s_], chan="xr%d" % s_)
        banks = []
        for cg in range(4):
            b = ps_next(C)
            banks.append(b)
            for kc in range(16):
                P.op("pe", lambda e, b=b, kc=kc, cg=cg: e.matmul(out=C.ps[b][:, :], lhsT=mg[:, kc, tsl],
                                                               rhs=wout[:, kc, 512 * cg:512 * (cg + 1)],
                                                               start=(kc == 0), stop=(kc == 15)),
                     reads=["mg%d" % kc, "wout%d" % cg], writes=["ps%d" % b], lhs=["mg%d" % kc])
            P.op("act", lambda e, b=b, cg=cg: e.activation(out=junk4, in_=C.ps[b][:, :], func=AF.Square, accum_out=ssq[:, cg:cg + 1]),
                 reads=["ps%d" % b], writes=["junk4", "ssq%d" % cg])
        P.op("dve", lambda e: e.tensor_reduce(out=sst, in_=ssq, axis=mybir.AxisListType.X, op=ALU.add),
             reads=["ssq%d" % cg for cg in range(4)], writes=["sst"])
        P.op("act", lambda e: e.activation(out=sdo, in_=sst, func=AF.Sqrt, scale=1.0 / D, bias=C.eps_sb),
             reads=["sst", "eps"], writes=["sdo"])
        P.op("dve", lambda e: e.reciprocal(out=rso, in_=sdo), reads=["sdo"], writes=["rso"])
        for cg in range(4):
            b = banks[cg]
            csl = slice(512 * cg, 512 * (cg + 1))
            P.op("dve", lambda e, b=b, csl=csl: e.scalar_tensor_tensor(out=ot[s_][:, csl], in0=C.ps[b][:, :], scalar=rso,
                                                                     in1=gpost[:, csl], op0=ALU.mult, op1=ALU.mult),
                 reads=["ps%d" % b, "rso", "gpost"], writes=["ot%d_%d" % (s_, cg)])
            P.op("pool", lambda e, csl=csl: e.tensor_tensor(out=ot[s_][:, csl], in0=ot[s_][:, csl], in1=xr[s_][:, csl], op=ALU.add),
                 reads=["ot%d_%d" % (s_, cg), "xr%d" % s_], writes=["ot%d_%d" % (s_, cg)])
        C.final_ops.append(
            P.dma("sp", lambda e: e.dma_start(out=C.out[tsl, :], in_=ot[s_]),
                  reads=["ot%d_%d" % (s_, cg) for cg in range(4)], writes=["out"], chan="ot%d" % s_))

    for blk in range(8):
        r4(blk)


def _own_rows(c):
    return np.concatenate([np.arange(128 * m + 16 * c, 128 * m + 16 * c + 16) for m in range(64)])


def prep(x, positions, pre_norm_g, w_in, q_a_norm_g, w_q_b, kv_a_norm_g, w_kv_b,
         conv_w, w_o_mla, w_o_conv, w_out, post_norm_g, cores=range(NCORES)):
    import ml_dtypes
    x2 = np.ascontiguousarray(np.asarray(x, dtype=np.float32).reshape(S, D))
    pos = np.ascontiguousarray(np.asarray(positions, dtype=np.int32).reshape(1, S))
    w_in = np.asarray(w_in, dtype=np.float32)
    kr = w_in[:, 1024:1088]
    krs = np.concatenate([kr[:, 32:], kr[:, :32]], axis=1)
    w_lat = np.ascontiguousarray(np.concatenate([w_in[:, 512:1024], kr, kr, krs, krs], axis=1))
    wkv = np.asarray(w_kv_b, dtype=np.float32).reshape(512, NH, 256)
    w_k = np.ascontiguousarray(wkv[:, :, :128].reshape(512, 2048))
    w_v = np.ascontiguousarray(wkv[:, :, 128:].reshape(512, 2048))
    wq = np.asarray(w_q_b, dtype=np.float32).reshape(512, NH, 192)
    w_qn = np.ascontiguousarray(wq[:, :, :128].reshape(512, 2048))
    qr = wq[:, :, 128:]
    w_qr = np.ascontiguousarray(qr.reshape(512, 1024))
    w_qrs = np.ascontiguousarray(np.concatenate([qr[:, :, 32:], qr[:, :, :32]], axis=2).reshape(512, 1024))
    consts = make_consts()
    shared = dict(x_all=x2, pos_all=pos, g_pre=np.asarray(pre_norm_g, np.float32).reshape(1, D), w_lat=w_lat,
                  g_kv=np.asarray(kv_a_norm_g, np.float32), w_k=w_k, w_v=w_v, w_in=np.ascontiguousarray(w_in),
                  g_qa=np.asarray(q_a_norm_g, np.float32), w_qn=w_qn, w_qr=w_qr, w_qrs=w_qrs,
                  conv_w=np.ascontiguousarray(np.asarray(conv_w, np.float32)),
                  w_o_mla=np.ascontiguousarray(np.asarray(w_o_mla, np.float32)),
                  w_o_conv=np.ascontiguousarray(np.asarray(w_o_conv, np.float32)),
                  w_out=np.ascontiguousarray(np.asarray(w_out, np.float32)),
                  g_post=np.asarray(post_norm_g, np.float32).reshape(1, D), **consts)
    in_maps = []
    rows_all = []
    for c in cores:
        rows = _own_rows(c)
        rows_all.append(rows)
        x_halo = np.zeros((128, D), np.float32)
        for m in range(64):
            for t in range(2):
                g = 128 * m + 16 * c - 2 + t
                if g >= 0:
                    x_halo[2 * m + t] = x2[g]
        kk = np.arange(128)[:, None]
        jj = np.arange(16)[None, :]
        mask16 = (kk <= 16 * c + jj).astype(np.float32).astype(ml_dtypes.bfloat16)
        im = dict(shared)
        im.update(x_own=np.ascontiguousarray(x2[rows]), pos_own=np.ascontiguousarray(pos[:, rows]),
                  x_halo=x_halo, mask16=mask16)
        in_maps.append(im)
    return in_maps, rows_all


def kernel(x, positions, pre_norm_g, w_in, q_a_norm_g, w_q_b, kv_a_norm_g, w_kv_b,
           conv_w, w_o_mla, w_o_conv, w_out, post_norm_g):
    in_maps, rows_all = prep(x, positions, pre_norm_g, w_in, q_a_norm_g, w_q_b, kv_a_norm_g, w_kv_b,
                             conv_w, w_o_mla, w_o_conv, w_out, post_norm_g)
    nc = build()
    res = run_bass_kernel_spmd(nc, in_maps, core_ids=list(range(NCORES)))
    out = np.zeros((S, D), np.float32)
    for c in range(NCORES):
        out[rows_all[c]] = np.asarray(res.results[c]["out"], dtype=np.float32)
    return out.reshape(1, S, D)
```
